# Optimizing a Trainium2 kernel written in Bass

```python
import jax
import jax.numpy as jnp
from jax import lax
import numpy as np

D_MODEL = 1024
BATCH = 8
SEQ = 2048
DEPTH = 4

N_A_LAYERS = DEPTH // 2
N_B_LAYERS = DEPTH - N_A_LAYERS
EPS = 1e-6
D_RNN = D_MODEL
RG_HEADS = 4
RG_BLOCK = D_RNN // RG_HEADS
CONV_W = 4
RG_C = 8.0
HEAD_DIM = 128
N_HEADS = D_MODEL // HEAD_DIM
D_ATT = N_HEADS * HEAD_DIM
MOBA_BLOCK = 256
MOBA_TOPK = 3
Q_CHUNK = 16
NEG_INF = -1e30

kernel_name = 'hawk_moba_yoco_hybrid'


def _rmsnorm(x, g):
    xf = x.astype(jnp.float32)
    y = xf * lax.rsqrt(jnp.mean(xf * xf, axis=-1, keepdims=True) + EPS)
    return (y * g.astype(jnp.float32)).astype(x.dtype)


def _modulate(h, shift, scale):
    return h * (1.0 + scale[:, None, :]) + shift[:, None, :]


def _lin_combine(left, right):
    a_l, b_l = left
    a_r, b_r = right
    return a_l * a_r, a_r * b_l + b_r


def _block_diag(u, w):
    b, s, _ = u.shape
    ub = u.reshape(b, s, RG_HEADS, RG_BLOCK)
    return jnp.einsum('bshi,hij->bshj', ub, w).reshape(b, s, D_RNN)


def _rglru_mixer(h, w_in, conv_w, conv_b, w_a, b_a, w_x, b_x, lam, w_out):
    u, g = jnp.split(h @ w_in, 2, axis=-1)
    u = lax.conv_general_dilated(
        u, conv_w[:, None, :], window_strides=(1,), padding=[(CONV_W - 1, 0)],
        dimension_numbers=('NWC', 'WIO', 'NWC'), feature_group_count=D_RNN) + conv_b
    r = jax.nn.sigmoid((_block_diag(u, w_a) + b_a).astype(jnp.float32))
    gi = jax.nn.sigmoid((_block_diag(u, w_x) + b_x).astype(jnp.float32))
    log_a = -RG_C * r * jax.nn.softplus(-lam.astype(jnp.float32))
    a = jnp.exp(log_a)
    b_in = jnp.sqrt(-jnp.expm1(2.0 * log_a)) * (gi * u.astype(jnp.float32))
    _, hs = lax.associative_scan(_lin_combine, (a, b_in), axis=1)
    y = hs.astype(h.dtype) * jax.nn.silu(g)
    return y @ w_out


def _shared_kv(x, cs, kv_norm_g, kv_mod_w, kv_mod_b, w_kv):
    b, s, _ = x.shape
    shift, scale = jnp.split(cs @ kv_mod_w + kv_mod_b, 2, axis=-1)
    h = _modulate(_rmsnorm(x, kv_norm_g), shift, scale)
    k, v = jnp.split(h @ w_kv, 2, axis=-1)
    k = k.reshape(b, s, N_HEADS, HEAD_DIM).transpose(0, 2, 1, 3)
    v = v.reshape(b, s, N_HEADS, HEAD_DIM).transpose(0, 2, 1, 3)
    s_pad = -(-s // MOBA_BLOCK) * MOBA_BLOCK
    pad = ((0, 0), (0, 0), (0, s_pad - s), (0, 0))
    k = jnp.pad(k, pad)
    v = jnp.pad(v, pad)
    n_blk = s_pad // MOBA_BLOCK
    k_mean = k.astype(jnp.float32).reshape(b, N_HEADS, n_blk, MOBA_BLOCK, HEAD_DIM).mean(axis=3)
    return k, v, k_mean.astype(k.dtype)


def _moba_attention(q, k, v, k_mean):
    b, nh, s, dh = q.shape
    n = b * nh
    s_pad = k.shape[2]
    n_blk = s_pad // MOBA_BLOCK
    k_sel = min(MOBA_TOPK, n_blk)
    qf = q.reshape(n, s, dh)
    kf = k.reshape(n, s_pad, dh)
    vf = v.reshape(n, s_pad, dh)
    kb = k.reshape(n * n_blk, MOBA_BLOCK, dh)
    vb = v.reshape(n * n_blk, MOBA_BLOCK, dh)
    km = k_mean.reshape(n, n_blk, dh)
    base = (jnp.arange(n, dtype=jnp.int32) * n_blk)[:, None, None]
    blk_ids = jnp.arange(n_blk, dtype=jnp.int32)
    sm_scale = dh ** -0.5

    def one_chunk(ci):
        q0 = ci * Q_CHUNK
        blk = q0 // MOBA_BLOCK
        qc = lax.dynamic_slice_in_dim(qf, q0, Q_CHUNK, axis=1)
        gate = jnp.einsum('nqd,nbd->nqb', qc, km).astype(jnp.float32)
        gate = jnp.where(blk_ids < blk, gate, NEG_INF)
        _, idx = lax.top_k(gate, k_sel)
        valid = idx < blk
        kg = kb[base + idx]
        vg = vb[base + idx]
        s_sel = jnp.einsum('nqd,nqkld->nqkl', qc, kg).astype(jnp.float32) * sm_scale
        s_sel = jnp.where(valid[..., None], s_sel, NEG_INF).reshape(n, Q_CHUNK, k_sel * MOBA_BLOCK)
        k_own = lax.dynamic_slice_in_dim(kf, blk * MOBA_BLOCK, MOBA_BLOCK, axis=1)
        v_own = lax.dynamic_slice_in_dim(vf, blk * MOBA_BLOCK, MOBA_BLOCK, axis=1)
        s_own = jnp.einsum('nqd,nld->nql', qc, k_own).astype(jnp.float32) * sm_scale
        q_pos = q0 + jnp.arange(Q_CHUNK, dtype=jnp.int32)
        k_pos = blk * MOBA_BLOCK + jnp.arange(MOBA_BLOCK, dtype=jnp.int32)
        s_own = jnp.where(k_pos[None, None, :] <= q_pos[None, :, None], s_own, NEG_INF)
        p = jax.nn.softmax(jnp.concatenate([s_sel, s_own], axis=-1), axis=-1).astype(v.dtype)
        p_sel = p[..., :k_sel * MOBA_BLOCK].reshape(n, Q_CHUNK, k_sel, MOBA_BLOCK)
        p_own = p[..., k_sel * MOBA_BLOCK:]
        return (jnp.einsum('nqkl,nqkld->nqd', p_sel, vg)
                + jnp.einsum('nql,nld->nqd', p_own, v_own))

    out = lax.map(one_chunk, jnp.arange(s // Q_CHUNK, dtype=jnp.int32))
    return out.transpose(1, 0, 2, 3).reshape(b, nh, s, dh)


def _moba_mixer(h, k, v, k_mean, w_in, w_out):
    b, s, _ = h.shape
    q, g = jnp.split(h @ w_in, 2, axis=-1)
    q = q.reshape(b, s, N_HEADS, HEAD_DIM).transpose(0, 2, 1, 3)
    o = _moba_attention(q, k, v, k_mean).transpose(0, 2, 1, 3).reshape(b, s, D_ATT)
    return (o * jax.nn.silu(g)) @ w_out


def setup_inputs(seed: int = 0) -> dict:
    key = jax.random.key(seed)
    ks = jax.random.split(key, 24)
    f32 = jnp.float32

    def nrm(k, shape, scale):
        return scale * jax.random.normal(k, shape, f32)

    a0 = jax.random.uniform(ks[12], (N_A_LAYERS, D_RNN), f32, 0.9, 0.999)
    s0 = a0 ** (1.0 / RG_C)
    return {
        'x': nrm(ks[0], (BATCH, SEQ, D_MODEL), 1.0),
        'c': nrm(ks[1], (BATCH, D_MODEL), 1.0),
        'mod_w': nrm(ks[2], (DEPTH, D_MODEL, 3 * D_MODEL), 0.5 * D_MODEL ** -0.5),
        'mod_b': nrm(ks[3], (DEPTH, 3 * D_MODEL), 0.02),
        'norm_g': 1.0 + nrm(ks[4], (DEPTH, D_MODEL), 0.02),
        'rg_w_in': nrm(ks[5], (N_A_LAYERS, D_MODEL, 2 * D_RNN), D_MODEL ** -0.5),
        'rg_conv_w': nrm(ks[6], (N_A_LAYERS, CONV_W, D_RNN), CONV_W ** -0.5),
        'rg_conv_b': nrm(ks[7], (N_A_LAYERS, D_RNN), 0.01),
        'rg_w_a': nrm(ks[8], (N_A_LAYERS, RG_HEADS, RG_BLOCK, RG_BLOCK), RG_BLOCK ** -0.5),
        'rg_b_a': nrm(ks[9], (N_A_LAYERS, D_RNN), 0.01),
        'rg_w_x': nrm(ks[10], (N_A_LAYERS, RG_HEADS, RG_BLOCK, RG_BLOCK), RG_BLOCK ** -0.5),
        'rg_b_x': nrm(ks[11], (N_A_LAYERS, D_RNN), 0.01),
        'rg_lambda': jnp.log(s0) - jnp.log1p(-s0),
        'rg_w_out': nrm(ks[13], (N_A_LAYERS, D_RNN, D_MODEL), D_RNN ** -0.5),
        'kv_norm_g': 1.0 + nrm(ks[14], (D_MODEL,), 0.02),
        'kv_mod_w': nrm(ks[15], (D_MODEL, 2 * D_MODEL), 0.5 * D_MODEL ** -0.5),
        'kv_mod_b': nrm(ks[16], (2 * D_MODEL,), 0.02),
        'w_kv': nrm(ks[17], (D_MODEL, 2 * D_ATT), D_MODEL ** -0.5),
        'att_w_in': nrm(ks[18], (N_B_LAYERS, D_MODEL, 2 * D_ATT), D_MODEL ** -0.5),
        'att_w_out': nrm(ks[19], (N_B_LAYERS, D_ATT, D_MODEL), D_ATT ** -0.5),
        'final_norm_g': 1.0 + nrm(ks[20], (D_MODEL,), 0.02),
    }


def reference(x, c, mod_w, mod_b, norm_g, rg_w_in, rg_conv_w, rg_conv_b, rg_w_a, rg_b_a,
              rg_w_x, rg_b_x, rg_lambda, rg_w_out, kv_norm_g, kv_mod_w, kv_mod_b, w_kv,
              att_w_in, att_w_out, final_norm_g):
    cs = jax.nn.silu(c)
    k_sh = v_sh = km_sh = None
    for layer in range(DEPTH):
        shift, scale, gate = jnp.split(cs @ mod_w[layer] + mod_b[layer], 3, axis=-1)
        h = _modulate(_rmsnorm(x, norm_g[layer]), shift, scale)
        if layer < N_A_LAYERS:
            i = layer
            y = _rglru_mixer(h, rg_w_in[i], rg_conv_w[i], rg_conv_b[i], rg_w_a[i], rg_b_a[i],
                             rg_w_x[i], rg_b_x[i], rg_lambda[i], rg_w_out[i])
        else:
            if layer == N_A_LAYERS:
                k_sh, v_sh, km_sh = _shared_kv(x, cs, kv_norm_g, kv_mod_w, kv_mod_b, w_kv)
            i = layer - N_A_LAYERS
            y = _moba_mixer(h, k_sh, v_sh, km_sh, att_w_in[i], att_w_out[i])
        x = x + gate[:, None, :] * y
    return _rmsnorm(x, final_norm_g)
```

```python
import contextlib
import os
import numpy as np
import concourse.bass as bass
import concourse.mybir as mybir
from concourse.bass_utils import run_bass_kernel_spmd

F32 = mybir.dt.float32
F32R = mybir.dt.float32r
BF16 = mybir.dt.bfloat16
AF = mybir.ActivationFunctionType
ALU = mybir.AluOpType
AX = mybir.AxisListType

S = 2048
D = 1024
KC = 8
TT = 512
NT = S // TT
EPS = 1e-6
NEG = -30000.0
SM_SCALE = 128 ** -0.5

PV_NORMG = 0
PV_KVG = 32
PV_FING = 40
PV_MODB = 48
PV_KVMODB = 144
PV_CONVW = 160
PV_CONVB = 224
PV_BA = 240
PV_BX = 256
PV_LAM = 272
NPV = 288
NMOD = 112


class _Stop(Exception):
    pass


STOP = int(os.environ.get("K_STOP", "0"))
ATTACH = os.environ.get("K_ATTACH", "1") == "1"


def stage(n):
    if STOP == n:
        raise _Stop()


class Sch:
    def __init__(self, nc, es):
        self.nc, self.es = nc, es
        self.E = {"pe": nc.tensor, "act": nc.scalar, "dve": nc.vector, "pool": nc.gpsimd, "sp": nc.sync}
        self.sem, self.cnt = {}, {}
        for e in self.E:
            self.mksem(e)
        self.known = {e: {} for e in self.E}
        self.lw, self.rd = {}, {}
        self.nwait = 0
        self.nins = 0
        self.pend = None

    def mksem(self, name):
        self.sem[name] = self.es.enter_context(self.nc.semaphore("s_" + name))
        self.cnt[name] = 0

    def _deps(self, e, r, w, strict=False):
        deps = []
        inorder = (e in ("act", "dve", "pe")) and not strict
        for t in r:
            ev = self.lw.get(t)
            if ev is not None:
                deps.append(ev)
            if isinstance(t, tuple) and t[0] == "ps":
                for ev in self.rd.get(t, {}).values():
                    if ev[0] != e:
                        deps.append(ev)
        for t in w:
            ev = self.lw.get(t)
            if ev is not None and not (inorder and ev[0] == e):
                deps.append(ev)
            for ev in self.rd.get(t, {}).values():
                if not (inorder and ev[0] == e):
                    deps.append(ev)
        return deps

    def _wait(self, e, deps, attach=False):
        K = self.known[e]
        need = {}
        for (sn, v, vc) in deps:
            if K.get(sn, 0) < v:
                need[sn] = max(need.get(sn, 0), v)
        implied = {}
        for (sn, v, vc) in deps:
            if need.get(sn, 0) == v:
                for k2, v2 in vc.items():
                    if k2 != sn and implied.get(k2, 0) < v2:
                        implied[k2] = v2
        pend = None
        for sn, v in need.items():
            if implied.get(sn, 0) >= v:
                continue
            if attach and pend is None:
                pend = (sn, v)
                continue
            self.E[e].wait_ge(self.sem[sn], v)
            self.nwait += 1
        self.pend = pend
        for (sn, v, vc) in deps:
            for k2, v2 in vc.items():
                if K.get(k2, 0) < v2:
                    K[k2] = v2
            if K.get(sn, 0) < v:
                K[sn] = v

    def _reg(self, ev, r, w):
        for t in r:
            self.rd.setdefault(t, {})[ev[0]] = ev
        for t in w:
            self.lw[t] = ev
            self.rd[t] = {}

    def op(self, e, fn, r=(), w=(), strict=False, multi=False, nl=None):
        if e == "pe" and nl is not None and ATTACH:
            self._wait(e, self._deps(e, r[:nl], (), strict), attach=False)
            self._wait(e, self._deps(e, r[nl:], w, strict), attach=True)
        else:
            self._wait(e, self._deps(e, r, w, strict), attach=(e in ("act", "dve") and not multi and ATTACH))
        ins = fn(self.E[e])
        if self.pend is not None:
            ins._wait_ge(self.sem[self.pend[0]], self.pend[1])
            self.pend = None
        self.cnt[e] += 1
        ins.then_inc(self.sem[e], 1)
        self.nins += 1
        self._reg((e, self.cnt[e], dict(self.known[e])), r, w)

    def dma(self, q, dsem, out, in_, r=(), w=()):
        if dsem not in self.sem:
            self.mksem(dsem)
        self._wait(q, self._deps(q, r, w))
        ins = self.E[q].dma_start(out=out, in_=in_)
        self.cnt[dsem] += 16
        ins.then_inc(self.sem[dsem], 16)
        self.nins += 1
        self._reg((dsem, self.cnt[dsem], dict(self.known[q])), r, w)

    def barrier(self):
        for e in self.E:
            for sn, c in self.cnt.items():
                if c > 0 and self.known[e].get(sn, 0) < c:
                    self.E[e].wait_ge(self.sem[sn], c)
                    self.known[e][sn] = c
        self.lw.clear()
        self.rd.clear()

    def final_wait(self, e, sn):
        self.E[e].wait_ge(self.sem[sn], self.cnt[sn])


def build(nlayers=4, dbg=False):
    nc = bass.Bass("TRN2", target_bir_lowering=False)
    dram = lambda n, s: nc.dram_tensor(n, s, F32, kind="ExternalInput").ap()
    x_d = dram("x", [S, D])
    c_d = dram("cT", [128, KC])
    pv_d = dram("pv", [128, NPV])
    cst_d = dram("cst", [128, 384])
    modw_d = dram("mod_w", [4, D, 3 * D])
    kvmodw_d = dram("kv_mod_w", [D, 2 * D])
    rgwin_d = dram("rg_w_in", [2, D, 2 * D])
    rgwa_d = dram("rg_w_a", [2, 4, 256, 256])
    rgwx_d = dram("rg_w_x", [2, 4, 256, 256])
    rgwout_d = dram("rg_w_out", [2, D, D])
    wkv_d = dram("w_kv", [D, 2 * D])
    attwin_d = dram("att_w_in", [2, D, 2 * D])
    attwout_d = dram("att_w_out", [2, D, D])
    out_d = nc.dram_tensor("out", [S, D], F32, kind="ExternalOutput").ap()

    es = contextlib.ExitStack()
    with es:
        sc = Sch(nc, es)
        op, dma = sc.op, sc.dma
        REM = [nc.sbuf_bytes_remaining]

        def _alloc(stack, n, s, dt):
            t = stack.enter_context(nc.sbuf_tensor("sb_" + n, s, dt))
            REM.append(nc.sbuf_bytes_remaining)
            return t
        sb = lambda n, s, dt=F32: _alloc(es, n, s, dt)

        xT = sb("xT", [128, KC, S])
        hT = sb("hT", [128, KC, S], BF16)
        cst = sb("cst", [128, 384])
        pv = sb("pv", [128, NPV])
        modT = sb("modT", [128, NMOD])
        identb = sb("identb", [128, 128], BF16)
        cmaskb = sb("cmaskb", [128, 128], BF16)
        onesb = sb("onesb", [128, 128], BF16)
        Avec = sb("Avec", [128, 6 * KC])
        cvec = sb("cvec", [128, 2 * KC])
        hvec = sb("hvec", [128, 48])
        scr = sb("scr", [128, 4096])
        sqb = scr[:, 0:2048].bitcast(BF16).rearrange("p (a b) -> p a b", b=TT)
        rms = scr[:, 2048:2560]
        rstd = scr[:, 2560:3072]
        htmp = scr[:, 3072:4096].rearrange("p (a b) -> p a b", b=TT)
        ident = cst[:, 0:128]
        ones = cst[:, 256:384]

        ps = [es.enter_context(nc.psum_tensor(f"ps{i}", [128, 512], F32)) for i in range(8)]
        PS = lambda i: ("ps", i)

        def xtok(tt):
            return [("xT", tt, k) for k in range(KC)]

        def col(base, n=KC):
            return pv[:, base:base + n]

        dma("sp", "d_cst", cst[:], cst_d[:, :], w=["cst"])
        dma("sp", "d_pv", pv[:], pv_d[:, :], w=["pv"])
        cT = sb("cTs", [128, KC])
        dma("sp", "d_c", cT[:], c_d[:, :], w=["cT"])
        op("dve", lambda e: e.tensor_copy(out=identb[:], in_=cst[:, 0:128]), r=["cst", "pv", "cT"], w=["identb"])
        op("dve", lambda e: e.tensor_copy(out=cmaskb[:], in_=cst[:, 128:256]), r=["cst"], w=["cmaskb"])
        op("dve", lambda e: e.tensor_copy(out=onesb[:], in_=cst[:, 256:384]), r=["cst"], w=["onesb"])
        cs = sb("cs", [128, KC])
        op("act", lambda e: e.activation(out=cs[:], in_=cT[:], func=AF.Silu), r=["cT", "cst", "pv"], w=["cs"])

        with contextlib.ExitStack() as es2:
            sb2 = lambda n, s, dt=F32: _alloc(es2, n, s, dt)
            xin = [sb2(f"xin{i}", [128, D]) for i in range(3)]
            wm = [sb2(f"wm{i}", [128, KC, 512], BF16) for i in range(4)]
            msb = sb2("msb", [128, 512])
            csrep = sb2("csrep", [128, KC, 128], BF16)
            for kc in range(KC):
                op("dve", lambda e, kc=kc: e.tensor_scalar(out=csrep[:, kc, :], in0=ones, scalar1=cs[:, kc:kc + 1], scalar2=None, op0=ALU.mult),
                   r=["cs", "cst"], w=[("csrep", kc)])
            xv = x_d.rearrange("(n p) d -> n p d", p=128)
            for n in range(S // 128):
                sl = n % 3
                dma("sp", f"d_x{sl}", xin[sl][:], xv[n], w=[("xin", sl)])
                for half in range(2):
                    b = (n * 2 + half) % 2
                    for q in range(4):
                        kc = half * 4 + q
                        op("pe", lambda e, b=b, q=q, kc=kc, sl=sl: e.transpose(out=ps[b][:, q * 128:(q + 1) * 128], in_=xin[sl][:, kc * 128:(kc + 1) * 128], identity=ident),
                           r=[("xin", sl), "cst"], w=[PS(b)])
                    src = ps[b][:, :].rearrange("p (a b) -> p a b", b=128)
                    dst = xT[:, half * 4:half * 4 + 4, n * 128:(n + 1) * 128]
                    if half == 0:
                        op("act", lambda e, src=src, dst=dst: e.activation(out=dst, in_=src, func=AF.Copy), r=[PS(b)], w=[("xT", n // 4, k) for k in range(half * 4, half * 4 + 4)])
                    else:
                        op("dve", lambda e, src=src, dst=dst: e.tensor_copy(out=dst, in_=src), r=[PS(b)], w=[("xT", n // 4, k) for k in range(half * 4, half * 4 + 4)])

            groups = [(l, modw_d[l], jg, l * 24 + jg * 4, PV_MODB + l * 24 + jg * 4) for l in range(4) for jg in range(6)]
            groups += [(4, kvmodw_d, jg, 96 + jg * 4, PV_KVMODB + jg * 4) for jg in range(4)]
            for gi, (l, wd, jg, mcol, bcol) in enumerate(groups):
                sl = gi % 4
                wsrc = wd[:, jg * 512:(jg + 1) * 512].rearrange("(kc p) j -> p kc j", p=128)
                dma("pool", f"d_m{sl}", wm[sl][:], wsrc, w=[("wm", sl)])
                b = 2 + (gi % 2)
                for kc in range(KC):
                    op("pe", lambda e, kc=kc, sl=sl, b=b: e.matmul(out=ps[b][:, :], lhsT=csrep[:, kc, :], rhs=wm[sl][:, kc, :],
                                                                     start=(kc == 0), stop=(kc == KC - 1)),
                       r=[("wm", sl)] + [("csrep", k) for k in range(KC)], w=[PS(b)])
                op("act", lambda e, b=b: e.activation(out=msb[:], in_=ps[b][:, :], func=AF.Copy), r=[PS(b)], w=["msb"])
                b2 = 4 + (gi % 2)
                for q in range(4):
                    op("pe", lambda e, q=q, b2=b2: e.transpose(out=ps[b2][:, q * 128:(q + 1) * 128], in_=msb[:, q * 128:(q + 1) * 128], identity=ident),
                       r=["msb", "cst"], w=[PS(b2)])
                srcv = ps[b2][:, :].rearrange("p (a b) -> p a b", b=128)[:, :, 0]
                op("dve", lambda e, srcv=srcv, mcol=mcol, bcol=bcol: e.tensor_tensor(out=modT[:, mcol:mcol + 4], in0=srcv, in1=pv[:, bcol:bcol + 4], op=ALU.add),
                   r=[PS(b2), "pv"], w=["modT"])

            for l in range(4):
                op("dve", lambda e, l=l: e.scalar_tensor_tensor(out=Avec[:, l * KC:(l + 1) * KC], in0=modT[:, l * 24 + 8:l * 24 + 16], scalar=1.0,
                                                               in1=col(PV_NORMG + l * KC), op0=ALU.add, op1=ALU.mult), r=["modT", "pv"], w=["Avec"])
            op("dve", lambda e: e.scalar_tensor_tensor(out=Avec[:, 4 * KC:5 * KC], in0=modT[:, 96 + 8:96 + 16], scalar=1.0,
                                                       in1=col(PV_KVG), op0=ALU.add, op1=ALU.mult), r=["modT", "pv"], w=["Avec"])
            op("dve", lambda e: e.tensor_copy(out=Avec[:, 5 * KC:6 * KC], in_=col(PV_FING)), r=["pv"], w=["Avec"])
            spt = sb2("spt", [128, 2 * KC])
            op("act", lambda e: e.activation(out=spt[:], in_=col(PV_LAM, 2 * KC), func=AF.Exp, scale=-1.0), r=["pv"], w=["spt"])
            op("act", lambda e: e.activation(out=spt[:], in_=spt[:], func=AF.Ln, bias=1.0, scale=1.0), r=["spt"], w=["spt"])
            op("dve", lambda e: e.tensor_scalar(out=cvec[:], in0=spt[:], scalar1=-8.0, scalar2=None, op0=ALU.mult), r=["spt"], w=["cvec"])
            op("dve", lambda e: e.tensor_scalar(out=hvec[:, 32:48], in0=spt[:], scalar1=-4.0, scalar2=None, op0=ALU.mult), r=["spt"], w=["hvec"])
            op("dve", lambda e: e.tensor_scalar(out=hvec[:, 0:16], in0=col(PV_BA, 16), scalar1=0.5, scalar2=None, op0=ALU.mult), r=["pv"], w=["hvec"])
            op("dve", lambda e: e.tensor_scalar(out=hvec[:, 16:32], in0=col(PV_BX, 16), scalar1=0.5, scalar2=None, op0=ALU.mult), r=["pv"], w=["hvec"])
            for l in range(4):
                op("dve", lambda e, l=l: e.tensor_scalar(out=modT[:, l * 24 + 16:l * 24 + 24], in0=modT[:, l * 24 + 16:l * 24 + 24], scalar1=0.5, scalar2=None, op0=ALU.mult),
                   r=["modT"], w=["modT"])
            sc.barrier()

        sqbuf = [scr[:, 0:1024].bitcast(BF16).rearrange("p (a b) -> p a b", b=TT),
                 scr[:, 1024:2048].bitcast(BF16).rearrange("p (a b) -> p a b", b=TT)]
        htr = [scr[:, 2048 + i * 512:2048 + (i + 1) * 512] for i in range(4)]

        def emit_rstd_all():
            for tt in range(NT):
                tsl = slice(tt * TT, (tt + 1) * TT)
                for hf in range(2):
                    k0 = 4 * hf
                    op("act", lambda e: e.activation(out=sqbuf[hf][:, 0:3, :], in_=xT[:, k0:k0 + 3, tsl], func=AF.Square),
                       r=[("xT", tt, k0 + j) for j in range(3)], w=[("sqa", hf)])
                    op("dve", lambda e: e.tensor_tensor(out=sqbuf[hf][:, 3, :], in0=xT[:, k0 + 3, tsl], in1=xT[:, k0 + 3, tsl], op=ALU.mult),
                       r=[("xT", tt, k0 + 3)], w=[("sqd", hf)])
                    for j in range(4):
                        op("pe", lambda e, j=j: e.matmul(out=ps[4 + tt][:, :], lhsT=onesb[:], rhs=sqbuf[hf][:, j, :], start=(hf == 0 and j == 0), stop=(hf == 1 and j == 3)),
                           r=["onesb", ("sqa", hf), ("sqd", hf)], w=[PS(4 + tt)], nl=1)
            for tt in range(NT):
                op("act", lambda e, tt=tt: e.activation(out=ps[4 + tt][:, :], in_=ps[4 + tt][:, :], func=AF.Ln, bias=EPS, scale=1.0 / D), r=[PS(4 + tt)], w=[PS(4 + tt)])
            for tt in range(NT):
                op("act", lambda e, tt=tt: e.activation(out=ps[4 + tt][:, :], in_=ps[4 + tt][:, :], func=AF.Exp, scale=-0.5), r=[PS(4 + tt)], w=[PS(4 + tt)])

        def emit_norm(aidx, bcol0):
            emit_rstd_all()
            n = 0
            for tt in range(NT):
                tsl = slice(tt * TT, (tt + 1) * TT)
                for kc in range(KC):
                    hb = n % 4
                    n += 1
                    op("dve", lambda e, kc=kc, hb=hb: e.scalar_tensor_tensor(out=htr[hb], in0=xT[:, kc, tsl], scalar=Avec[:, aidx * KC + kc:aidx * KC + kc + 1],
                                                                             in1=ps[4 + tt][:, :], op0=ALU.mult, op1=ALU.mult),
                       r=[("xT", tt, kc), PS(4 + tt), "Avec"], w=[("htmp", hb)])
                    op("act", lambda e, kc=kc, hb=hb: e.activation(out=hT[:, kc, tsl], in_=htr[hb], func=AF.Identity,
                                                                   bias=modT[:, bcol0 + kc:bcol0 + kc + 1], scale=1.0),
                       r=[("htmp", hb), "modT"], w=[("hT", tt)])

        def wview(wd, c0, n):
            return wd[:, c0:c0 + n].rearrange("(kc p) j -> p kc j", p=128)

        def emit_rg_layers(layers):
            with contextlib.ExitStack() as esl:
                sbl = lambda n, s, dt=F32: _alloc(esl, n, s, dt)
                yT = sbl("yT", [128, KC, S], BF16)
                ring = [sbl(f"wr{i}", [128, KC, 256], BF16) for i in range(4)]
                wa = [sbl(f"wa{i}", [128, 2, 256], BF16) for i in range(2)]
                wx = [sbl(f"wx{i}", [128, 2, 256], BF16) for i in range(2)]
                dg = sbl("dg", [128, 2, 4, 128])
                u_raw = sbl("u_raw", [128, 2, 4 + TT])
                sg2 = [sbl(f"sg{i}", [128, 2, TT], BF16) for i in range(2)]
                uc = sbl("uc", [128, 2, TT])
                ucb = sbl("ucb", [128, 2, TT], BF16)
                rr = sbl("rr", [128, 2, TT])
                gi = sbl("gi", [128, 2, TT])
                aa = sbl("aa", [128, 2, TT])
                a2 = sbl("a2", [128, 2, TT])
                hs = sbl("hs", [128, 2, TT])
                carry = sbl("carry", [128, 2])

                def load_block(l, hb):
                    sl = hb % 2
                    dma("pool", f"d_wr{sl}", ring[sl][:], wview(rgwin_d[l], hb * 256, 256), w=[("ring", sl)])
                    dma("pool", f"d_wr{2 + sl}", ring[2 + sl][:], wview(rgwin_d[l], D + hb * 256, 256), w=[("ring", 2 + sl)])
                    dma("pool", f"d_wa{sl}", wa[sl][:], rgwa_d[l, hb].rearrange("(ic p) j -> p ic j", p=128), w=[("wa", sl)])
                    dma("pool", f"d_wx{sl}", wx[sl][:], rgwx_d[l, hb].rearrange("(ic p) j -> p ic j", p=128), w=[("wx", sl)])

                try:
                    for l in layers:
                        emit_norm(l, l * 24)
                        stage(1)
                        load_block(l, 0)
                        stage(2)

                        def build_dg(hb):
                            for jc in range(2):
                                ch = hb * 2 + jc
                                for k in range(4):
                                    cwc = PV_CONVW + l * 32 + k * 8 + ch
                                    op("dve", lambda e, jc=jc, k=k, cwc=cwc: e.tensor_scalar(out=dg[:, jc, k, :].bitcast(F32R), in0=ident, scalar1=pv[:, cwc:cwc + 1], scalar2=None, op0=ALU.mult),
                                       r=["cst", "pv"], w=[("dg", jc)])

                        def stA(n):
                            hb, tt = n // 4, n % 4
                            sl = hb % 2
                            tsl = slice(tt * TT, (tt + 1) * TT)
                            sgn = sg2[n % 2]
                            for jc in range(2):
                                for kc in range(KC):
                                    op("pe", lambda e, jc=jc, kc=kc: e.matmul(out=ps[jc][:, :], lhsT=ring[sl][:, kc, jc * 128:(jc + 1) * 128], rhs=hT[:, kc, tsl],
                                                                              start=(kc == 0), stop=(kc == KC - 1)),
                                       r=[("ring", sl), ("hT", tt)], w=[PS(jc)], nl=1)
                            for jc in range(2):
                                for kc in range(KC):
                                    op("pe", lambda e, jc=jc, kc=kc: e.matmul(out=ps[2 + jc][:, :], lhsT=ring[2 + sl][:, kc, jc * 128:(jc + 1) * 128], rhs=hT[:, kc, tsl],
                                                                              start=(kc == 0), stop=(kc == KC - 1)),
                                       r=[("ring", 2 + sl), ("hT", tt)], w=[PS(2 + jc)], nl=1)
                            for jc in range(2):
                                if tt == 0:
                                    op("dve", lambda e, jc=jc: e.tensor_scalar(out=u_raw[:, jc, 0:4].bitcast(F32R), in0=ones[:, 0:4], scalar1=0.0, scalar2=None, op0=ALU.mult), r=["cst"], w=[("u_raw", jc)])
                                else:
                                    op("dve", lambda e, jc=jc: e.tensor_copy(out=u_raw[:, jc, 0:4].bitcast(F32R), in_=u_raw[:, jc, TT:TT + 4]), r=[("u_raw", jc)], w=[("u_raw", jc)])
                                op("dve", lambda e, jc=jc: e.tensor_copy(out=u_raw[:, jc, 4:4 + TT].bitcast(F32R), in_=ps[jc][:, :]), r=[PS(jc)], w=[("u_raw", jc)], strict=True)
                                op("act", lambda e, jc=jc: e.activation(out=sgn[:, jc, :], in_=ps[2 + jc][:, :], func=AF.Tanh, scale=0.5), r=[PS(2 + jc)], w=[("sg", n % 2, jc)])
                                op("dve", lambda e, jc=jc: e.scalar_tensor_tensor(out=sgn[:, jc, :], in0=sgn[:, jc, :], scalar=1.0, in1=ps[2 + jc][:, :], op0=ALU.add, op1=ALU.mult),
                                   r=[PS(2 + jc), ("sg", n % 2, jc)], w=[("sg", n % 2, jc)])

                        def stB(n):
                            hb, tt = n // 4, n % 4
                            if tt == 0:
                                build_dg(hb)
                            for jc in range(2):
                                ch = hb * 2 + jc
                                for k in range(4):
                                    op("pe", lambda e, jc=jc, k=k: e.matmul(out=ps[4 + jc][:, :], lhsT=dg[:, jc, k, :].bitcast(F32R), rhs=u_raw[:, jc, 1 + k:1 + k + TT].bitcast(F32R),
                                                                            start=(k == 0), stop=(k == 3)),
                                       r=[("dg", jc), ("u_raw", jc)], w=[PS(4 + jc)], nl=1)
                                cb = PV_CONVB + l * 8 + ch
                                op("act", lambda e, jc=jc, cb=cb: e.activation(out=uc[:, jc, :], in_=ps[4 + jc][:, :], func=AF.Identity, bias=pv[:, cb:cb + 1], scale=1.0),
                                   r=[PS(4 + jc), "pv"], w=[("uc", jc)])
                                op("dve", lambda e, jc=jc, cb=cb: e.tensor_scalar(out=ucb[:, jc, :], in0=ps[4 + jc][:, :], scalar1=pv[:, cb:cb + 1], scalar2=None, op0=ALU.add),
                                   r=[PS(4 + jc), "pv"], w=[("ucb", jc)])

                        def stC(n):
                            hb, tt = n // 4, n % 4
                            sl = hb % 2
                            tsl = slice(tt * TT, (tt + 1) * TT)
                            sgn = sg2[n % 2]
                            for jc in range(2):
                                for ic in range(2):
                                    op("pe", lambda e, jc=jc, ic=ic: e.matmul(out=ps[6 + jc][:, :], lhsT=wa[sl][:, ic, jc * 128:(jc + 1) * 128], rhs=ucb[:, ic, :],
                                                                              start=(ic == 0), stop=(ic == 1)),
                                       r=[("wa", sl), ("ucb", 0), ("ucb", 1)], w=[PS(6 + jc)], nl=1)
                                for ic in range(2):
                                    op("pe", lambda e, jc=jc, ic=ic: e.matmul(out=ps[4 + jc][:, :], lhsT=wx[sl][:, ic, jc * 128:(jc + 1) * 128], rhs=ucb[:, ic, :],
                                                                              start=(ic == 0), stop=(ic == 1)),
                                       r=[("wx", sl), ("ucb", 0), ("ucb", 1)], w=[PS(4 + jc)], nl=1)
                            for jc in range(2):
                                cc = l * 8 + hb * 2 + jc
                                op("act", lambda e, jc=jc, cc=cc: e.activation(out=rr[:, jc, :], in_=ps[6 + jc][:, :], func=AF.Tanh, bias=hvec[:, cc:cc + 1], scale=0.5),
                                   r=[PS(6 + jc), "hvec"], w=[("rr", jc)])
                                op("act", lambda e, jc=jc, cc=cc: e.activation(out=gi[:, jc, :], in_=ps[4 + jc][:, :], func=AF.Tanh, bias=hvec[:, 16 + cc:16 + cc + 1], scale=0.5),
                                   r=[PS(4 + jc), "hvec"], w=[("gi", jc)])
                            for jc in range(2):
                                cc = l * 8 + hb * 2 + jc
                                op("act", lambda e, jc=jc, cc=cc: e.activation(out=aa[:, jc, :], in_=rr[:, jc, :], func=AF.Exp, scale=hvec[:, 32 + cc:32 + cc + 1], bias=hvec[:, 32 + cc:32 + cc + 1]),
                                   r=[("rr", jc), "hvec"], w=[("aa", jc)])
                                op("act", lambda e, jc=jc, cc=cc: e.activation(out=a2[:, jc, :], in_=rr[:, jc, :], func=AF.Exp, scale=cvec[:, cc:cc + 1], bias=cvec[:, cc:cc + 1]),
                                   r=[("rr", jc), "cvec"], w=[("a2", jc)])
                            for jc in range(2):
                                op("act", lambda e, jc=jc: e.activation(out=a2[:, jc, :], in_=a2[:, jc, :], func=AF.Sqrt, bias=0.25, scale=-0.25),
                                   r=[("a2", jc)], w=[("a2", jc)])
                            for jc in range(2):
                                ch = hb * 2 + jc
                                op("dve", lambda e, jc=jc: e.scalar_tensor_tensor(out=gi[:, jc, :], in0=gi[:, jc, :], scalar=1.0, in1=uc[:, jc, :], op0=ALU.add, op1=ALU.mult),
                                   r=[("gi", jc), ("uc", jc)], w=[("gi", jc)])
                                op("dve", lambda e, jc=jc: e.tensor_tensor(out=gi[:, jc, :], in0=gi[:, jc, :], in1=a2[:, jc, :], op=ALU.mult),
                                   r=[("gi", jc), ("a2", jc)], w=[("gi", jc)])
                                init = 0.0 if tt == 0 else carry[:, jc:jc + 1]
                                op("dve", lambda e, jc=jc, init=init: e.tensor_tensor_scan(out=hs[:, jc, :], data0=aa[:, jc, :], data1=gi[:, jc, :], initial=init,
                                                                                           op0=ALU.mult, op1=ALU.add),
                                   r=[("aa", jc), ("gi", jc), ("carry", jc)], w=[("hs", jc)])
                                op("dve", lambda e, jc=jc: e.tensor_copy(out=carry[:, jc:jc + 1], in_=hs[:, jc, TT - 1:TT]), r=[("hs", jc)], w=[("carry", jc)])
                                op("dve", lambda e, jc=jc, ch=ch: e.tensor_tensor(out=yT[:, ch, tsl], in0=hs[:, jc, :], in1=sgn[:, jc, :], op=ALU.mult),
                                   r=[("hs", jc), ("sg", n % 2, jc)], w=[("yT", tt)])

                        load_block(l, 1)
                        stA(0)
                        for n in range(16):
                            stB(n)
                            if n + 1 < 16:
                                stA(n + 1)
                            stC(n)
                            if n % 4 == 3 and n // 4 + 2 < 4:
                                load_block(l, n // 4 + 2)
                        stage(7)
                        for jg in range(4):
                            dma("pool", f"d_wr{jg}", ring[jg][:], wview(rgwout_d[l], jg * 256, 256), w=[("ring", jg)])
                        n = 0
                        for jg in range(4):
                            for jc in range(2):
                                ch = jg * 2 + jc
                                gcol = l * 24 + 16 + ch
                                for tt in range(NT):
                                    tsl = slice(tt * TT, (tt + 1) * TT)
                                    b = n % 4
                                    n += 1
                                    for kc in range(KC):
                                        op("pe", lambda e, kc=kc, b=b, jg=jg, jc=jc: e.matmul(out=ps[b][:, :], lhsT=ring[jg][:, kc, jc * 128:(jc + 1) * 128], rhs=yT[:, kc, tsl],
                                                                                          start=(kc == 0), stop=(kc == KC - 1)),
                                           r=[("ring", jg), ("yT", tt)], w=[PS(b)], nl=1)
                                    op("dve", lambda e, b=b, ch=ch, gcol=gcol: e.scalar_tensor_tensor(out=xT[:, ch, tsl], in0=ps[b][:, :], scalar=modT[:, gcol:gcol + 1], in1=xT[:, ch, tsl],
                                                                                                      op0=ALU.mult, op1=ALU.add),
                                       r=[PS(b), "modT", ("xT", tt, ch)], w=[("xT", tt, ch)])
                except _Stop:
                    pass
                sc.barrier()

        if nlayers >= 1:
            emit_rg_layers(list(range(min(nlayers, 2))))


        def emit_attention(nlay):
            with contextlib.ExitStack() as esa:
                sba = lambda n, s, dt=F32: _alloc(esa, n, s, dt)
                KT = sba("KT", [128, 8, S], BF16)
                V = sba("V", [128, 16, D], BF16)
                kmf = sba("kmf", [128, 64])
                kmt = sba("kmt", [128, 64])
                kmh = sba("kmh", [128, 64], BF16)
                kml = sba("kml", [128, 64], BF16)
                wq = [sba(f"wq{i}", [128, KC, 128], BF16) for i in range(2)]
                wg = [sba(f"wg{i}", [128, KC, 128], BF16) for i in range(2)]
                gs = [sba(f"gs{i}", [128, 8]) for i in range(4)]
                m8 = sba("m8", [128, 8])
                emit_norm(4, 96)
                sc.barrier()
                wv = [scr[:, i * 2048:(i + 1) * 2048].bitcast(BF16).rearrange("p (a b) -> p a b", b=512) for i in range(2)]
                for jg in range(2):
                    dma("pool", f"d_wv{jg}", wv[jg], wview(wkv_d, D + jg * 512, 512), w=[("wv", jg)])
                for h in range(8):
                    sl = h % 2
                    dma("pool", f"d_wq{sl}", wq[sl][:], wview(wkv_d, h * 128, 128), w=[("wq", sl)])
                    for tt in range(NT):
                        tsl = slice(tt * TT, (tt + 1) * TT)
                        b = tt % 2
                        for kc in range(KC):
                            op("pe", lambda e, kc=kc, b=b: e.matmul(out=ps[b][:, :], lhsT=wq[sl][:, kc, :], rhs=hT[:, kc, tsl], start=(kc == 0), stop=(kc == KC - 1)),
                               r=[("wq", sl), ("hT", tt)], w=[PS(b)], nl=1)
                        op("act", lambda e, b=b: e.activation(out=KT[:, h, tsl], in_=ps[b][:, :], func=AF.Copy), r=[PS(b)], w=[("KT", h)])
                        c0 = h * 8 + tt * 2
                        op("dve", lambda e, b=b, c0=c0: e.tensor_reduce(out=kmf[:, c0:c0 + 2], in_=ps[b][:, :].rearrange("p (a b) -> p a b", b=256), axis=AX.X, op=ALU.add),
                           r=[PS(b)], w=["kmf"])
                for jg in range(2):
                    for n in range(16):
                        b = 2 + n % 4
                        for kc in range(KC):
                            op("pe", lambda e, kc=kc, b=b, n=n: e.matmul(out=ps[b][:, :], lhsT=hT[:, kc, n * 128:(n + 1) * 128], rhs=wv[jg][:, kc, :], start=(kc == 0), stop=(kc == KC - 1)),
                               r=[("hT", n // 4), ("wv", jg)], w=[PS(b)], nl=1)
                        if n % 2 == 0:
                            op("act", lambda e, b=b, n=n: e.activation(out=V[:, n, jg * 512:(jg + 1) * 512], in_=ps[b][:, :], func=AF.Copy), r=[PS(b)], w=[("V", n)])
                        else:
                            op("dve", lambda e, b=b, n=n: e.tensor_copy(out=V[:, n, jg * 512:(jg + 1) * 512], in_=ps[b][:, :]), r=[PS(b)], w=[("V", n)])
                op("dve", lambda e: e.tensor_scalar(out=kmh[:], in0=kmf[:], scalar1=1.0 / 256, scalar2=None, op0=ALU.mult), r=["kmf"], w=["kmh"])
                op("dve", lambda e: e.tensor_copy(out=kmt[:], in_=kmh[:]), r=["kmh"], w=["kmt"])
                op("dve", lambda e: e.scalar_tensor_tensor(out=kml[:], in0=kmf[:], scalar=1.0 / 256, in1=kmt[:], op0=ALU.mult, op1=ALU.subtract), r=["kmf", "kmt"], w=["kml"])
                for i in range(4):
                    op("dve", lambda e, i=i: e.memset(gs[i][:], -1e30), w=[("gs", i)])
                sc.barrier()

                QTb = [scr[:, 0:1024].bitcast(BF16), scr[:, 1024:2048].bitcast(BF16)]
                SGb = [scr[:, 2048:3072].bitcast(BF16), scr[:, 3072:4096].bitcast(BF16)]
                Pb = [sba(f"Pb{i}", [128, 512], BF16) for i in range(3)]
                PTb = [sba(f"PTb{i}", [128, 512], BF16) for i in range(3)]
                Onb = [sba(f"On{i}", [128, 128], BF16) for i in range(2)]
                rsb = [sba(f"rs{i}", [128, 16]) for i in range(4)]
                rsumb = [sba(f"rsum{i}", [128, 1]) for i in range(2)]
                rinvb = [sba(f"rinv{i}", [128, 1]) for i in range(2)]
                mbb = [sba(f"mb{i}", [128, 8]) for i in range(2)]
                wo3 = [sba(f"wo3_{i}", [128, D], BF16) for i in range(3)]
                Tb1 = ps[2][:, :].bitcast(BF16)
                SBK = (0, 1)
                BGK = (7, 3)
                bgc = [0]
                OTb = ps[6][:, 256:320].bitcast(BF16)
                cnt = {"u": 0, "q": 0}

                def load_qg(la, h):
                    sl = h % 2
                    dma("pool", f"d_wq{sl}", wq[sl][:], wview(attwin_d[la], h * 128, 128), w=[("wq", sl)])
                    dma("pool", f"d_wg{sl}", wg[sl][:], wview(attwin_d[la], D + h * 128, 128), w=[("wg", sl)])

                def load_o(la, h):
                    sl = h % 3
                    dma("pool", f"d_wo{sl}", wo3[sl][:], attwout_d[la][h * 128:(h + 1) * 128, :], w=[("wo", sl)])

                def proj_group(hn, which, tt, bk=7, part=None):
                    sl = hn % 2
                    tsl = slice(tt * TT, (tt + 1) * TT)
                    wt = wq[sl] if which == "q" else wg[sl]
                    wtok = ("wq", sl) if which == "q" else ("wg", sl)
                    kcs = range(KC) if part is None else range(4 * part, 4 * part + 4)
                    for kc in kcs:
                        op("pe", lambda e, kc=kc: e.matmul(out=ps[bk][:, :], lhsT=wt[:, kc, :], rhs=hT[:, kc, tsl], start=(kc == 0), stop=(kc == KC - 1)),
                           r=[wtok, ("hT", tt)], w=[PS(bk)], nl=1)
                    if part == 0:
                        return
                    if which == "q":
                        op("act", lambda e: e.activation(out=QTb[sl][:, tsl], in_=ps[bk][:, :], func=AF.Copy), r=[PS(bk)], w=[("QT", sl, tt)])
                    else:
                        sgt = [("SG", sl, 4 * tt + j) for j in range(4)]
                        op("act", lambda e: e.activation(out=SGb[sl][:, tsl], in_=ps[bk][:, :], func=AF.Tanh, scale=0.5), r=[PS(bk)], w=sgt)
                        op("dve", lambda e: e.scalar_tensor_tensor(out=SGb[sl][:, tsl], in0=SGb[sl][:, tsl], scalar=1.0, in1=ps[bk][:, :], op0=ALU.add, op1=ALU.mult),
                           r=[PS(bk)] + sgt, w=sgt)

                def outproj_group(l, hp, jc, tt, bk=7):
                    tsl = slice(tt * TT, (tt + 1) * TT)
                    gcol = l * 24 + 16 + jc
                    op("pe", lambda e: e.matmul(out=ps[bk][:, :], lhsT=wo3[hp % 3][:, jc * 128:(jc + 1) * 128], rhs=SGb[hp % 2][:, tsl], start=True, stop=True),
                       r=[("wo", hp % 3)] + [("SG", hp % 2, 4 * tt + j) for j in range(4)], w=[PS(bk)], nl=1)
                    op("dve", lambda e: e.scalar_tensor_tensor(out=xT[:, jc, tsl], in0=ps[bk][:, :], scalar=modT[:, gcol:gcol + 1], in1=xT[:, jc, tsl], op0=ALU.mult, op1=ALU.add),
                       r=[PS(bk), "modT", ("xT", tt, jc)], w=[("xT", tt, jc)])

                def nextbg():
                    bgc[0] += 1
                    return BGK[bgc[0] % 2]

                def attn_head(la, l, h, bg_early, bg_late):
                    qb = h % 2
                    QT, SG = QTb[qb], SGb[qb]
                    units = []
                    for i in range(16):
                        nk = 128 * (i + 1)
                        nb = (nk + 511) // 512
                        for c in range(nb):
                            units.append((i, c, min(512, nk - 512 * c), c == 0, c == nb - 1))
                    U = len(units)
                    u0 = cnt["u"]
                    q0 = cnt["q"]
                    nseg = {}

                    def st1(s):
                        i, c, w_, first, last = units[s]
                        qs = slice(i * 128, (i + 1) * 128)
                        sbk = SBK[(u0 + s) % 2]
                        q2 = (q0 + i) % 2
                        op("pe", lambda e: e.matmul(out=ps[sbk][:, 0:w_], lhsT=QT[:, qs], rhs=KT[:, h, c * 512:c * 512 + w_], start=True, stop=(not last)),
                           r=[("QT", qb, i // 4), ("KT", h)], w=[PS(sbk)], nl=1)
                        if last:
                            op("pe", lambda e: e.matmul(out=ps[sbk][:, w_ - 128:w_], lhsT=identb[:], rhs=cmaskb[:], start=False, stop=True),
                               r=["identb", "cmaskb"], w=[PS(sbk)], nl=1)
                        if first and i >= 8:
                            blk = i // 2
                            g_ = gs[blk - 4]
                            op("pe", lambda e: e.matmul(out=ps[6][:, 0:8], lhsT=QT[:, qs], rhs=kmh[:, h * 8:(h + 1) * 8], start=True, stop=False),
                               r=[("QT", qb, i // 4), "kmh"], w=[PS(6)], nl=1)
                            op("pe", lambda e: e.matmul(out=ps[6][:, 0:8], lhsT=QT[:, qs], rhs=kml[:, h * 8:(h + 1) * 8], start=False, stop=True),
                               r=[("QT", qb, i // 4), "kml"], w=[PS(6)], nl=1)
                            op("dve", lambda e: e.tensor_copy(out=g_[:, 0:blk], in_=ps[6][:, 0:blk]), r=[PS(6)], w=[("gs", blk - 4)])
                            op("dve", lambda e: e.max(out=m8[:], in_=g_[:]), r=[("gs", blk - 4)], w=["m8"])
                            op("dve", lambda e: e.tensor_scalar(out=mbb[q2][:], in0=g_[:], scalar1=m8[:, 2:3], scalar2=NEG, op0=ALU.is_lt, op1=ALU.mult),
                               r=[("gs", blk - 4), "m8"], w=[("mb", q2)])

                    def st2(s):
                        i, c, w_, first, last = units[s]
                        sbk = SBK[(u0 + s) % 2]
                        pb = (u0 + s) % 3
                        q2 = (q0 + i) % 2
                        q4 = (q0 + i) % 4
                        blk = i // 2
                        segs = []
                        if i < 8:
                            segs.append((0, w_, None))
                        else:
                            for b_ in (2 * c, 2 * c + 1):
                                o0 = (b_ - 2 * c) * 256
                                if b_ < blk:
                                    segs.append((o0, o0 + 256, b_))
                                elif b_ == blk:
                                    segs.append((o0, w_, None))
                        for (o0, o1, bcol) in segs:
                            k = nseg.get(i, 0)
                            nseg[i] = k + 1
                            kw = dict(out=Pb[pb][:, o0:o1], in_=ps[sbk][:, o0:o1], func=AF.Exp, scale=SM_SCALE, accum_out=rsb[q4][:, k:k + 1])
                            rr_ = [PS(sbk)]
                            if bcol is not None:
                                kw["bias"] = mbb[q2][:, bcol:bcol + 1]
                                rr_.append(("mb", q2))
                            op("act", lambda e, kw=kw: e.activation(**kw), r=rr_, w=[("P", pb), ("rs", q4)])

                    def st3(s):
                        i, c, w_, first, last = units[s]
                        pb = (u0 + s) % 3
                        for j in range(w_ // 128):
                            op("pe", lambda e, j=j: e.transpose(out=Tb1[:, j * 128:(j + 1) * 128], in_=Pb[pb][:, j * 128:(j + 1) * 128], identity=identb[:]),
                               r=[("P", pb), "identb"], w=[PS(2)], nl=1)
                        op("dve", lambda e: e.tensor_copy(out=PTb[pb][:, 0:w_], in_=Tb1[:, 0:w_]), r=[PS(2)], w=[("PT", pb)])

                    def st5(s):
                        i, c, w_, first, last = units[s]
                        pb = (u0 + s) % 3
                        q2 = (q0 + i) % 2
                        q4 = (q0 + i) % 4
                        nj = w_ // 128
                        for j in range(nj):
                            op("pe", lambda e, j=j: e.matmul(out=ps[4 + q2][:, 0:128], lhsT=PTb[pb][:, j * 128:(j + 1) * 128], rhs=V[:, 4 * c + j, h * 128:(h + 1) * 128],
                                                             start=(first and j == 0), stop=(last and j == nj - 1)),
                               r=[("PT", pb), ("V", 4 * c + j)], w=[PS(4 + q2)], nl=1)
                        if last:
                            ns = nseg[i]
                            op("dve", lambda e: e.tensor_reduce(out=rsumb[q2][:], in_=rsb[q4][:, 0:ns], axis=AX.X, op=ALU.add), r=[("rs", q4)], w=[("rsum", q2)])
                            op("dve", lambda e: e.reciprocal(out=rinvb[q2][:], in_=rsumb[q2][:]), r=[("rsum", q2)], w=[("rinv", q2)])
                            op("act", lambda e: e.activation(out=Onb[q2][:], in_=ps[4 + q2][:, 0:128], func=AF.Identity, scale=rinvb[q2][:, 0:1]),
                               r=[PS(4 + q2), ("rinv", q2)], w=[("On", q2)])

                    def st6(s):
                        i, c, w_, first, last = units[s]
                        if not last:
                            return
                        qs = slice(i * 128, (i + 1) * 128)
                        q2 = (q0 + i) % 2
                        op("pe", lambda e: e.transpose(out=OTb, in_=Onb[q2][:], identity=identb[:]), r=[("On", q2), "identb"], w=[PS(6)], nl=1)
                        op("dve", lambda e: e.tensor_tensor(out=SG[:, qs], in0=OTb, in1=SG[:, qs], op=ALU.mult), r=[PS(6), ("SG", qb, i)], w=[("SG", qb, i)])

                    st1(0)
                    for s in range(U + 3):
                        if s + 1 < U:
                            st1(s + 1)
                        if s < U:
                            st2(s)
                        if 0 <= s - 1 < U:
                            st3(s - 1)
                        if 0 <= s - 2 < U:
                            st5(s - 2)
                        if 0 <= s - 3 < U:
                            st6(s - 3)
                        if bg_early:
                            bg_early.pop(0)()
                        elif bg_late and s >= U - len(bg_late) - 2:
                            bg_late.pop(0)()
                    while bg_early:
                        bg_early.pop(0)()
                    while bg_late:
                        bg_late.pop(0)()
                    cnt["u"] += U
                    cnt["q"] += 16

                for la in range(nlay):
                    l = 2 + la
                    emit_norm(l, l * 24)
                    sc.barrier()
                    load_qg(la, 0)
                    load_qg(la, 1)
                    load_o(la, 0)
                    nb_ = 0
                    for which in ("q", "g"):
                        for tt in range(NT):
                            proj_group(0, which, tt, bk=(7, 6, 0, 1)[nb_ % 4])
                            nb_ += 1
                    for h in range(8):
                        early, late = [], []
                        if h >= 1:
                            for jc in range(8):
                                for tt in range(NT):
                                    early.append(lambda jc=jc, tt=tt, hp=h - 1: outproj_group(l, hp, jc, tt, bk=nextbg()))
                        if h + 1 < 8:
                            for tt in range(NT):
                                pos = min(len(early), 10 * tt + 4)
                                early.insert(pos, (lambda tt=tt, hn=h + 1: proj_group(hn, "q", tt, part=0)))
                                early.insert(pos + 1, (lambda tt=tt, hn=h + 1: proj_group(hn, "q", tt, part=1)))
                            for tt in range(NT):
                                late.append(lambda tt=tt, hn=h + 1: proj_group(hn, "g", tt, part=0))
                                late.append(lambda tt=tt, hn=h + 1: proj_group(hn, "g", tt, part=1))
                        if h + 2 < 8:
                            load_qg(la, h + 2)
                        if h + 1 < 8:
                            load_o(la, h + 1)
                        attn_head(la, l, h, early, late)
                    nb_ = 0
                    for jc in range(8):
                        for tt in range(NT):
                            outproj_group(l, 7, jc, tt, bk=(7, 6, 0, 1)[nb_ % 4])
                            nb_ += 1
                    sc.barrier()

        if nlayers >= 3:
            emit_attention(nlayers - 2)

        with contextlib.ExitStack() as es3:
            sb3 = lambda n, s, dt=F32: _alloc(es3, n, s, dt)
            of2 = [sb3(f"of{i}", [128, KC, TT]) for i in range(2)]
            ot = [sb3(f"ot{i}", [128, D]) for i in range(4)]
            ov = out_d.rearrange("(n p) d -> n p d", p=128)
            if not dbg:
                emit_rstd_all()
            for tt in range(NT):
                tsl = slice(tt * TT, (tt + 1) * TT)
                of = of2[tt % 2]
                if dbg:
                    for kc in range(KC):
                        op("dve", lambda e, kc=kc: e.tensor_copy(out=of[:, kc, :], in_=xT[:, kc, tsl]), r=[("xT", tt, kc)], w=[("of", tt % 2, kc)])
                else:
                    for kc in range(KC):
                        op("dve", lambda e, kc=kc: e.scalar_tensor_tensor(out=of[:, kc, :], in0=xT[:, kc, tsl], scalar=Avec[:, 5 * KC + kc:5 * KC + kc + 1],
                                                                         in1=ps[4 + tt][:, :], op0=ALU.mult, op1=ALU.mult),
                           r=[("xT", tt, kc), PS(4 + tt), "Avec"], w=[("of", tt % 2, kc)])
                for sub in range(4):
                    n = tt * 4 + sub
                    sl = n % 4
                    for half in range(2):
                        b = (n * 2 + half) % 4
                        for q in range(4):
                            kc = half * 4 + q
                            op("pe", lambda e, b=b, q=q, kc=kc, sub=sub: e.transpose(out=ps[b][:, q * 128:(q + 1) * 128], in_=of[:, kc, sub * 128:(sub + 1) * 128], identity=ident),
                               r=[("of", tt % 2, kc), "cst"], w=[PS(b)], nl=1)
                        if half == 0:
                            op("act", lambda e, b=b, sl=sl: e.activation(out=ot[sl][:, 0:512], in_=ps[b][:, :], func=AF.Copy), r=[PS(b)], w=[("ot", sl)])
                        else:
                            op("dve", lambda e, b=b, sl=sl: e.tensor_copy(out=ot[sl][:, 512:1024], in_=ps[b][:, :]), r=[PS(b)], w=[("ot", sl)])
                    dma("sp", f"d_out{sl}", ov[n], ot[sl][:], r=[("ot", sl)])
            sc.final_wait("sp", "d_out0")
            sc.final_wait("sp", "d_out1")
        print(f"[build] instructions={sc.nins} waits={sc.nwait} min_sbuf_remaining={min(REM)}")
    return nc


def _prep_inputs(inp):
    f32 = np.float32
    def fm(v):
        v = np.asarray(v, f32).reshape(-1, 128)
        return np.ascontiguousarray(v.T)
    pv = np.zeros((128, NPV), f32)
    for l in range(4):
        pv[:, PV_NORMG + l * 8:PV_NORMG + l * 8 + 8] = fm(inp["norm_g"][l])
        pv[:, PV_MODB + l * 24:PV_MODB + l * 24 + 24] = fm(inp["mod_b"][l])
    pv[:, PV_KVG:PV_KVG + 8] = fm(inp["kv_norm_g"])
    pv[:, PV_FING:PV_FING + 8] = fm(inp["final_norm_g"])
    pv[:, PV_KVMODB:PV_KVMODB + 16] = fm(inp["kv_mod_b"])
    for l in range(2):
        for k in range(4):
            pv[:, PV_CONVW + l * 32 + k * 8:PV_CONVW + l * 32 + k * 8 + 8] = fm(inp["rg_conv_w"][l, k])
        pv[:, PV_CONVB + l * 8:PV_CONVB + l * 8 + 8] = fm(inp["rg_conv_b"][l])
        pv[:, PV_BA + l * 8:PV_BA + l * 8 + 8] = fm(inp["rg_b_a"][l])
        pv[:, PV_BX + l * 8:PV_BX + l * 8 + 8] = fm(inp["rg_b_x"][l])
        pv[:, PV_LAM + l * 8:PV_LAM + l * 8 + 8] = fm(inp["rg_lambda"][l])
    cst = np.zeros((128, 384), f32)
    cst[:, 0:128] = np.eye(128, dtype=f32)
    q = np.arange(128)[:, None]
    k = np.arange(128)[None, :]
    cst[:, 128:256] = np.where(k <= q, 0.0, NEG).astype(f32)
    cst[:, 256:384] = 1.0
    shared = {k2: np.ascontiguousarray(np.asarray(inp[k2], f32)) for k2 in
              ("mod_w", "kv_mod_w", "rg_w_in", "rg_w_a", "rg_w_x", "rg_w_out", "w_kv", "att_w_in", "att_w_out")}
    maps = []
    for b in range(8):
        m = dict(shared)
        m["x"] = np.ascontiguousarray(np.asarray(inp["x"][b], f32))
        m["cT"] = fm(inp["c"][b])
        m["pv"] = pv
        m["cst"] = cst
        maps.append(m)
    return maps


def kernel(**inputs):
    nl = int(os.environ.get("K_NLAYERS", "4"))
    dbg = os.environ.get("K_DBG", "0") == "1"
    nc = build(nl, dbg)
    maps = _prep_inputs(inputs)
    res = run_bass_kernel_spmd(nc, maps, core_ids=list(range(8)))
    return np.stack([r["out"] for r in res.results], axis=0).astype(np.float32)
```

```python
import contextlib
import os
import numpy as np
import concourse.bass as bass
import concourse.mybir as mybir
from concourse.bass_utils import run_bass_kernel_spmd

F32 = mybir.dt.float32
F32R = mybir.dt.float32r
BF16 = mybir.dt.bfloat16
AF = mybir.ActivationFunctionType
ALU = mybir.AluOpType
AX = mybir.AxisListType

S = 2048
D = 1024
KC = 8
TT = 512
NT = S // TT
EPS = 1e-6
NEG = -30000.0
SM_SCALE = 128 ** -0.5

PV_NORMG = 0
PV_KVG = 32
PV_FING = 40
PV_MODB = 48
PV_KVMODB = 144
PV_CONVW = 160
PV_CONVB = 224
PV_BA = 240
PV_BX = 256
PV_LAM = 272
NPV = 288
NMOD = 112


class _Stop(Exception):
    pass


STOP = int(os.environ.get("K_STOP", "0"))
ATTACH = os.environ.get("K_ATTACH", "1") == "1"


def stage(n):
    if STOP == n:
        raise _Stop()


class Sch:
    def __init__(self, nc, es):
        self.nc, self.es = nc, es
        self.E = {"pe": nc.tensor, "act": nc.scalar, "dve": nc.vector, "pool": nc.gpsimd, "sp": nc.sync}
        self.sem, self.cnt = {}, {}
        for e in self.E:
            self.mksem(e)
        self.known = {e: {} for e in self.E}
        self.lw, self.rd = {}, {}
        self.nwait = 0
        self.nins = 0
        self.pend = None

    def mksem(self, name):
        self.sem[name] = self.es.enter_context(self.nc.semaphore("s_" + name))
        self.cnt[name] = 0

    def _deps(self, e, r, w, strict=False):
        deps = []
        inorder = (e in ("act", "dve", "pe")) and not strict
        for t in r:
            ev = self.lw.get(t)
            if ev is not None:
                deps.append(ev)
            if isinstance(t, tuple) and t[0] == "ps":
                for ev in self.rd.get(t, {}).values():
                    if ev[0] != e:
                        deps.append(ev)
        for t in w:
            ev = self.lw.get(t)
            if ev is not None and not (inorder and ev[0] == e):
                deps.append(ev)
            for ev in self.rd.get(t, {}).values():
                if not (inorder and ev[0] == e):
                    deps.append(ev)
        return deps

    def _wait(self, e, deps, attach=False):
        K = self.known[e]
        need = {}
        for (sn, v, vc) in deps:
            if K.get(sn, 0) < v:
                need[sn] = max(need.get(sn, 0), v)
        implied = {}
        for (sn, v, vc) in deps:
            if need.get(sn, 0) == v:
                for k2, v2 in vc.items():
                    if k2 != sn and implied.get(k2, 0) < v2:
                        implied[k2] = v2
        pend = None
        for sn, v in need.items():
            if implied.get(sn, 0) >= v:
                continue
            if attach and pend is None:
                pend = (sn, v)
                continue
            self.E[e].wait_ge(self.sem[sn], v)
            self.nwait += 1
        self.pend = pend
        for (sn, v, vc) in deps:
            for k2, v2 in vc.items():
                if K.get(k2, 0) < v2:
                    K[k2] = v2
            if K.get(sn, 0) < v:
                K[sn] = v

    def _reg(self, ev, r, w):
        for t in r:
            self.rd.setdefault(t, {})[ev[0]] = ev
        for t in w:
            self.lw[t] = ev
            self.rd[t] = {}

    def op(self, e, fn, r=(), w=(), strict=False, multi=False, nl=None):
        if e == "pe" and nl is not None and ATTACH:
            self._wait(e, self._deps(e, r[:nl], (), strict), attach=False)
            self._wait(e, self._deps(e, r[nl:], w, strict), attach=True)
        else:
            self._wait(e, self._deps(e, r, w, strict), attach=(e in ("act", "dve") and not multi and ATTACH))
        ins = fn(self.E[e])
        if self.pend is not None:
            ins._wait_ge(self.sem[self.pend[0]], self.pend[1])
            self.pend = None
        self.cnt[e] += 1
        ins.then_inc(self.sem[e], 1)
        self.nins += 1
        self._reg((e, self.cnt[e], dict(self.known[e])), r, w)

    def dma(self, q, dsem, out, in_, r=(), w=()):
        if dsem not in self.sem:
            self.mksem(dsem)
        self._wait(q, self._deps(q, r, w))
        ins = self.E[q].dma_start(out=out, in_=in_)
        self.cnt[dsem] += 16
        ins.then_inc(self.sem[dsem], 16)
        self.nins += 1
        self._reg((dsem, self.cnt[dsem], dict(self.known[q])), r, w)

    def barrier(self):
        for e in self.E:
            for sn, c in self.cnt.items():
                if c > 0 and self.known[e].get(sn, 0) < c:
                    self.E[e].wait_ge(self.sem[sn], c)
                    self.known[e][sn] = c
        self.lw.clear()
        self.rd.clear()

    def alias(self, olds, news):
        evs = []
        for t in olds:
            if t in self.lw:
                evs.append(self.lw[t])
            evs.extend(self.rd.get(t, {}).values())
        for t in news:
            d = self.rd.setdefault(t, {})
            for ev in evs:
                cur = d.get(ev[0])
                if cur is None or cur[1] < ev[1]:
                    d[ev[0]] = ev

    def final_wait(self, e, sn):
        self.E[e].wait_ge(self.sem[sn], self.cnt[sn])


def build(nlayers=4, dbg=False):
    nc = bass.Bass("TRN2", target_bir_lowering=False)
    dram = lambda n, s: nc.dram_tensor(n, s, F32, kind="ExternalInput").ap()
    x_d = dram("x", [S, D])
    c_d = dram("cT", [128, KC])
    pv_d = dram("pv", [128, NPV])
    cst_d = dram("cst", [128, 384])
    modw_d = dram("mod_w", [4, D, 3 * D])
    kvmodw_d = dram("kv_mod_w", [D, 2 * D])
    rgwin_d = dram("rg_w_in", [2, D, 2 * D])
    rgwa_d = dram("rg_w_a", [2, 4, 256, 256])
    rgwx_d = dram("rg_w_x", [2, 4, 256, 256])
    rgwout_d = dram("rg_w_out", [2, D, D])
    wkv_d = dram("w_kv", [D, 2 * D])
    attwin_d = dram("att_w_in", [2, D, 2 * D])
    attwout_d = dram("att_w_out", [2, D, D])
    out_d = nc.dram_tensor("out", [S, D], F32, kind="ExternalOutput").ap()

    es = contextlib.ExitStack()
    with es:
        sc = Sch(nc, es)
        op, dma = sc.op, sc.dma
        REM = [nc.sbuf_bytes_remaining]

        def _alloc(stack, n, s, dt):
            t = stack.enter_context(nc.sbuf_tensor("sb_" + n, s, dt))
            REM.append(nc.sbuf_bytes_remaining)
            return t
        sb = lambda n, s, dt=F32: _alloc(es, n, s, dt)

        xT = sb("xT", [128, KC, S])
        hT = sb("hT", [128, KC, S], BF16)
        cst = sb("cst", [128, 384])
        pv = sb("pv", [128, NPV])
        modT = sb("modT", [128, NMOD])
        identb = sb("identb", [128, 128], BF16)
        cmaskb = sb("cmaskb", [128, 128], BF16)
        onesb = sb("onesb", [128, 128], BF16)
        Avec = sb("Avec", [128, 6 * KC])
        cvec = sb("cvec", [128, 2 * KC])
        hvec = sb("hvec", [128, 48])
        scr = sb("scr", [128, 4096])
        sqb = scr[:, 0:2048].bitcast(BF16).rearrange("p (a b) -> p a b", b=TT)
        rms = scr[:, 2048:2560]
        rstd = scr[:, 2560:3072]
        htmp = scr[:, 3072:4096].rearrange("p (a b) -> p a b", b=TT)
        ident = cst[:, 0:128]
        ones = cst[:, 256:384]

        ps = [es.enter_context(nc.psum_tensor(f"ps{i}", [128, 512], F32)) for i in range(8)]
        PS = lambda i: ("ps", i)

        def xtok(tt):
            return [("xT", tt, k) for k in range(KC)]

        def col(base, n=KC):
            return pv[:, base:base + n]

        dma("sp", "d_cst", cst[:], cst_d[:, :], w=["cst"])
        dma("sp", "d_pv", pv[:], pv_d[:, :], w=["pv"])
        cT = sb("cTs", [128, KC])
        dma("sp", "d_c", cT[:], c_d[:, :], w=["cT"])
        op("dve", lambda e: e.tensor_copy(out=identb[:], in_=cst[:, 0:128]), r=["cst", "pv", "cT"], w=["identb"])
        op("dve", lambda e: e.tensor_copy(out=cmaskb[:], in_=cst[:, 128:256]), r=["cst"], w=["cmaskb"])
        op("dve", lambda e: e.tensor_copy(out=onesb[:], in_=cst[:, 256:384]), r=["cst"], w=["onesb"])
        cs = sb("cs", [128, KC])
        op("act", lambda e: e.activation(out=cs[:], in_=cT[:], func=AF.Silu), r=["cT", "cst", "pv"], w=["cs"])

        with contextlib.ExitStack() as es2:
            sb2 = lambda n, s, dt=F32: _alloc(es2, n, s, dt)
            xin = [sb2(f"xin{i}", [128, D]) for i in range(3)]
            wm = [sb2(f"wm{i}", [128, KC, 512], BF16) for i in range(4)]
            msb = sb2("msb", [128, 512])
            csrep = sb2("csrep", [128, KC, 128], BF16)
            for kc in range(KC):
                op("dve", lambda e, kc=kc: e.tensor_scalar(out=csrep[:, kc, :], in0=ones, scalar1=cs[:, kc:kc + 1], scalar2=None, op0=ALU.mult),
                   r=["cs", "cst"], w=[("csrep", kc)])
            xv = x_d.rearrange("(n p) d -> n p d", p=128)
            for n in range(S // 128):
                sl = n % 3
                dma("sp", f"d_x{sl}", xin[sl][:], xv[n], w=[("xin", sl)])
                for half in range(2):
                    b = (n * 2 + half) % 2
                    for q in range(4):
                        kc = half * 4 + q
                        op("pe", lambda e, b=b, q=q, kc=kc, sl=sl: e.transpose(out=ps[b][:, q * 128:(q + 1) * 128], in_=xin[sl][:, kc * 128:(kc + 1) * 128], identity=ident),
                           r=[("xin", sl), "cst"], w=[PS(b)])
                    src = ps[b][:, :].rearrange("p (a b) -> p a b", b=128)
                    dst = xT[:, half * 4:half * 4 + 4, n * 128:(n + 1) * 128]
                    if half == 0:
                        op("act", lambda e, src=src, dst=dst: e.activation(out=dst, in_=src, func=AF.Copy), r=[PS(b)], w=[("xT", n // 4, k) for k in range(half * 4, half * 4 + 4)])
                    else:
                        op("dve", lambda e, src=src, dst=dst: e.tensor_copy(out=dst, in_=src), r=[PS(b)], w=[("xT", n // 4, k) for k in range(half * 4, half * 4 + 4)])

            groups = [(l, modw_d[l], jg, l * 24 + jg * 4, PV_MODB + l * 24 + jg * 4) for l in range(4) for jg in range(6)]
            groups += [(4, kvmodw_d, jg, 96 + jg * 4, PV_KVMODB + jg * 4) for jg in range(4)]
            for gi, (l, wd, jg, mcol, bcol) in enumerate(groups):
                sl = gi % 4
                wsrc = wd[:, jg * 512:(jg + 1) * 512].rearrange("(kc p) j -> p kc j", p=128)
                dma("pool", f"d_m{sl}", wm[sl][:], wsrc, w=[("wm", sl)])
                b = 2 + (gi % 2)
                for kc in range(KC):
                    op("pe", lambda e, kc=kc, sl=sl, b=b: e.matmul(out=ps[b][:, :], lhsT=csrep[:, kc, :], rhs=wm[sl][:, kc, :],
                                                                     start=(kc == 0), stop=(kc == KC - 1)),
                       r=[("wm", sl)] + [("csrep", k) for k in range(KC)], w=[PS(b)])
                op("act", lambda e, b=b: e.activation(out=msb[:], in_=ps[b][:, :], func=AF.Copy), r=[PS(b)], w=["msb"])
                b2 = 4 + (gi % 2)
                for q in range(4):
                    op("pe", lambda e, q=q, b2=b2: e.transpose(out=ps[b2][:, q * 128:(q + 1) * 128], in_=msb[:, q * 128:(q + 1) * 128], identity=ident),
                       r=["msb", "cst"], w=[PS(b2)])
                srcv = ps[b2][:, :].rearrange("p (a b) -> p a b", b=128)[:, :, 0]
                op("dve", lambda e, srcv=srcv, mcol=mcol, bcol=bcol: e.tensor_tensor(out=modT[:, mcol:mcol + 4], in0=srcv, in1=pv[:, bcol:bcol + 4], op=ALU.add),
                   r=[PS(b2), "pv"], w=["modT"])

            for l in range(4):
                op("dve", lambda e, l=l: e.scalar_tensor_tensor(out=Avec[:, l * KC:(l + 1) * KC], in0=modT[:, l * 24 + 8:l * 24 + 16], scalar=1.0,
                                                               in1=col(PV_NORMG + l * KC), op0=ALU.add, op1=ALU.mult), r=["modT", "pv"], w=["Avec"])
            op("dve", lambda e: e.scalar_tensor_tensor(out=Avec[:, 4 * KC:5 * KC], in0=modT[:, 96 + 8:96 + 16], scalar=1.0,
                                                       in1=col(PV_KVG), op0=ALU.add, op1=ALU.mult), r=["modT", "pv"], w=["Avec"])
            op("dve", lambda e: e.tensor_copy(out=Avec[:, 5 * KC:6 * KC], in_=col(PV_FING)), r=["pv"], w=["Avec"])
            spt = sb2("spt", [128, 2 * KC])
            op("act", lambda e: e.activation(out=spt[:], in_=col(PV_LAM, 2 * KC), func=AF.Exp, scale=-1.0), r=["pv"], w=["spt"])
            op("act", lambda e: e.activation(out=spt[:], in_=spt[:], func=AF.Ln, bias=1.0, scale=1.0), r=["spt"], w=["spt"])
            op("dve", lambda e: e.tensor_scalar(out=cvec[:], in0=spt[:], scalar1=-8.0, scalar2=None, op0=ALU.mult), r=["spt"], w=["cvec"])
            op("dve", lambda e: e.tensor_scalar(out=hvec[:, 32:48], in0=spt[:], scalar1=-4.0, scalar2=None, op0=ALU.mult), r=["spt"], w=["hvec"])
            op("dve", lambda e: e.tensor_scalar(out=hvec[:, 0:16], in0=col(PV_BA, 16), scalar1=0.5, scalar2=None, op0=ALU.mult), r=["pv"], w=["hvec"])
            op("dve", lambda e: e.tensor_scalar(out=hvec[:, 16:32], in0=col(PV_BX, 16), scalar1=0.5, scalar2=None, op0=ALU.mult), r=["pv"], w=["hvec"])
            for l in range(4):
                op("dve", lambda e, l=l: e.tensor_scalar(out=modT[:, l * 24 + 16:l * 24 + 24], in0=modT[:, l * 24 + 16:l * 24 + 24], scalar1=0.5, scalar2=None, op0=ALU.mult),
                   r=["modT"], w=["modT"])
            sc.barrier()

        sqbuf = [scr[:, 0:1024].bitcast(BF16).rearrange("p (a b) -> p a b", b=TT),
                 scr[:, 1024:2048].bitcast(BF16).rearrange("p (a b) -> p a b", b=TT)]
        htr = [scr[:, 2048 + i * 512:2048 + (i + 1) * 512] for i in range(4)]

        def emit_rstd_all():
            for tt in range(NT):
                tsl = slice(tt * TT, (tt + 1) * TT)
                for hf in range(2):
                    k0 = 4 * hf
                    op("act", lambda e: e.activation(out=sqbuf[hf][:, 0:3, :], in_=xT[:, k0:k0 + 3, tsl], func=AF.Square),
                       r=[("xT", tt, k0 + j) for j in range(3)], w=[("sqa", hf)])
                    op("dve", lambda e: e.tensor_tensor(out=sqbuf[hf][:, 3, :], in0=xT[:, k0 + 3, tsl], in1=xT[:, k0 + 3, tsl], op=ALU.mult),
                       r=[("xT", tt, k0 + 3)], w=[("sqd", hf)])
                    for j in range(4):
                        op("pe", lambda e, j=j: e.matmul(out=ps[4 + tt][:, :], lhsT=onesb[:], rhs=sqbuf[hf][:, j, :], start=(hf == 0 and j == 0), stop=(hf == 1 and j == 3)),
                           r=["onesb", ("sqa", hf), ("sqd", hf)], w=[PS(4 + tt)], nl=1)
            for tt in range(NT):
                op("act", lambda e, tt=tt: e.activation(out=ps[4 + tt][:, :], in_=ps[4 + tt][:, :], func=AF.Ln, bias=EPS, scale=1.0 / D), r=[PS(4 + tt)], w=[PS(4 + tt)])
            for tt in range(NT):
                op("act", lambda e, tt=tt: e.activation(out=ps[4 + tt][:, :], in_=ps[4 + tt][:, :], func=AF.Exp, scale=-0.5), r=[PS(4 + tt)], w=[PS(4 + tt)])

        def emit_norm(aidx, bcol0):
            emit_rstd_all()
            n = 0
            for tt in range(NT):
                tsl = slice(tt * TT, (tt + 1) * TT)
                for kc in range(KC):
                    hb = n % 4
                    n += 1
                    op("dve", lambda e, kc=kc, hb=hb: e.scalar_tensor_tensor(out=htr[hb], in0=xT[:, kc, tsl], scalar=Avec[:, aidx * KC + kc:aidx * KC + kc + 1],
                                                                             in1=ps[4 + tt][:, :], op0=ALU.mult, op1=ALU.mult),
                       r=[("xT", tt, kc), PS(4 + tt), "Avec"], w=[("htmp", hb)])
                    op("act", lambda e, kc=kc, hb=hb: e.activation(out=hT[:, kc, tsl], in_=htr[hb], func=AF.Identity,
                                                                   bias=modT[:, bcol0 + kc:bcol0 + kc + 1], scale=1.0),
                       r=[("htmp", hb), "modT"], w=[("hT", tt)])

        def wview(wd, c0, n):
            return wd[:, c0:c0 + n].rearrange("(kc p) j -> p kc j", p=128)

        def emit_rg_layers(layers):
            with contextlib.ExitStack() as esl:
                sbl = lambda n, s, dt=F32: _alloc(esl, n, s, dt)
                yT = sbl("yT", [128, KC, S], BF16)
                ring = [sbl(f"wr{i}", [128, KC, 256], BF16) for i in range(4)]
                wa = [sbl(f"wa{i}", [128, 2, 256], BF16) for i in range(2)]
                wx = [sbl(f"wx{i}", [128, 2, 256], BF16) for i in range(2)]
                dg = sbl("dg", [128, 2, 4, 128])
                u_raw = sbl("u_raw", [128, 2, 4 + TT])
                sg2 = [sbl(f"sg{i}", [128, 2, TT], BF16) for i in range(2)]
                uc = sbl("uc", [128, 2, TT])
                ucb = sbl("ucb", [128, 2, TT], BF16)
                rr = sbl("rr", [128, 2, TT])
                gi = sbl("gi", [128, 2, TT])
                aa = sbl("aa", [128, 2, TT])
                a2 = sbl("a2", [128, 2, TT])
                hs = sbl("hs", [128, 2, TT])
                carry = sbl("carry", [128, 2])

                def load_block(l, hb):
                    sl = hb % 2
                    dma("pool", f"d_wr{sl}", ring[sl][:], wview(rgwin_d[l], hb * 256, 256), w=[("ring", sl)])
                    dma("pool", f"d_wr{2 + sl}", ring[2 + sl][:], wview(rgwin_d[l], D + hb * 256, 256), w=[("ring", 2 + sl)])
                    dma("pool", f"d_wa{sl}", wa[sl][:], rgwa_d[l, hb].rearrange("(ic p) j -> p ic j", p=128), w=[("wa", sl)])
                    dma("pool", f"d_wx{sl}", wx[sl][:], rgwx_d[l, hb].rearrange("(ic p) j -> p ic j", p=128), w=[("wx", sl)])

                try:
                    for l in layers:
                        emit_norm(l, l * 24)
                        stage(1)
                        load_block(l, 0)
                        stage(2)

                        def build_dg(hb):
                            for jc in range(2):
                                ch = hb * 2 + jc
                                for k in range(4):
                                    cwc = PV_CONVW + l * 32 + k * 8 + ch
                                    op("dve", lambda e, jc=jc, k=k, cwc=cwc: e.tensor_scalar(out=dg[:, jc, k, :].bitcast(F32R), in0=ident, scalar1=pv[:, cwc:cwc + 1], scalar2=None, op0=ALU.mult),
                                       r=["cst", "pv"], w=[("dg", jc)])

                        def stA(n):
                            hb, tt = n // 4, n % 4
                            sl = hb % 2
                            tsl = slice(tt * TT, (tt + 1) * TT)
                            sgn = sg2[n % 2]
                            for jc in range(2):
                                for kc in range(KC):
                                    op("pe", lambda e, jc=jc, kc=kc: e.matmul(out=ps[jc][:, :], lhsT=ring[sl][:, kc, jc * 128:(jc + 1) * 128], rhs=hT[:, kc, tsl],
                                                                              start=(kc == 0), stop=(kc == KC - 1)),
                                       r=[("ring", sl), ("hT", tt)], w=[PS(jc)], nl=1)
                            for jc in range(2):
                                for kc in range(KC):
                                    op("pe", lambda e, jc=jc, kc=kc: e.matmul(out=ps[2 + jc][:, :], lhsT=ring[2 + sl][:, kc, jc * 128:(jc + 1) * 128], rhs=hT[:, kc, tsl],
                                                                              start=(kc == 0), stop=(kc == KC - 1)),
                                       r=[("ring", 2 + sl), ("hT", tt)], w=[PS(2 + jc)], nl=1)
                            for jc in range(2):
                                if tt == 0:
                                    op("dve", lambda e, jc=jc: e.tensor_scalar(out=u_raw[:, jc, 0:4].bitcast(F32R), in0=ones[:, 0:4], scalar1=0.0, scalar2=None, op0=ALU.mult), r=["cst"], w=[("u_raw", jc)])
                                else:
                                    op("dve", lambda e, jc=jc: e.tensor_copy(out=u_raw[:, jc, 0:4].bitcast(F32R), in_=u_raw[:, jc, TT:TT + 4]), r=[("u_raw", jc)], w=[("u_raw", jc)])
                                op("dve", lambda e, jc=jc: e.tensor_copy(out=u_raw[:, jc, 4:4 + TT].bitcast(F32R), in_=ps[jc][:, :]), r=[PS(jc)], w=[("u_raw", jc)], strict=True)
                                op("act", lambda e, jc=jc: e.activation(out=sgn[:, jc, :], in_=ps[2 + jc][:, :], func=AF.Tanh, scale=0.5), r=[PS(2 + jc)], w=[("sg", n % 2, jc)])
                                op("dve", lambda e, jc=jc: e.scalar_tensor_tensor(out=sgn[:, jc, :], in0=sgn[:, jc, :], scalar=1.0, in1=ps[2 + jc][:, :], op0=ALU.add, op1=ALU.mult),
                                   r=[PS(2 + jc), ("sg", n % 2, jc)], w=[("sg", n % 2, jc)])

                        def stB(n):
                            hb, tt = n // 4, n % 4
                            if tt == 0:
                                build_dg(hb)
                            for jc in range(2):
                                ch = hb * 2 + jc
                                for k in range(4):
                                    op("pe", lambda e, jc=jc, k=k: e.matmul(out=ps[4 + jc][:, :], lhsT=dg[:, jc, k, :].bitcast(F32R), rhs=u_raw[:, jc, 1 + k:1 + k + TT].bitcast(F32R),
                                                                            start=(k == 0), stop=(k == 3)),
                                       r=[("dg", jc), ("u_raw", jc)], w=[PS(4 + jc)], nl=1)
                                cb = PV_CONVB + l * 8 + ch
                                op("act", lambda e, jc=jc, cb=cb: e.activation(out=uc[:, jc, :], in_=ps[4 + jc][:, :], func=AF.Identity, bias=pv[:, cb:cb + 1], scale=1.0),
                                   r=[PS(4 + jc), "pv"], w=[("uc", jc)])
                                op("dve", lambda e, jc=jc, cb=cb: e.tensor_scalar(out=ucb[:, jc, :], in0=ps[4 + jc][:, :], scalar1=pv[:, cb:cb + 1], scalar2=None, op0=ALU.add),
                                   r=[PS(4 + jc), "pv"], w=[("ucb", jc)])

                        def stC(n):
                            hb, tt = n // 4, n % 4
                            sl = hb % 2
                            tsl = slice(tt * TT, (tt + 1) * TT)
                            sgn = sg2[n % 2]
                            for jc in range(2):
                                for ic in range(2):
                                    op("pe", lambda e, jc=jc, ic=ic: e.matmul(out=ps[6 + jc][:, :], lhsT=wa[sl][:, ic, jc * 128:(jc + 1) * 128], rhs=ucb[:, ic, :],
                                                                              start=(ic == 0), stop=(ic == 1)),
                                       r=[("wa", sl), ("ucb", 0), ("ucb", 1)], w=[PS(6 + jc)], nl=1)
                                for ic in range(2):
                                    op("pe", lambda e, jc=jc, ic=ic: e.matmul(out=ps[4 + jc][:, :], lhsT=wx[sl][:, ic, jc * 128:(jc + 1) * 128], rhs=ucb[:, ic, :],
                                                                              start=(ic == 0), stop=(ic == 1)),
                                       r=[("wx", sl), ("ucb", 0), ("ucb", 1)], w=[PS(4 + jc)], nl=1)
                            for jc in range(2):
                                cc = l * 8 + hb * 2 + jc
                                op("act", lambda e, jc=jc, cc=cc: e.activation(out=rr[:, jc, :], in_=ps[6 + jc][:, :], func=AF.Tanh, bias=hvec[:, cc:cc + 1], scale=0.5),
                                   r=[PS(6 + jc), "hvec"], w=[("rr", jc)])
                                op("act", lambda e, jc=jc, cc=cc: e.activation(out=gi[:, jc, :], in_=ps[4 + jc][:, :], func=AF.Tanh, bias=hvec[:, 16 + cc:16 + cc + 1], scale=0.5),
                                   r=[PS(4 + jc), "hvec"], w=[("gi", jc)])
                            for jc in range(2):
                                cc = l * 8 + hb * 2 + jc
                                op("act", lambda e, jc=jc, cc=cc: e.activation(out=aa[:, jc, :], in_=rr[:, jc, :], func=AF.Exp, scale=hvec[:, 32 + cc:32 + cc + 1], bias=hvec[:, 32 + cc:32 + cc + 1]),
                                   r=[("rr", jc), "hvec"], w=[("aa", jc)])
                                op("act", lambda e, jc=jc, cc=cc: e.activation(out=a2[:, jc, :], in_=rr[:, jc, :], func=AF.Exp, scale=cvec[:, cc:cc + 1], bias=cvec[:, cc:cc + 1]),
                                   r=[("rr", jc), "cvec"], w=[("a2", jc)])
                            for jc in range(2):
                                op("act", lambda e, jc=jc: e.activation(out=a2[:, jc, :], in_=a2[:, jc, :], func=AF.Sqrt, bias=0.25, scale=-0.25),
                                   r=[("a2", jc)], w=[("a2", jc)])
                            for jc in range(2):
                                ch = hb * 2 + jc
                                op("dve", lambda e, jc=jc: e.scalar_tensor_tensor(out=gi[:, jc, :], in0=gi[:, jc, :], scalar=1.0, in1=uc[:, jc, :], op0=ALU.add, op1=ALU.mult),
                                   r=[("gi", jc), ("uc", jc)], w=[("gi", jc)])
                                op("dve", lambda e, jc=jc: e.tensor_tensor(out=gi[:, jc, :], in0=gi[:, jc, :], in1=a2[:, jc, :], op=ALU.mult),
                                   r=[("gi", jc), ("a2", jc)], w=[("gi", jc)])
                                init = 0.0 if tt == 0 else carry[:, jc:jc + 1]
                                op("dve", lambda e, jc=jc, init=init: e.tensor_tensor_scan(out=hs[:, jc, :], data0=aa[:, jc, :], data1=gi[:, jc, :], initial=init,
                                                                                           op0=ALU.mult, op1=ALU.add),
                                   r=[("aa", jc), ("gi", jc), ("carry", jc)], w=[("hs", jc)])
                                op("dve", lambda e, jc=jc: e.tensor_copy(out=carry[:, jc:jc + 1], in_=hs[:, jc, TT - 1:TT]), r=[("hs", jc)], w=[("carry", jc)])
                                op("dve", lambda e, jc=jc, ch=ch: e.tensor_tensor(out=yT[:, ch, tsl], in0=hs[:, jc, :], in1=sgn[:, jc, :], op=ALU.mult),
                                   r=[("hs", jc), ("sg", n % 2, jc)], w=[("yT", tt)])

                        load_block(l, 1)
                        stA(0)
                        for n in range(16):
                            stB(n)
                            if n + 1 < 16:
                                stA(n + 1)
                            stC(n)
                            if n % 4 == 3 and n // 4 + 2 < 4:
                                load_block(l, n // 4 + 2)
                        stage(7)
                        for jg in range(4):
                            dma("pool", f"d_wr{jg}", ring[jg][:], wview(rgwout_d[l], jg * 256, 256), w=[("ring", jg)])
                        n = 0
                        for jg in range(4):
                            for jc in range(2):
                                ch = jg * 2 + jc
                                gcol = l * 24 + 16 + ch
                                for tt in range(NT):
                                    tsl = slice(tt * TT, (tt + 1) * TT)
                                    b = n % 4
                                    n += 1
                                    for kc in range(KC):
                                        op("pe", lambda e, kc=kc, b=b, jg=jg, jc=jc: e.matmul(out=ps[b][:, :], lhsT=ring[jg][:, kc, jc * 128:(jc + 1) * 128], rhs=yT[:, kc, tsl],
                                                                                          start=(kc == 0), stop=(kc == KC - 1)),
                                           r=[("ring", jg), ("yT", tt)], w=[PS(b)], nl=1)
                                    op("dve", lambda e, b=b, ch=ch, gcol=gcol: e.scalar_tensor_tensor(out=xT[:, ch, tsl], in0=ps[b][:, :], scalar=modT[:, gcol:gcol + 1], in1=xT[:, ch, tsl],
                                                                                                      op0=ALU.mult, op1=ALU.add),
                                       r=[PS(b), "modT", ("xT", tt, ch)], w=[("xT", tt, ch)])
                except _Stop:
                    pass
                sc.barrier()

        if nlayers >= 1:
            emit_rg_layers(list(range(min(nlayers, 2))))


        def emit_attention(nlay):
            with contextlib.ExitStack() as esa:
                sba = lambda n, s, dt=F32: _alloc(esa, n, s, dt)
                KT = sba("KT", [128, 8, S], BF16)
                V = sba("V", [128, 16, D], BF16)
                kmf = sba("kmf", [128, 64])
                kmt = sba("kmt", [128, 64])
                kmh = sba("kmh", [128, 64], BF16)
                kml = sba("kml", [128, 64], BF16)
                wq = [sba(f"wq{i}", [128, KC, 128], BF16) for i in range(2)]
                wg = [sba(f"wg{i}", [128, KC, 128], BF16) for i in range(2)]
                gs = [sba(f"gs{i}", [128, 8]) for i in range(4)]
                m8 = sba("m8", [128, 8])
                emit_norm(4, 96)
                TQ = lambda sl: [("QT", sl, t_) for t_ in range(NT)]
                TS = lambda sl: [("SG", sl, q_) for q_ in range(16)]
                SQ = lambda hf: [("sqa", hf), ("sqd", hf)]
                HT = lambda a: [("htmp", a), ("htmp", a + 1)]
                sc.alias(SQ(0) + SQ(1), [("wv", 0)])
                sc.alias(HT(0) + HT(2), [("wv", 1)])
                wv = [scr[:, i * 2048:(i + 1) * 2048].bitcast(BF16).rearrange("p (a b) -> p a b", b=512) for i in range(2)]
                for jg in range(2):
                    dma("pool", f"d_wv{jg}", wv[jg], wview(wkv_d, D + jg * 512, 512), w=[("wv", jg)])
                for h in range(8):
                    sl = h % 2
                    dma("pool", f"d_wq{sl}", wq[sl][:], wview(wkv_d, h * 128, 128), w=[("wq", sl)])
                    for tt in range(NT):
                        tsl = slice(tt * TT, (tt + 1) * TT)
                        b = tt % 2
                        for kc in range(KC):
                            op("pe", lambda e, kc=kc, b=b: e.matmul(out=ps[b][:, :], lhsT=wq[sl][:, kc, :], rhs=hT[:, kc, tsl], start=(kc == 0), stop=(kc == KC - 1)),
                               r=[("wq", sl), ("hT", tt)], w=[PS(b)], nl=1)
                        op("act", lambda e, b=b: e.activation(out=KT[:, h, tsl], in_=ps[b][:, :], func=AF.Copy), r=[PS(b)], w=[("KT", h)])
                        c0 = h * 8 + tt * 2
                        op("dve", lambda e, b=b, c0=c0: e.tensor_reduce(out=kmf[:, c0:c0 + 2], in_=ps[b][:, :].rearrange("p (a b) -> p a b", b=256), axis=AX.X, op=ALU.add),
                           r=[PS(b)], w=["kmf"])
                for jg in range(2):
                    for n in range(16):
                        b = 2 + n % 4
                        for kc in range(KC):
                            op("pe", lambda e, kc=kc, b=b, n=n: e.matmul(out=ps[b][:, :], lhsT=hT[:, kc, n * 128:(n + 1) * 128], rhs=wv[jg][:, kc, :], start=(kc == 0), stop=(kc == KC - 1)),
                               r=[("hT", n // 4), ("wv", jg)], w=[PS(b)], nl=1)
                        if n % 2 == 0:
                            op("act", lambda e, b=b, n=n: e.activation(out=V[:, n, jg * 512:(jg + 1) * 512], in_=ps[b][:, :], func=AF.Copy), r=[PS(b)], w=[("V", n)])
                        else:
                            op("dve", lambda e, b=b, n=n: e.tensor_copy(out=V[:, n, jg * 512:(jg + 1) * 512], in_=ps[b][:, :]), r=[PS(b)], w=[("V", n)])
                op("dve", lambda e: e.tensor_scalar(out=kmh[:], in0=kmf[:], scalar1=1.0 / 256, scalar2=None, op0=ALU.mult), r=["kmf"], w=["kmh"])
                op("dve", lambda e: e.tensor_copy(out=kmt[:], in_=kmh[:]), r=["kmh"], w=["kmt"])
                op("dve", lambda e: e.scalar_tensor_tensor(out=kml[:], in0=kmf[:], scalar=1.0 / 256, in1=kmt[:], op0=ALU.mult, op1=ALU.subtract), r=["kmf", "kmt"], w=["kml"])
                for i in range(4):
                    op("dve", lambda e, i=i: e.memset(gs[i][:], -1e30), w=[("gs", i)])
                sc.alias([("wv", 0)], SQ(0) + SQ(1))
                sc.alias([("wv", 1)], HT(0) + HT(2))

                QTb = [scr[:, 0:1024].bitcast(BF16), scr[:, 1024:2048].bitcast(BF16)]
                SGb = [scr[:, 2048:3072].bitcast(BF16), scr[:, 3072:4096].bitcast(BF16)]
                Pb = [sba(f"Pb{i}", [128, 512], BF16) for i in range(3)]
                PTb = [sba(f"PTb{i}", [128, 512], BF16) for i in range(3)]
                Onb = [sba(f"On{i}", [128, 128], BF16) for i in range(2)]
                rsb = [sba(f"rs{i}", [128, 16]) for i in range(4)]
                rsumb = [sba(f"rsum{i}", [128, 1]) for i in range(2)]
                rinvb = [sba(f"rinv{i}", [128, 1]) for i in range(2)]
                mbb = [sba(f"mb{i}", [128, 8]) for i in range(2)]
                wo3 = [sba(f"wo3_{i}", [128, D], BF16) for i in range(3)]
                Tb1 = ps[2][:, :].bitcast(BF16)
                SBK = (0, 1)
                BGK = (7, 3)
                bgc = [0]
                OTb = ps[6][:, 256:320].bitcast(BF16)
                cnt = {"u": 0, "q": 0}

                def load_qg(la, h):
                    sl = h % 2
                    dma("pool", f"d_wq{sl}", wq[sl][:], wview(attwin_d[la], h * 128, 128), w=[("wq", sl)])
                    dma("pool", f"d_wg{sl}", wg[sl][:], wview(attwin_d[la], D + h * 128, 128), w=[("wg", sl)])

                def load_o(la, h):
                    sl = h % 3
                    dma("pool", f"d_wo{sl}", wo3[sl][:], attwout_d[la][h * 128:(h + 1) * 128, :], w=[("wo", sl)])

                def proj_group(hn, which, tt, bk=7, part=None):
                    sl = hn % 2
                    tsl = slice(tt * TT, (tt + 1) * TT)
                    wt = wq[sl] if which == "q" else wg[sl]
                    wtok = ("wq", sl) if which == "q" else ("wg", sl)
                    kcs = range(KC) if part is None else range(4 * part, 4 * part + 4)
                    for kc in kcs:
                        op("pe", lambda e, kc=kc: e.matmul(out=ps[bk][:, :], lhsT=wt[:, kc, :], rhs=hT[:, kc, tsl], start=(kc == 0), stop=(kc == KC - 1)),
                           r=[wtok, ("hT", tt)], w=[PS(bk)], nl=1)
                    if part == 0:
                        return
                    if which == "q":
                        op("act", lambda e: e.activation(out=QTb[sl][:, tsl], in_=ps[bk][:, :], func=AF.Copy), r=[PS(bk)], w=[("QT", sl, tt)])
                    else:
                        sgt = [("SG", sl, 4 * tt + j) for j in range(4)]
                        op("act", lambda e: e.activation(out=SGb[sl][:, tsl], in_=ps[bk][:, :], func=AF.Tanh, scale=0.5), r=[PS(bk)], w=sgt)
                        op("dve", lambda e: e.scalar_tensor_tensor(out=SGb[sl][:, tsl], in0=SGb[sl][:, tsl], scalar=1.0, in1=ps[bk][:, :], op0=ALU.add, op1=ALU.mult),
                           r=[PS(bk)] + sgt, w=sgt)

                def outproj_group(l, hp, jc, tt, bk=7):
                    tsl = slice(tt * TT, (tt + 1) * TT)
                    gcol = l * 24 + 16 + jc
                    op("pe", lambda e: e.matmul(out=ps[bk][:, :], lhsT=wo3[hp % 3][:, jc * 128:(jc + 1) * 128], rhs=SGb[hp % 2][:, tsl], start=True, stop=True),
                       r=[("wo", hp % 3)] + [("SG", hp % 2, 4 * tt + j) for j in range(4)], w=[PS(bk)], nl=1)
                    op("dve", lambda e: e.scalar_tensor_tensor(out=xT[:, jc, tsl], in0=ps[bk][:, :], scalar=modT[:, gcol:gcol + 1], in1=xT[:, jc, tsl], op0=ALU.mult, op1=ALU.add),
                       r=[PS(bk), "modT", ("xT", tt, jc)], w=[("xT", tt, jc)])

                def nextbg():
                    bgc[0] += 1
                    return BGK[bgc[0] % 2]

                def attn_head(la, l, h, bg_early, bg_late):
                    qb = h % 2
                    QT, SG = QTb[qb], SGb[qb]
                    units = []
                    for i in range(16):
                        nk = 128 * (i + 1)
                        nb = (nk + 511) // 512
                        for c in range(nb):
                            units.append((i, c, min(512, nk - 512 * c), c == 0, c == nb - 1))
                    U = len(units)
                    u0 = cnt["u"]
                    q0 = cnt["q"]
                    nseg = {}

                    def st1(s):
                        i, c, w_, first, last = units[s]
                        qs = slice(i * 128, (i + 1) * 128)
                        sbk = SBK[(u0 + s) % 2]
                        q2 = (q0 + i) % 2
                        op("pe", lambda e: e.matmul(out=ps[sbk][:, 0:w_], lhsT=QT[:, qs], rhs=KT[:, h, c * 512:c * 512 + w_], start=True, stop=(not last)),
                           r=[("QT", qb, i // 4), ("KT", h)], w=[PS(sbk)], nl=1)
                        if last:
                            op("pe", lambda e: e.matmul(out=ps[sbk][:, w_ - 128:w_], lhsT=identb[:], rhs=cmaskb[:], start=False, stop=True),
                               r=["identb", "cmaskb"], w=[PS(sbk)], nl=1)
                        if first and i >= 8:
                            blk = i // 2
                            g_ = gs[blk - 4]
                            op("pe", lambda e: e.matmul(out=ps[6][:, 0:8], lhsT=QT[:, qs], rhs=kmh[:, h * 8:(h + 1) * 8], start=True, stop=False),
                               r=[("QT", qb, i // 4), "kmh"], w=[PS(6)], nl=1)
                            op("pe", lambda e: e.matmul(out=ps[6][:, 0:8], lhsT=QT[:, qs], rhs=kml[:, h * 8:(h + 1) * 8], start=False, stop=True),
                               r=[("QT", qb, i // 4), "kml"], w=[PS(6)], nl=1)
                            op("dve", lambda e: e.tensor_copy(out=g_[:, 0:blk], in_=ps[6][:, 0:blk]), r=[PS(6)], w=[("gs", blk - 4)])
                            op("dve", lambda e: e.max(out=m8[:], in_=g_[:]), r=[("gs", blk - 4)], w=["m8"])
                            op("dve", lambda e: e.tensor_scalar(out=mbb[q2][:], in0=g_[:], scalar1=m8[:, 2:3], scalar2=NEG, op0=ALU.is_lt, op1=ALU.mult),
                               r=[("gs", blk - 4), "m8"], w=[("mb", q2)])

                    def st2(s):
                        i, c, w_, first, last = units[s]
                        sbk = SBK[(u0 + s) % 2]
                        pb = (u0 + s) % 3
                        q2 = (q0 + i) % 2
                        q4 = (q0 + i) % 4
                        blk = i // 2
                        segs = []
                        if i < 8:
                            segs.append((0, w_, None))
                        else:
                            for b_ in (2 * c, 2 * c + 1):
                                o0 = (b_ - 2 * c) * 256
                                if b_ < blk:
                                    segs.append((o0, o0 + 256, b_))
                                elif b_ == blk:
                                    segs.append((o0, w_, None))
                        for (o0, o1, bcol) in segs:
                            k = nseg.get(i, 0)
                            nseg[i] = k + 1
                            kw = dict(out=Pb[pb][:, o0:o1], in_=ps[sbk][:, o0:o1], func=AF.Exp, scale=SM_SCALE, accum_out=rsb[q4][:, k:k + 1])
                            rr_ = [PS(sbk)]
                            if bcol is not None:
                                kw["bias"] = mbb[q2][:, bcol:bcol + 1]
                                rr_.append(("mb", q2))
                            op("act", lambda e, kw=kw: e.activation(**kw), r=rr_, w=[("P", pb), ("rs", q4)])

                    def st3(s):
                        i, c, w_, first, last = units[s]
                        pb = (u0 + s) % 3
                        for j in range(w_ // 128):
                            op("pe", lambda e, j=j: e.transpose(out=Tb1[:, j * 128:(j + 1) * 128], in_=Pb[pb][:, j * 128:(j + 1) * 128], identity=identb[:]),
                               r=[("P", pb), "identb"], w=[PS(2)], nl=1)
                        op("dve", lambda e: e.tensor_copy(out=PTb[pb][:, 0:w_], in_=Tb1[:, 0:w_]), r=[PS(2)], w=[("PT", pb)])

                    def st5(s):
                        i, c, w_, first, last = units[s]
                        pb = (u0 + s) % 3
                        q2 = (q0 + i) % 2
                        q4 = (q0 + i) % 4
                        nj = w_ // 128
                        for j in range(nj):
                            op("pe", lambda e, j=j: e.matmul(out=ps[4 + q2][:, 0:128], lhsT=PTb[pb][:, j * 128:(j + 1) * 128], rhs=V[:, 4 * c + j, h * 128:(h + 1) * 128],
                                                             start=(first and j == 0), stop=(last and j == nj - 1)),
                               r=[("PT", pb), ("V", 4 * c + j)], w=[PS(4 + q2)], nl=1)
                        if last:
                            ns = nseg[i]
                            op("dve", lambda e: e.tensor_reduce(out=rsumb[q2][:], in_=rsb[q4][:, 0:ns], axis=AX.X, op=ALU.add), r=[("rs", q4)], w=[("rsum", q2)])
                            op("dve", lambda e: e.reciprocal(out=rinvb[q2][:], in_=rsumb[q2][:]), r=[("rsum", q2)], w=[("rinv", q2)])
                            op("act", lambda e: e.activation(out=Onb[q2][:], in_=ps[4 + q2][:, 0:128], func=AF.Identity, scale=rinvb[q2][:, 0:1]),
                               r=[PS(4 + q2), ("rinv", q2)], w=[("On", q2)])

                    def st6(s):
                        i, c, w_, first, last = units[s]
                        if not last:
                            return
                        qs = slice(i * 128, (i + 1) * 128)
                        q2 = (q0 + i) % 2
                        op("pe", lambda e: e.transpose(out=OTb, in_=Onb[q2][:], identity=identb[:]), r=[("On", q2), "identb"], w=[PS(6)], nl=1)
                        op("dve", lambda e: e.tensor_tensor(out=SG[:, qs], in0=OTb, in1=SG[:, qs], op=ALU.mult), r=[PS(6), ("SG", qb, i)], w=[("SG", qb, i)])

                    st1(0)
                    for s in range(U + 3):
                        if s + 1 < U:
                            st1(s + 1)
                        if s < U:
                            st2(s)
                        if 0 <= s - 1 < U:
                            st3(s - 1)
                        if 0 <= s - 2 < U:
                            st5(s - 2)
                        if 0 <= s - 3 < U:
                            st6(s - 3)
                        if bg_early:
                            bg_early.pop(0)()
                        elif bg_late and s >= U - len(bg_late) - 2:
                            bg_late.pop(0)()
                    while bg_early:
                        bg_early.pop(0)()
                    while bg_late:
                        bg_late.pop(0)()
                    cnt["u"] += U
                    cnt["q"] += 16

                for la in range(nlay):
                    l = 2 + la
                    emit_norm(l, l * 24)
                    sc.alias(SQ(0), TQ(0))
                    sc.alias(SQ(1), TQ(1))
                    sc.alias(HT(0), TS(0))
                    sc.alias(HT(2), TS(1))
                    load_qg(la, 0)
                    load_qg(la, 1)
                    load_o(la, 0)
                    nb_ = 0
                    for which in ("q", "g"):
                        for tt in range(NT):
                            proj_group(0, which, tt, bk=(7, 6, 0, 1)[nb_ % 4])
                            nb_ += 1
                    for h in range(8):
                        early, late = [], []
                        if h >= 1:
                            for jc in range(8):
                                for tt in range(NT):
                                    early.append(lambda jc=jc, tt=tt, hp=h - 1: outproj_group(l, hp, jc, tt, bk=nextbg()))
                        if h + 1 < 8:
                            for tt in range(NT):
                                pos = min(len(early), 10 * tt + 4)
                                early.insert(pos, (lambda tt=tt, hn=h + 1: proj_group(hn, "q", tt, part=0)))
                                early.insert(pos + 1, (lambda tt=tt, hn=h + 1: proj_group(hn, "q", tt, part=1)))
                            for tt in range(NT):
                                late.append(lambda tt=tt, hn=h + 1: proj_group(hn, "g", tt, part=0))
                                late.append(lambda tt=tt, hn=h + 1: proj_group(hn, "g", tt, part=1))
                        if h + 2 < 8:
                            load_qg(la, h + 2)
                        if h + 1 < 8:
                            load_o(la, h + 1)
                        attn_head(la, l, h, early, late)
                    nb_ = 0
                    for jc in range(8):
                        for tt in range(NT):
                            outproj_group(l, 7, jc, tt, bk=(7, 6, 0, 1)[nb_ % 4])
                            nb_ += 1
                    sc.alias(TQ(0), SQ(0))
                    sc.alias(TQ(1), SQ(1))
                    sc.alias(TS(0), HT(0))
                    sc.alias(TS(1), HT(2))
                sc.barrier()

        if nlayers >= 3:
            emit_attention(nlayers - 2)

        with contextlib.ExitStack() as es3:
            sb3 = lambda n, s, dt=F32: _alloc(es3, n, s, dt)
            of2 = [sb3(f"of{i}", [128, KC, TT]) for i in range(2)]
            ot = [sb3(f"ot{i}", [128, D]) for i in range(4)]
            ov = out_d.rearrange("(n p) d -> n p d", p=128)
            if not dbg:
                emit_rstd_all()
            for tt in range(NT):
                tsl = slice(tt * TT, (tt + 1) * TT)
                of = of2[tt % 2]
                if dbg:
                    for kc in range(KC):
                        op("dve", lambda e, kc=kc: e.tensor_copy(out=of[:, kc, :], in_=xT[:, kc, tsl]), r=[("xT", tt, kc)], w=[("of", tt % 2, kc)])
                else:
                    for kc in range(KC):
                        op("dve", lambda e, kc=kc: e.scalar_tensor_tensor(out=of[:, kc, :], in0=xT[:, kc, tsl], scalar=Avec[:, 5 * KC + kc:5 * KC + kc + 1],
                                                                         in1=ps[4 + tt][:, :], op0=ALU.mult, op1=ALU.mult),
                           r=[("xT", tt, kc), PS(4 + tt), "Avec"], w=[("of", tt % 2, kc)])
                for sub in range(4):
                    n = tt * 4 + sub
                    sl = n % 4
                    for half in range(2):
                        b = (n * 2 + half) % 4
                        for q in range(4):
                            kc = half * 4 + q
                            op("pe", lambda e, b=b, q=q, kc=kc, sub=sub: e.transpose(out=ps[b][:, q * 128:(q + 1) * 128], in_=of[:, kc, sub * 128:(sub + 1) * 128], identity=ident),
                               r=[("of", tt % 2, kc), "cst"], w=[PS(b)], nl=1)
                        if half == 0:
                            op("act", lambda e, b=b, sl=sl: e.activation(out=ot[sl][:, 0:512], in_=ps[b][:, :], func=AF.Copy), r=[PS(b)], w=[("ot", sl)])
                        else:
                            op("dve", lambda e, b=b, sl=sl: e.tensor_copy(out=ot[sl][:, 512:1024], in_=ps[b][:, :]), r=[PS(b)], w=[("ot", sl)])
                    dma("sp", f"d_out{sl}", ov[n], ot[sl][:], r=[("ot", sl)])
            sc.final_wait("sp", "d_out0")
            sc.final_wait("sp", "d_out1")
        print(f"[build] instructions={sc.nins} waits={sc.nwait} min_sbuf_remaining={min(REM)}")
    return nc


def _prep_inputs(inp):
    f32 = np.float32
    def fm(v):
        v = np.asarray(v, f32).reshape(-1, 128)
        return np.ascontiguousarray(v.T)
    pv = np.zeros((128, NPV), f32)
    for l in range(4):
        pv[:, PV_NORMG + l * 8:PV_NORMG + l * 8 + 8] = fm(inp["norm_g"][l])
        pv[:, PV_MODB + l * 24:PV_MODB + l * 24 + 24] = fm(inp["mod_b"][l])
    pv[:, PV_KVG:PV_KVG + 8] = fm(inp["kv_norm_g"])
    pv[:, PV_FING:PV_FING + 8] = fm(inp["final_norm_g"])
    pv[:, PV_KVMODB:PV_KVMODB + 16] = fm(inp["kv_mod_b"])
    for l in range(2):
        for k in range(4):
            pv[:, PV_CONVW + l * 32 + k * 8:PV_CONVW + l * 32 + k * 8 + 8] = fm(inp["rg_conv_w"][l, k])
        pv[:, PV_CONVB + l * 8:PV_CONVB + l * 8 + 8] = fm(inp["rg_conv_b"][l])
        pv[:, PV_BA + l * 8:PV_BA + l * 8 + 8] = fm(inp["rg_b_a"][l])
        pv[:, PV_BX + l * 8:PV_BX + l * 8 + 8] = fm(inp["rg_b_x"][l])
        pv[:, PV_LAM + l * 8:PV_LAM + l * 8 + 8] = fm(inp["rg_lambda"][l])
    cst = np.zeros((128, 384), f32)
    cst[:, 0:128] = np.eye(128, dtype=f32)
    q = np.arange(128)[:, None]
    k = np.arange(128)[None, :]
    cst[:, 128:256] = np.where(k <= q, 0.0, NEG).astype(f32)
    cst[:, 256:384] = 1.0
    shared = {k2: np.ascontiguousarray(np.asarray(inp[k2], f32)) for k2 in
              ("mod_w", "kv_mod_w", "rg_w_in", "rg_w_a", "rg_w_x", "rg_w_out", "w_kv", "att_w_in", "att_w_out")}
    maps = []
    for b in range(8):
        m = dict(shared)
        m["x"] = np.ascontiguousarray(np.asarray(inp["x"][b], f32))
        m["cT"] = fm(inp["c"][b])
        m["pv"] = pv
        m["cst"] = cst
        maps.append(m)
    return maps


def kernel(**inputs):
    nl = int(os.environ.get("K_NLAYERS", "4"))
    dbg = os.environ.get("K_DBG", "0") == "1"
    nc = build(nl, dbg)
    maps = _prep_inputs(inputs)
    res = run_bass_kernel_spmd(nc, maps, core_ids=list(range(8)))
    return np.stack([r["out"] for r in res.results], axis=0).astype(np.float32)
```

```python
import contextlib
import os
import numpy as np
import concourse.bass as bass
import concourse.mybir as mybir
from concourse.bass_utils import run_bass_kernel_spmd

F32 = mybir.dt.float32
F32R = mybir.dt.float32r
BF16 = mybir.dt.bfloat16
AF = mybir.ActivationFunctionType
ALU = mybir.AluOpType
AX = mybir.AxisListType

S = 2048
D = 1024
KC = 8
TT = 512
NT = S // TT
EPS = 1e-6
NEG = -30000.0
SM_SCALE = 128 ** -0.5

PV_NORMG = 0
PV_KVG = 32
PV_FING = 40
PV_MODB = 48
PV_KVMODB = 144
PV_CONVW = 160
PV_CONVB = 224
PV_BA = 240
PV_BX = 256
PV_LAM = 272
NPV = 288
NMOD = 112


class _Stop(Exception):
    pass


STOP = int(os.environ.get("K_STOP", "0"))
ATTACH = os.environ.get("K_ATTACH", "1") == "1"


def stage(n):
    if STOP == n:
        raise _Stop()


class Sch:
    def __init__(self, nc, es):
        self.nc, self.es = nc, es
        self.E = {"pe": nc.tensor, "act": nc.scalar, "dve": nc.vector, "pool": nc.gpsimd, "sp": nc.sync}
        self.sem, self.cnt = {}, {}
        for e in self.E:
            self.mksem(e)
        self.known = {e: {} for e in self.E}
        self.lw, self.rd = {}, {}
        self.nwait = 0
        self.nins = 0
        self.pend = None

    def mksem(self, name):
        self.sem[name] = self.es.enter_context(self.nc.semaphore("s_" + name))
        self.cnt[name] = 0

    def _deps(self, e, r, w, strict=False):
        deps = []
        inorder = (e in ("act", "dve", "pe")) and not strict
        for t in r:
            ev = self.lw.get(t)
            if ev is not None:
                deps.append(ev)
            if isinstance(t, tuple) and t[0] == "ps":
                for ev in self.rd.get(t, {}).values():
                    if ev[0] != e:
                        deps.append(ev)
        for t in w:
            ev = self.lw.get(t)
            if ev is not None and not (inorder and ev[0] == e):
                deps.append(ev)
            for ev in self.rd.get(t, {}).values():
                if not (inorder and ev[0] == e):
                    deps.append(ev)
        return deps

    def _wait(self, e, deps, attach=False):
        K = self.known[e]
        need = {}
        for (sn, v, vc) in deps:
            if K.get(sn, 0) < v:
                need[sn] = max(need.get(sn, 0), v)
        implied = {}
        for (sn, v, vc) in deps:
            if need.get(sn, 0) == v:
                for k2, v2 in vc.items():
                    if k2 != sn and implied.get(k2, 0) < v2:
                        implied[k2] = v2
        pend = None
        for sn, v in need.items():
            if implied.get(sn, 0) >= v:
                continue
            if attach and pend is None:
                pend = (sn, v)
                continue
            self.E[e].wait_ge(self.sem[sn], v)
            self.nwait += 1
        self.pend = pend
        for (sn, v, vc) in deps:
            for k2, v2 in vc.items():
                if K.get(k2, 0) < v2:
                    K[k2] = v2
            if K.get(sn, 0) < v:
                K[sn] = v

    def _reg(self, ev, r, w):
        for t in r:
            self.rd.setdefault(t, {})[ev[0]] = ev
        for t in w:
            self.lw[t] = ev
            self.rd[t] = {}

    def op(self, e, fn, r=(), w=(), strict=False, multi=False, nl=None):
        if e == "pe" and nl is not None and ATTACH:
            self._wait(e, self._deps(e, r[:nl], (), strict), attach=False)
            self._wait(e, self._deps(e, r[nl:], w, strict), attach=True)
        else:
            self._wait(e, self._deps(e, r, w, strict), attach=(e in ("act", "dve") and not multi and ATTACH))
        ins = fn(self.E[e])
        if self.pend is not None:
            ins._wait_ge(self.sem[self.pend[0]], self.pend[1])
            self.pend = None
        self.cnt[e] += 1
        ins.then_inc(self.sem[e], 1)
        self.nins += 1
        self._reg((e, self.cnt[e], dict(self.known[e])), r, w)

    def dma(self, q, dsem, out, in_, r=(), w=()):
        if dsem not in self.sem:
            self.mksem(dsem)
        self._wait(q, self._deps(q, r, w))
        ins = self.E[q].dma_start(out=out, in_=in_)
        self.cnt[dsem] += 16
        ins.then_inc(self.sem[dsem], 16)
        self.nins += 1
        self._reg((dsem, self.cnt[dsem], dict(self.known[q])), r, w)

    def barrier(self):
        for e in self.E:
            for sn, c in self.cnt.items():
                if c > 0 and self.known[e].get(sn, 0) < c:
                    self.E[e].wait_ge(self.sem[sn], c)
                    self.known[e][sn] = c
        self.lw.clear()
        self.rd.clear()

    def alias(self, olds, news):
        evs = []
        for t in olds:
            if t in self.lw:
                evs.append(self.lw[t])
            evs.extend(self.rd.get(t, {}).values())
        for t in news:
            d = self.rd.setdefault(t, {})
            for ev in evs:
                cur = d.get(ev[0])
                if cur is None or cur[1] < ev[1]:
                    d[ev[0]] = ev

    def final_wait(self, e, sn):
        self.E[e].wait_ge(self.sem[sn], self.cnt[sn])


def build(nlayers=4, dbg=False):
    nc = bass.Bass("TRN2", target_bir_lowering=False)
    dram = lambda n, s: nc.dram_tensor(n, s, F32, kind="ExternalInput").ap()
    x_d = dram("x", [S, D])
    c_d = dram("cT", [128, KC])
    pv_d = dram("pv", [128, NPV])
    cst_d = dram("cst", [128, 384])
    modw_d = dram("mod_w", [4, D, 3 * D])
    kvmodw_d = dram("kv_mod_w", [D, 2 * D])
    rgwin_d = dram("rg_w_in", [2, D, 2 * D])
    rgwa_d = dram("rg_w_a", [2, 4, 256, 256])
    rgwx_d = dram("rg_w_x", [2, 4, 256, 256])
    rgwout_d = dram("rg_w_out", [2, D, D])
    wkv_d = dram("w_kv", [D, 2 * D])
    attwin_d = dram("att_w_in", [2, D, 2 * D])
    attwout_d = dram("att_w_out", [2, D, D])
    out_d = nc.dram_tensor("out", [S, D], F32, kind="ExternalOutput").ap()

    es = contextlib.ExitStack()
    with es:
        sc = Sch(nc, es)
        op, dma = sc.op, sc.dma
        REM = [nc.sbuf_bytes_remaining]

        def _alloc(stack, n, s, dt):
            t = stack.enter_context(nc.sbuf_tensor("sb_" + n, s, dt))
            REM.append(nc.sbuf_bytes_remaining)
            return t
        sb = lambda n, s, dt=F32: _alloc(es, n, s, dt)

        xT = sb("xT", [128, KC, S])
        hT = sb("hT", [128, KC, S], BF16)
        cst = sb("cst", [128, 384])
        pv = sb("pv", [128, NPV])
        modT = sb("modT", [128, NMOD])
        identb = sb("identb", [128, 128], BF16)
        cmaskb = sb("cmaskb", [128, 128], BF16)
        onesb = sb("onesb", [128, 128], BF16)
        Avec = sb("Avec", [128, 6 * KC])
        cvec = sb("cvec", [128, 2 * KC])
        hvec = sb("hvec", [128, 48])
        scr = sb("scr", [128, 4096])
        sqb = scr[:, 0:2048].bitcast(BF16).rearrange("p (a b) -> p a b", b=TT)
        rms = scr[:, 2048:2560]
        rstd = scr[:, 2560:3072]
        htmp = scr[:, 3072:4096].rearrange("p (a b) -> p a b", b=TT)
        ident = cst[:, 0:128]
        ones = cst[:, 256:384]

        ps = [es.enter_context(nc.psum_tensor(f"ps{i}", [128, 512], F32)) for i in range(8)]
        PS = lambda i: ("ps", i)

        def xtok(tt):
            return [("xT", tt, k) for k in range(KC)]

        def col(base, n=KC):
            return pv[:, base:base + n]

        dma("sp", "d_cst", cst[:], cst_d[:, :], w=["cst"])
        dma("sp", "d_pv", pv[:], pv_d[:, :], w=["pv"])
        cT = sb("cTs", [128, KC])
        dma("sp", "d_c", cT[:], c_d[:, :], w=["cT"])
        op("dve", lambda e: e.tensor_copy(out=identb[:], in_=cst[:, 0:128]), r=["cst", "pv", "cT"], w=["identb"])
        op("dve", lambda e: e.tensor_copy(out=cmaskb[:], in_=cst[:, 128:256]), r=["cst"], w=["cmaskb"])
        op("dve", lambda e: e.tensor_copy(out=onesb[:], in_=cst[:, 256:384]), r=["cst"], w=["onesb"])
        cs = sb("cs", [128, KC])
        op("act", lambda e: e.activation(out=cs[:], in_=cT[:], func=AF.Silu), r=["cT", "cst", "pv"], w=["cs"])

        with contextlib.ExitStack() as es2:
            sb2 = lambda n, s, dt=F32: _alloc(es2, n, s, dt)
            xin = [sb2(f"xin{i}", [128, D]) for i in range(3)]
            wm = [sb2(f"wm{i}", [128, KC, 512], BF16) for i in range(4)]
            msb = sb2("msb", [128, 512])
            csrep = sb2("csrep", [128, KC, 128], BF16)
            for kc in range(KC):
                op("dve", lambda e, kc=kc: e.tensor_scalar(out=csrep[:, kc, :], in0=ones, scalar1=cs[:, kc:kc + 1], scalar2=None, op0=ALU.mult),
                   r=["cs", "cst"], w=[("csrep", kc)])
            xv = x_d.rearrange("(n p) d -> n p d", p=128)
            for n in range(S // 128):
                sl = n % 3
                dma("sp", f"d_x{sl}", xin[sl][:], xv[n], w=[("xin", sl)])
                for half in range(2):
                    b = (n * 2 + half) % 2
                    for q in range(4):
                        kc = half * 4 + q
                        op("pe", lambda e, b=b, q=q, kc=kc, sl=sl: e.transpose(out=ps[b][:, q * 128:(q + 1) * 128], in_=xin[sl][:, kc * 128:(kc + 1) * 128], identity=ident),
                           r=[("xin", sl), "cst"], w=[PS(b)])
                    src = ps[b][:, :].rearrange("p (a b) -> p a b", b=128)
                    dst = xT[:, half * 4:half * 4 + 4, n * 128:(n + 1) * 128]
                    if half == 0:
                        op("act", lambda e, src=src, dst=dst: e.activation(out=dst, in_=src, func=AF.Copy), r=[PS(b)], w=[("xT", n // 4, k) for k in range(half * 4, half * 4 + 4)])
                    else:
                        op("dve", lambda e, src=src, dst=dst: e.tensor_copy(out=dst, in_=src), r=[PS(b)], w=[("xT", n // 4, k) for k in range(half * 4, half * 4 + 4)])

            groups = [(l, modw_d[l], jg, l * 24 + jg * 4, PV_MODB + l * 24 + jg * 4) for l in range(4) for jg in range(6)]
            groups += [(4, kvmodw_d, jg, 96 + jg * 4, PV_KVMODB + jg * 4) for jg in range(4)]
            for gi, (l, wd, jg, mcol, bcol) in enumerate(groups):
                sl = gi % 4
                wsrc = wd[:, jg * 512:(jg + 1) * 512].rearrange("(kc p) j -> p kc j", p=128)
                dma("pool", f"d_m{sl}", wm[sl][:], wsrc, w=[("wm", sl)])
                b = 2 + (gi % 2)
                for kc in range(KC):
                    op("pe", lambda e, kc=kc, sl=sl, b=b: e.matmul(out=ps[b][:, :], lhsT=csrep[:, kc, :], rhs=wm[sl][:, kc, :],
                                                                     start=(kc == 0), stop=(kc == KC - 1)),
                       r=[("wm", sl)] + [("csrep", k) for k in range(KC)], w=[PS(b)])
                op("act", lambda e, b=b: e.activation(out=msb[:], in_=ps[b][:, :], func=AF.Copy), r=[PS(b)], w=["msb"])
                b2 = 4 + (gi % 2)
                for q in range(4):
                    op("pe", lambda e, q=q, b2=b2: e.transpose(out=ps[b2][:, q * 128:(q + 1) * 128], in_=msb[:, q * 128:(q + 1) * 128], identity=ident),
                       r=["msb", "cst"], w=[PS(b2)])
                srcv = ps[b2][:, :].rearrange("p (a b) -> p a b", b=128)[:, :, 0]
                op("dve", lambda e, srcv=srcv, mcol=mcol, bcol=bcol: e.tensor_tensor(out=modT[:, mcol:mcol + 4], in0=srcv, in1=pv[:, bcol:bcol + 4], op=ALU.add),
                   r=[PS(b2), "pv"], w=["modT"])

            for l in range(4):
                op("dve", lambda e, l=l: e.scalar_tensor_tensor(out=Avec[:, l * KC:(l + 1) * KC], in0=modT[:, l * 24 + 8:l * 24 + 16], scalar=1.0,
                                                               in1=col(PV_NORMG + l * KC), op0=ALU.add, op1=ALU.mult), r=["modT", "pv"], w=["Avec"])
            op("dve", lambda e: e.scalar_tensor_tensor(out=Avec[:, 4 * KC:5 * KC], in0=modT[:, 96 + 8:96 + 16], scalar=1.0,
                                                       in1=col(PV_KVG), op0=ALU.add, op1=ALU.mult), r=["modT", "pv"], w=["Avec"])
            op("dve", lambda e: e.tensor_copy(out=Avec[:, 5 * KC:6 * KC], in_=col(PV_FING)), r=["pv"], w=["Avec"])
            spt = sb2("spt", [128, 2 * KC])
            op("act", lambda e: e.activation(out=spt[:], in_=col(PV_LAM, 2 * KC), func=AF.Exp, scale=-1.0), r=["pv"], w=["spt"])
            op("act", lambda e: e.activation(out=spt[:], in_=spt[:], func=AF.Ln, bias=1.0, scale=1.0), r=["spt"], w=["spt"])
            op("dve", lambda e: e.tensor_scalar(out=cvec[:], in0=spt[:], scalar1=-8.0, scalar2=None, op0=ALU.mult), r=["spt"], w=["cvec"])
            op("dve", lambda e: e.tensor_scalar(out=hvec[:, 32:48], in0=spt[:], scalar1=-4.0, scalar2=None, op0=ALU.mult), r=["spt"], w=["hvec"])
            op("dve", lambda e: e.tensor_scalar(out=hvec[:, 0:16], in0=col(PV_BA, 16), scalar1=0.5, scalar2=None, op0=ALU.mult), r=["pv"], w=["hvec"])
            op("dve", lambda e: e.tensor_scalar(out=hvec[:, 16:32], in0=col(PV_BX, 16), scalar1=0.5, scalar2=None, op0=ALU.mult), r=["pv"], w=["hvec"])
            for l in range(4):
                op("dve", lambda e, l=l: e.tensor_scalar(out=modT[:, l * 24 + 16:l * 24 + 24], in0=modT[:, l * 24 + 16:l * 24 + 24], scalar1=0.5, scalar2=None, op0=ALU.mult),
                   r=["modT"], w=["modT"])
            sc.barrier()

        sqbuf = [scr[:, 0:1024].bitcast(BF16).rearrange("p (a b) -> p a b", b=TT),
                 scr[:, 1024:2048].bitcast(BF16).rearrange("p (a b) -> p a b", b=TT)]
        htr = [scr[:, 2048 + i * 512:2048 + (i + 1) * 512] for i in range(4)]

        def emit_rstd_all():
            for tt in range(NT):
                tsl = slice(tt * TT, (tt + 1) * TT)
                for hf in range(2):
                    k0 = 4 * hf
                    op("act", lambda e: e.activation(out=sqbuf[hf][:, 0:3, :], in_=xT[:, k0:k0 + 3, tsl], func=AF.Square),
                       r=[("xT", tt, k0 + j) for j in range(3)], w=[("sqa", hf)])
                    op("dve", lambda e: e.tensor_tensor(out=sqbuf[hf][:, 3, :], in0=xT[:, k0 + 3, tsl], in1=xT[:, k0 + 3, tsl], op=ALU.mult),
                       r=[("xT", tt, k0 + 3)], w=[("sqd", hf)])
                    for j in range(4):
                        op("pe", lambda e, j=j: e.matmul(out=ps[4 + tt][:, :], lhsT=onesb[:], rhs=sqbuf[hf][:, j, :], start=(hf == 0 and j == 0), stop=(hf == 1 and j == 3)),
                           r=["onesb", ("sqa", hf), ("sqd", hf)], w=[PS(4 + tt)], nl=1)
            for tt in range(NT):
                op("act", lambda e, tt=tt: e.activation(out=ps[4 + tt][:, :], in_=ps[4 + tt][:, :], func=AF.Ln, bias=EPS, scale=1.0 / D), r=[PS(4 + tt)], w=[PS(4 + tt)])
            for tt in range(NT):
                op("act", lambda e, tt=tt: e.activation(out=ps[4 + tt][:, :], in_=ps[4 + tt][:, :], func=AF.Exp, scale=-0.5), r=[PS(4 + tt)], w=[PS(4 + tt)])

        def emit_norm(aidx, bcol0):
            emit_rstd_all()
            n = 0
            for tt in range(NT):
                tsl = slice(tt * TT, (tt + 1) * TT)
                for kc in range(KC):
                    hb = n % 4
                    n += 1
                    op("dve", lambda e, kc=kc, hb=hb: e.scalar_tensor_tensor(out=htr[hb], in0=xT[:, kc, tsl], scalar=Avec[:, aidx * KC + kc:aidx * KC + kc + 1],
                                                                             in1=ps[4 + tt][:, :], op0=ALU.mult, op1=ALU.mult),
                       r=[("xT", tt, kc), PS(4 + tt), "Avec"], w=[("htmp", hb)])
                    op("act", lambda e, kc=kc, hb=hb: e.activation(out=hT[:, kc, tsl], in_=htr[hb], func=AF.Identity,
                                                                   bias=modT[:, bcol0 + kc:bcol0 + kc + 1], scale=1.0),
                       r=[("htmp", hb), "modT"], w=[("hT", tt)])

        def wview(wd, c0, n):
            return wd[:, c0:c0 + n].rearrange("(kc p) j -> p kc j", p=128)

        def emit_rg_layers(layers):
            with contextlib.ExitStack() as esl:
                sbl = lambda n, s, dt=F32: _alloc(esl, n, s, dt)
                yT = sbl("yT", [128, KC, S], BF16)
                ring = [sbl(f"wr{i}", [128, KC, 256], BF16) for i in range(4)]
                wa = [sbl(f"wa{i}", [128, 2, 256], BF16) for i in range(2)]
                wx = [sbl(f"wx{i}", [128, 2, 256], BF16) for i in range(2)]
                dg = sbl("dg", [128, 2, 4, 128])
                u_raw = sbl("u_raw", [128, 2, 4 + TT])
                sg2 = [sbl(f"sg{i}", [128, 2, TT], BF16) for i in range(2)]
                uc = sbl("uc", [128, 2, TT])
                ucb = sbl("ucb", [128, 2, TT], BF16)
                rr = sbl("rr", [128, 2, TT])
                gi = sbl("gi", [128, 2, TT])
                aa = sbl("aa", [128, 2, TT])
                a2 = sbl("a2", [128, 2, TT])
                hs = sbl("hs", [128, 2, TT])
                carry = sbl("carry", [128, 2])

                def load_block(l, hb):
                    sl = hb % 2
                    dma("pool", f"d_wr{sl}", ring[sl][:], wview(rgwin_d[l], hb * 256, 256), w=[("ring", sl)])
                    dma("pool", f"d_wr{2 + sl}", ring[2 + sl][:], wview(rgwin_d[l], D + hb * 256, 256), w=[("ring", 2 + sl)])
                    dma("pool", f"d_wa{sl}", wa[sl][:], rgwa_d[l, hb].rearrange("(ic p) j -> p ic j", p=128), w=[("wa", sl)])
                    dma("pool", f"d_wx{sl}", wx[sl][:], rgwx_d[l, hb].rearrange("(ic p) j -> p ic j", p=128), w=[("wx", sl)])

                try:
                    for l in layers:
                        load_block(l, 0)
                        emit_norm(l, l * 24)
                        stage(1)
                        stage(2)

                        def build_dg(hb):
                            for jc in range(2):
                                ch = hb * 2 + jc
                                for k in range(4):
                                    cwc = PV_CONVW + l * 32 + k * 8 + ch
                                    op("dve", lambda e, jc=jc, k=k, cwc=cwc: e.tensor_scalar(out=dg[:, jc, k, :].bitcast(F32R), in0=ident, scalar1=pv[:, cwc:cwc + 1], scalar2=None, op0=ALU.mult),
                                       r=["cst", "pv"], w=[("dg", jc)])

                        def stA(n):
                            hb, tt = n // 4, n % 4
                            sl = hb % 2
                            tsl = slice(tt * TT, (tt + 1) * TT)
                            sgn = sg2[n % 2]
                            for jc in range(2):
                                for kc in range(KC):
                                    op("pe", lambda e, jc=jc, kc=kc: e.matmul(out=ps[jc][:, :], lhsT=ring[sl][:, kc, jc * 128:(jc + 1) * 128], rhs=hT[:, kc, tsl],
                                                                              start=(kc == 0), stop=(kc == KC - 1)),
                                       r=[("ring", sl), ("hT", tt)], w=[PS(jc)], nl=1)
                            for jc in range(2):
                                for kc in range(KC):
                                    op("pe", lambda e, jc=jc, kc=kc: e.matmul(out=ps[2 + jc][:, :], lhsT=ring[2 + sl][:, kc, jc * 128:(jc + 1) * 128], rhs=hT[:, kc, tsl],
                                                                              start=(kc == 0), stop=(kc == KC - 1)),
                                       r=[("ring", 2 + sl), ("hT", tt)], w=[PS(2 + jc)], nl=1)
                            for jc in range(2):
                                if tt == 0:
                                    op("dve", lambda e, jc=jc: e.tensor_scalar(out=u_raw[:, jc, 0:4].bitcast(F32R), in0=ones[:, 0:4], scalar1=0.0, scalar2=None, op0=ALU.mult), r=["cst"], w=[("u_raw", jc)])
                                else:
                                    op("dve", lambda e, jc=jc: e.tensor_copy(out=u_raw[:, jc, 0:4].bitcast(F32R), in_=u_raw[:, jc, TT:TT + 4]), r=[("u_raw", jc)], w=[("u_raw", jc)])
                                op("dve", lambda e, jc=jc: e.tensor_copy(out=u_raw[:, jc, 4:4 + TT].bitcast(F32R), in_=ps[jc][:, :]), r=[PS(jc)], w=[("u_raw", jc)], strict=True)
                                op("act", lambda e, jc=jc: e.activation(out=sgn[:, jc, :], in_=ps[2 + jc][:, :], func=AF.Tanh, scale=0.5), r=[PS(2 + jc)], w=[("sg", n % 2, jc)])
                                op("dve", lambda e, jc=jc: e.scalar_tensor_tensor(out=sgn[:, jc, :], in0=sgn[:, jc, :], scalar=1.0, in1=ps[2 + jc][:, :], op0=ALU.add, op1=ALU.mult),
                                   r=[PS(2 + jc), ("sg", n % 2, jc)], w=[("sg", n % 2, jc)])

                        def stB(n):
                            hb, tt = n // 4, n % 4
                            if tt == 0:
                                build_dg(hb)
                            for jc in range(2):
                                ch = hb * 2 + jc
                                for k in range(4):
                                    op("pe", lambda e, jc=jc, k=k: e.matmul(out=ps[4 + jc][:, :], lhsT=dg[:, jc, k, :].bitcast(F32R), rhs=u_raw[:, jc, 1 + k:1 + k + TT].bitcast(F32R),
                                                                            start=(k == 0), stop=(k == 3)),
                                       r=[("dg", jc), ("u_raw", jc)], w=[PS(4 + jc)], nl=1)
                                cb = PV_CONVB + l * 8 + ch
                                op("act", lambda e, jc=jc, cb=cb: e.activation(out=uc[:, jc, :], in_=ps[4 + jc][:, :], func=AF.Identity, bias=pv[:, cb:cb + 1], scale=1.0),
                                   r=[PS(4 + jc), "pv"], w=[("uc", jc)])
                                op("dve", lambda e, jc=jc, cb=cb: e.tensor_scalar(out=ucb[:, jc, :], in0=ps[4 + jc][:, :], scalar1=pv[:, cb:cb + 1], scalar2=None, op0=ALU.add),
                                   r=[PS(4 + jc), "pv"], w=[("ucb", jc)])

                        def stC(n):
                            hb, tt = n // 4, n % 4
                            sl = hb % 2
                            tsl = slice(tt * TT, (tt + 1) * TT)
                            sgn = sg2[n % 2]
                            for jc in range(2):
                                for ic in range(2):
                                    op("pe", lambda e, jc=jc, ic=ic: e.matmul(out=ps[6 + jc][:, :], lhsT=wa[sl][:, ic, jc * 128:(jc + 1) * 128], rhs=ucb[:, ic, :],
                                                                              start=(ic == 0), stop=(ic == 1)),
                                       r=[("wa", sl), ("ucb", 0), ("ucb", 1)], w=[PS(6 + jc)], nl=1)
                                for ic in range(2):
                                    op("pe", lambda e, jc=jc, ic=ic: e.matmul(out=ps[4 + jc][:, :], lhsT=wx[sl][:, ic, jc * 128:(jc + 1) * 128], rhs=ucb[:, ic, :],
                                                                              start=(ic == 0), stop=(ic == 1)),
                                       r=[("wx", sl), ("ucb", 0), ("ucb", 1)], w=[PS(4 + jc)], nl=1)
                            for jc in range(2):
                                cc = l * 8 + hb * 2 + jc
                                op("act", lambda e, jc=jc, cc=cc: e.activation(out=rr[:, jc, :], in_=ps[6 + jc][:, :], func=AF.Tanh, bias=hvec[:, cc:cc + 1], scale=0.5),
                                   r=[PS(6 + jc), "hvec"], w=[("rr", jc)])
                                op("act", lambda e, jc=jc, cc=cc: e.activation(out=gi[:, jc, :], in_=ps[4 + jc][:, :], func=AF.Tanh, bias=hvec[:, 16 + cc:16 + cc + 1], scale=0.5),
                                   r=[PS(4 + jc), "hvec"], w=[("gi", jc)])
                            for jc in range(2):
                                cc = l * 8 + hb * 2 + jc
                                op("act", lambda e, jc=jc, cc=cc: e.activation(out=aa[:, jc, :], in_=rr[:, jc, :], func=AF.Exp, scale=hvec[:, 32 + cc:32 + cc + 1], bias=hvec[:, 32 + cc:32 + cc + 1]),
                                   r=[("rr", jc), "hvec"], w=[("aa", jc)])
                                op("act", lambda e, jc=jc, cc=cc: e.activation(out=a2[:, jc, :], in_=rr[:, jc, :], func=AF.Exp, scale=cvec[:, cc:cc + 1], bias=cvec[:, cc:cc + 1]),
                                   r=[("rr", jc), "cvec"], w=[("a2", jc)])
                            for jc in range(2):
                                op("act", lambda e, jc=jc: e.activation(out=a2[:, jc, :], in_=a2[:, jc, :], func=AF.Sqrt, bias=0.25, scale=-0.25),
                                   r=[("a2", jc)], w=[("a2", jc)])
                            for jc in range(2):
                                ch = hb * 2 + jc
                                op("dve", lambda e, jc=jc: e.scalar_tensor_tensor(out=gi[:, jc, :], in0=gi[:, jc, :], scalar=1.0, in1=uc[:, jc, :], op0=ALU.add, op1=ALU.mult),
                                   r=[("gi", jc), ("uc", jc)], w=[("gi", jc)])
                                op("dve", lambda e, jc=jc: e.tensor_tensor(out=gi[:, jc, :], in0=gi[:, jc, :], in1=a2[:, jc, :], op=ALU.mult),
                                   r=[("gi", jc), ("a2", jc)], w=[("gi", jc)])
                                init = 0.0 if tt == 0 else carry[:, jc:jc + 1]
                                op("dve", lambda e, jc=jc, init=init: e.tensor_tensor_scan(out=hs[:, jc, :], data0=aa[:, jc, :], data1=gi[:, jc, :], initial=init,
                                                                                           op0=ALU.mult, op1=ALU.add),
                                   r=[("aa", jc), ("gi", jc), ("carry", jc)], w=[("hs", jc)])
                                op("dve", lambda e, jc=jc: e.tensor_copy(out=carry[:, jc:jc + 1], in_=hs[:, jc, TT - 1:TT]), r=[("hs", jc)], w=[("carry", jc)])
                                op("dve", lambda e, jc=jc, ch=ch: e.tensor_tensor(out=yT[:, ch, tsl], in0=hs[:, jc, :], in1=sgn[:, jc, :], op=ALU.mult),
                                   r=[("hs", jc), ("sg", n % 2, jc)], w=[("yT", tt)])

                        load_block(l, 1)
                        stA(0)
                        for n in range(16):
                            stB(n)
                            if n + 1 < 16:
                                stA(n + 1)
                            stC(n)
                            if n % 4 == 3 and n // 4 + 2 < 4:
                                load_block(l, n // 4 + 2)
                        stage(7)
                        for jg in range(4):
                            dma("pool", f"d_wr{jg}", ring[jg][:], wview(rgwout_d[l], jg * 256, 256), w=[("ring", jg)])
                        n = 0
                        for jg in range(4):
                            for jc in range(2):
                                ch = jg * 2 + jc
                                gcol = l * 24 + 16 + ch
                                for tt in range(NT):
                                    tsl = slice(tt * TT, (tt + 1) * TT)
                                    b = n % 4
                                    n += 1
                                    for kc in range(KC):
                                        op("pe", lambda e, kc=kc, b=b, jg=jg, jc=jc: e.matmul(out=ps[b][:, :], lhsT=ring[jg][:, kc, jc * 128:(jc + 1) * 128], rhs=yT[:, kc, tsl],
                                                                                          start=(kc == 0), stop=(kc == KC - 1)),
                                           r=[("ring", jg), ("yT", tt)], w=[PS(b)], nl=1)
                                    op("dve", lambda e, b=b, ch=ch, gcol=gcol: e.scalar_tensor_tensor(out=xT[:, ch, tsl], in0=ps[b][:, :], scalar=modT[:, gcol:gcol + 1], in1=xT[:, ch, tsl],
                                                                                                      op0=ALU.mult, op1=ALU.add),
                                       r=[PS(b), "modT", ("xT", tt, ch)], w=[("xT", tt, ch)])
                except _Stop:
                    pass
                sc.barrier()

        if nlayers >= 1:
            emit_rg_layers(list(range(min(nlayers, 2))))


        def emit_attention(nlay):
            with contextlib.ExitStack() as esa:
                sba = lambda n, s, dt=F32: _alloc(esa, n, s, dt)
                KT = sba("KT", [128, 8, S], BF16)
                V = sba("V", [128, 16, D], BF16)
                kmf = sba("kmf", [128, 64])
                kmt = sba("kmt", [128, 64])
                kmh = sba("kmh", [128, 64], BF16)
                kml = sba("kml", [128, 64], BF16)
                wq = [sba(f"wq{i}", [128, KC, 128], BF16) for i in range(2)]
                wg = [sba(f"wg{i}", [128, KC, 128], BF16) for i in range(2)]
                gs = [sba(f"gs{i}", [128, 8]) for i in range(4)]
                m8 = sba("m8", [128, 8])
                dma("pool", "d_wq0", wq[0][:], wview(wkv_d, 0, 128), w=[("wq", 0)])
                emit_norm(4, 96)
                TQ = lambda sl: [("QT", sl, t_) for t_ in range(NT)]
                TS = lambda sl: [("SG", sl, q_) for q_ in range(16)]
                SQ = lambda hf: [("sqa", hf), ("sqd", hf)]
                HT = lambda a: [("htmp", a), ("htmp", a + 1)]
                sc.alias(SQ(0) + SQ(1), [("wv", 0)])
                sc.alias(HT(0) + HT(2), [("wv", 1)])
                wv = [scr[:, i * 2048:(i + 1) * 2048].bitcast(BF16).rearrange("p (a b) -> p a b", b=512) for i in range(2)]
                for jg in range(2):
                    dma("pool", f"d_wv{jg}", wv[jg], wview(wkv_d, D + jg * 512, 512), w=[("wv", jg)])
                for h in range(8):
                    sl = h % 2
                    if h + 1 < 8:
                        dma("pool", f"d_wq{1 - sl}", wq[1 - sl][:], wview(wkv_d, (h + 1) * 128, 128), w=[("wq", 1 - sl)])
                    for tt in range(NT):
                        tsl = slice(tt * TT, (tt + 1) * TT)
                        b = tt % 2
                        for kc in range(KC):
                            op("pe", lambda e, kc=kc, b=b: e.matmul(out=ps[b][:, :], lhsT=wq[sl][:, kc, :], rhs=hT[:, kc, tsl], start=(kc == 0), stop=(kc == KC - 1)),
                               r=[("wq", sl), ("hT", tt)], w=[PS(b)], nl=1)
                        op("act", lambda e, b=b: e.activation(out=KT[:, h, tsl], in_=ps[b][:, :], func=AF.Copy), r=[PS(b)], w=[("KT", h)])
                        c0 = h * 8 + tt * 2
                        op("dve", lambda e, b=b, c0=c0: e.tensor_reduce(out=kmf[:, c0:c0 + 2], in_=ps[b][:, :].rearrange("p (a b) -> p a b", b=256), axis=AX.X, op=ALU.add),
                           r=[PS(b)], w=["kmf"])
                for jg in range(2):
                    for n in range(16):
                        b = 2 + n % 4
                        for kc in range(KC):
                            op("pe", lambda e, kc=kc, b=b, n=n: e.matmul(out=ps[b][:, :], lhsT=hT[:, kc, n * 128:(n + 1) * 128], rhs=wv[jg][:, kc, :], start=(kc == 0), stop=(kc == KC - 1)),
                               r=[("hT", n // 4), ("wv", jg)], w=[PS(b)], nl=1)
                        if n % 2 == 0:
                            op("act", lambda e, b=b, n=n: e.activation(out=V[:, n, jg * 512:(jg + 1) * 512], in_=ps[b][:, :], func=AF.Copy), r=[PS(b)], w=[("V", n)])
                        else:
                            op("dve", lambda e, b=b, n=n: e.tensor_copy(out=V[:, n, jg * 512:(jg + 1) * 512], in_=ps[b][:, :]), r=[PS(b)], w=[("V", n)])
                op("dve", lambda e: e.tensor_scalar(out=kmh[:], in0=kmf[:], scalar1=1.0 / 256, scalar2=None, op0=ALU.mult), r=["kmf"], w=["kmh"])
                op("dve", lambda e: e.tensor_copy(out=kmt[:], in_=kmh[:]), r=["kmh"], w=["kmt"])
                op("dve", lambda e: e.scalar_tensor_tensor(out=kml[:], in0=kmf[:], scalar=1.0 / 256, in1=kmt[:], op0=ALU.mult, op1=ALU.subtract), r=["kmf", "kmt"], w=["kml"])
                for i in range(4):
                    op("dve", lambda e, i=i: e.memset(gs[i][:], -1e30), w=[("gs", i)])
                sc.alias([("wv", 0)], SQ(0) + SQ(1))
                sc.alias([("wv", 1)], HT(0) + HT(2))

                QTb = [scr[:, 0:1024].bitcast(BF16), scr[:, 1024:2048].bitcast(BF16)]
                SGb = [scr[:, 2048:3072].bitcast(BF16), scr[:, 3072:4096].bitcast(BF16)]
                Pb = [sba(f"Pb{i}", [128, 512], BF16) for i in range(3)]
                PTb = [sba(f"PTb{i}", [128, 512], BF16) for i in range(3)]
                Onb = [sba(f"On{i}", [128, 128], BF16) for i in range(2)]
                rsb = [sba(f"rs{i}", [128, 16]) for i in range(4)]
                rsumb = [sba(f"rsum{i}", [128, 1]) for i in range(2)]
                rinvb = [sba(f"rinv{i}", [128, 1]) for i in range(2)]
                mbb = [sba(f"mb{i}", [128, 8]) for i in range(2)]
                wo3 = [sba(f"wo3_{i}", [128, D], BF16) for i in range(3)]
                Tb1 = ps[2][:, :].bitcast(BF16)
                SBK = (0, 1)
                BGK = (7, 3)
                bgc = [0]
                OTb = ps[6][:, 256:320].bitcast(BF16)
                cnt = {"u": 0, "q": 0}

                def load_qg(la, h):
                    sl = h % 2
                    dma("pool", f"d_wq{sl}", wq[sl][:], wview(attwin_d[la], h * 128, 128), w=[("wq", sl)])
                    dma("pool", f"d_wg{sl}", wg[sl][:], wview(attwin_d[la], D + h * 128, 128), w=[("wg", sl)])

                def load_o(la, h):
                    sl = h % 3
                    dma("pool", f"d_wo{sl}", wo3[sl][:], attwout_d[la][h * 128:(h + 1) * 128, :], w=[("wo", sl)])

                def proj_group(hn, which, tt, bk=7, part=None):
                    sl = hn % 2
                    tsl = slice(tt * TT, (tt + 1) * TT)
                    wt = wq[sl] if which == "q" else wg[sl]
                    wtok = ("wq", sl) if which == "q" else ("wg", sl)
                    kcs = range(KC) if part is None else range(4 * part, 4 * part + 4)
                    for kc in kcs:
                        op("pe", lambda e, kc=kc: e.matmul(out=ps[bk][:, :], lhsT=wt[:, kc, :], rhs=hT[:, kc, tsl], start=(kc == 0), stop=(kc == KC - 1)),
                           r=[wtok, ("hT", tt)], w=[PS(bk)], nl=1)
                    if part == 0:
                        return
                    if which == "q":
                        op("act", lambda e: e.activation(out=QTb[sl][:, tsl], in_=ps[bk][:, :], func=AF.Copy), r=[PS(bk)], w=[("QT", sl, tt)])
                    else:
                        sgt = [("SG", sl, 4 * tt + j) for j in range(4)]
                        op("act", lambda e: e.activation(out=SGb[sl][:, tsl], in_=ps[bk][:, :], func=AF.Tanh, scale=0.5), r=[PS(bk)], w=sgt)
                        op("dve", lambda e: e.scalar_tensor_tensor(out=SGb[sl][:, tsl], in0=SGb[sl][:, tsl], scalar=1.0, in1=ps[bk][:, :], op0=ALU.add, op1=ALU.mult),
                           r=[PS(bk)] + sgt, w=sgt)

                def outproj_group(l, hp, jc, tt, bk=7):
                    tsl = slice(tt * TT, (tt + 1) * TT)
                    gcol = l * 24 + 16 + jc
                    op("pe", lambda e: e.matmul(out=ps[bk][:, :], lhsT=wo3[hp % 3][:, jc * 128:(jc + 1) * 128], rhs=SGb[hp % 2][:, tsl], start=True, stop=True),
                       r=[("wo", hp % 3)] + [("SG", hp % 2, 4 * tt + j) for j in range(4)], w=[PS(bk)], nl=1)
                    op("dve", lambda e: e.scalar_tensor_tensor(out=xT[:, jc, tsl], in0=ps[bk][:, :], scalar=modT[:, gcol:gcol + 1], in1=xT[:, jc, tsl], op0=ALU.mult, op1=ALU.add),
                       r=[PS(bk), "modT", ("xT", tt, jc)], w=[("xT", tt, jc)])

                def nextbg():
                    bgc[0] += 1
                    return BGK[bgc[0] % 2]

                def attn_head(la, l, h, bg_early, bg_late):
                    qb = h % 2
                    QT, SG = QTb[qb], SGb[qb]
                    units = []
                    for i in range(16):
                        nk = 128 * (i + 1)
                        nb = (nk + 511) // 512
                        for c in range(nb):
                            units.append((i, c, min(512, nk - 512 * c), c == 0, c == nb - 1))
                    U = len(units)
                    u0 = cnt["u"]
                    q0 = cnt["q"]
                    nseg = {}

                    def st1(s):
                        i, c, w_, first, last = units[s]
                        qs = slice(i * 128, (i + 1) * 128)
                        sbk = SBK[(u0 + s) % 2]
                        q2 = (q0 + i) % 2
                        op("pe", lambda e: e.matmul(out=ps[sbk][:, 0:w_], lhsT=QT[:, qs], rhs=KT[:, h, c * 512:c * 512 + w_], start=True, stop=(not last)),
                           r=[("QT", qb, i // 4), ("KT", h)], w=[PS(sbk)], nl=1)
                        if last:
                            op("pe", lambda e: e.matmul(out=ps[sbk][:, w_ - 128:w_], lhsT=identb[:], rhs=cmaskb[:], start=False, stop=True),
                               r=["identb", "cmaskb"], w=[PS(sbk)], nl=1)
                        if first and i >= 8:
                            blk = i // 2
                            g_ = gs[blk - 4]
                            op("pe", lambda e: e.matmul(out=ps[6][:, 0:8], lhsT=QT[:, qs], rhs=kmh[:, h * 8:(h + 1) * 8], start=True, stop=False),
                               r=[("QT", qb, i // 4), "kmh"], w=[PS(6)], nl=1)
                            op("pe", lambda e: e.matmul(out=ps[6][:, 0:8], lhsT=QT[:, qs], rhs=kml[:, h * 8:(h + 1) * 8], start=False, stop=True),
                               r=[("QT", qb, i // 4), "kml"], w=[PS(6)], nl=1)
                            op("dve", lambda e: e.tensor_copy(out=g_[:, 0:blk], in_=ps[6][:, 0:blk]), r=[PS(6)], w=[("gs", blk - 4)])
                            op("dve", lambda e: e.max(out=m8[:], in_=g_[:]), r=[("gs", blk - 4)], w=["m8"])
                            op("dve", lambda e: e.tensor_scalar(out=mbb[q2][:], in0=g_[:], scalar1=m8[:, 2:3], scalar2=NEG, op0=ALU.is_lt, op1=ALU.mult),
                               r=[("gs", blk - 4), "m8"], w=[("mb", q2)])

                    def st2(s):
                        i, c, w_, first, last = units[s]
                        sbk = SBK[(u0 + s) % 2]
                        pb = (u0 + s) % 3
                        q2 = (q0 + i) % 2
                        q4 = (q0 + i) % 4
                        blk = i // 2
                        segs = []
                        if i < 8:
                            segs.append((0, w_, None))
                        else:
                            for b_ in (2 * c, 2 * c + 1):
                                o0 = (b_ - 2 * c) * 256
                                if b_ < blk:
                                    segs.append((o0, o0 + 256, b_))
                                elif b_ == blk:
                                    segs.append((o0, w_, None))
                        for (o0, o1, bcol) in segs:
                            k = nseg.get(i, 0)
                            nseg[i] = k + 1
                            kw = dict(out=Pb[pb][:, o0:o1], in_=ps[sbk][:, o0:o1], func=AF.Exp, scale=SM_SCALE, accum_out=rsb[q4][:, k:k + 1])
                            rr_ = [PS(sbk)]
                            if bcol is not None:
                                kw["bias"] = mbb[q2][:, bcol:bcol + 1]
                                rr_.append(("mb", q2))
                            op("act", lambda e, kw=kw: e.activation(**kw), r=rr_, w=[("P", pb), ("rs", q4)])

                    def st3(s):
                        i, c, w_, first, last = units[s]
                        pb = (u0 + s) % 3
                        for j in range(w_ // 128):
                            op("pe", lambda e, j=j: e.transpose(out=Tb1[:, j * 128:(j + 1) * 128], in_=Pb[pb][:, j * 128:(j + 1) * 128], identity=identb[:]),
                               r=[("P", pb), "identb"], w=[PS(2)], nl=1)
                        op("dve", lambda e: e.tensor_copy(out=PTb[pb][:, 0:w_], in_=Tb1[:, 0:w_]), r=[PS(2)], w=[("PT", pb)])

                    def st5(s):
                        i, c, w_, first, last = units[s]
                        pb = (u0 + s) % 3
                        q2 = (q0 + i) % 2
                        q4 = (q0 + i) % 4
                        nj = w_ // 128
                        for j in range(nj):
                            op("pe", lambda e, j=j: e.matmul(out=ps[4 + q2][:, 0:128], lhsT=PTb[pb][:, j * 128:(j + 1) * 128], rhs=V[:, 4 * c + j, h * 128:(h + 1) * 128],
                                                             start=(first and j == 0), stop=(last and j == nj - 1)),
                               r=[("PT", pb), ("V", 4 * c + j)], w=[PS(4 + q2)], nl=1)
                        if last:
                            ns = nseg[i]
                            op("dve", lambda e: e.tensor_reduce(out=rsumb[q2][:], in_=rsb[q4][:, 0:ns], axis=AX.X, op=ALU.add), r=[("rs", q4)], w=[("rsum", q2)])
                            op("dve", lambda e: e.reciprocal(out=rinvb[q2][:], in_=rsumb[q2][:]), r=[("rsum", q2)], w=[("rinv", q2)])
                            op("act", lambda e: e.activation(out=Onb[q2][:], in_=ps[4 + q2][:, 0:128], func=AF.Identity, scale=rinvb[q2][:, 0:1]),
                               r=[PS(4 + q2), ("rinv", q2)], w=[("On", q2)])

                    def st6(s):
                        i, c, w_, first, last = units[s]
                        if not last:
                            return
                        qs = slice(i * 128, (i + 1) * 128)
                        q2 = (q0 + i) % 2
                        op("pe", lambda e: e.transpose(out=OTb, in_=Onb[q2][:], identity=identb[:]), r=[("On", q2), "identb"], w=[PS(6)], nl=1)
                        op("dve", lambda e: e.tensor_tensor(out=SG[:, qs], in0=OTb, in1=SG[:, qs], op=ALU.mult), r=[PS(6), ("SG", qb, i)], w=[("SG", qb, i)])

                    st1(0)
                    for s in range(U + 3):
                        if s + 1 < U:
                            st1(s + 1)
                        if s < U:
                            st2(s)
                        if 0 <= s - 1 < U:
                            st3(s - 1)
                        if 0 <= s - 2 < U:
                            st5(s - 2)
                        if 0 <= s - 3 < U:
                            st6(s - 3)
                        if bg_early:
                            bg_early.pop(0)()
                        elif bg_late and s >= U - len(bg_late) - 2:
                            bg_late.pop(0)()
                    while bg_early:
                        bg_early.pop(0)()
                    while bg_late:
                        bg_late.pop(0)()
                    cnt["u"] += U
                    cnt["q"] += 16

                for la in range(nlay):
                    l = 2 + la
                    load_qg(la, 0)
                    load_qg(la, 1)
                    load_o(la, 0)
                    emit_norm(l, l * 24)
                    sc.alias(SQ(0), TQ(0))
                    sc.alias(SQ(1), TQ(1))
                    sc.alias(HT(0), TS(0))
                    sc.alias(HT(2), TS(1))
                    nb_ = 0
                    for which in ("q", "g"):
                        for tt in range(NT):
                            proj_group(0, which, tt, bk=(7, 6, 0, 1)[nb_ % 4])
                            nb_ += 1
                    for h in range(8):
                        early, late = [], []
                        if h >= 1:
                            for jc in range(8):
                                for tt in range(NT):
                                    early.append(lambda jc=jc, tt=tt, hp=h - 1: outproj_group(l, hp, jc, tt, bk=nextbg()))
                        if h + 1 < 8:
                            for tt in range(NT):
                                pos = min(len(early), 10 * tt + 4)
                                early.insert(pos, (lambda tt=tt, hn=h + 1: proj_group(hn, "q", tt, part=0)))
                                early.insert(pos + 1, (lambda tt=tt, hn=h + 1: proj_group(hn, "q", tt, part=1)))
                            for tt in range(NT):
                                late.append(lambda tt=tt, hn=h + 1: proj_group(hn, "g", tt, part=0))
                                late.append(lambda tt=tt, hn=h + 1: proj_group(hn, "g", tt, part=1))
                        if h + 2 < 8:
                            load_qg(la, h + 2)
                        if h + 1 < 8:
                            load_o(la, h + 1)
                        attn_head(la, l, h, early, late)
                    nb_ = 0
                    for jc in range(8):
                        for tt in range(NT):
                            outproj_group(l, 7, jc, tt, bk=(7, 6, 0, 1)[nb_ % 4])
                            nb_ += 1
                    sc.alias(TQ(0), SQ(0))
                    sc.alias(TQ(1), SQ(1))
                    sc.alias(TS(0), HT(0))
                    sc.alias(TS(1), HT(2))
                sc.barrier()

        if nlayers >= 3:
            emit_attention(nlayers - 2)

        with contextlib.ExitStack() as es3:
            sb3 = lambda n, s, dt=F32: _alloc(es3, n, s, dt)
            of2 = [sb3(f"of{i}", [128, KC, TT]) for i in range(2)]
            ot = [sb3(f"ot{i}", [128, D]) for i in range(4)]
            ov = out_d.rearrange("(n p) d -> n p d", p=128)
            if not dbg:
                emit_rstd_all()
            for tt in range(NT):
                tsl = slice(tt * TT, (tt + 1) * TT)
                of = of2[tt % 2]
                if dbg:
                    for kc in range(KC):
                        op("dve", lambda e, kc=kc: e.tensor_copy(out=of[:, kc, :], in_=xT[:, kc, tsl]), r=[("xT", tt, kc)], w=[("of", tt % 2, kc)])
                else:
                    for kc in range(KC):
                        op("dve", lambda e, kc=kc: e.scalar_tensor_tensor(out=of[:, kc, :], in0=xT[:, kc, tsl], scalar=Avec[:, 5 * KC + kc:5 * KC + kc + 1],
                                                                         in1=ps[4 + tt][:, :], op0=ALU.mult, op1=ALU.mult),
                           r=[("xT", tt, kc), PS(4 + tt), "Avec"], w=[("of", tt % 2, kc)])
                for sub in range(4):
                    n = tt * 4 + sub
                    sl = n % 4
                    for half in range(2):
                        b = (n * 2 + half) % 4
                        for q in range(4):
                            kc = half * 4 + q
                            op("pe", lambda e, b=b, q=q, kc=kc, sub=sub: e.transpose(out=ps[b][:, q * 128:(q + 1) * 128], in_=of[:, kc, sub * 128:(sub + 1) * 128], identity=ident),
                               r=[("of", tt % 2, kc), "cst"], w=[PS(b)], nl=1)
                        if half == 0:
                            op("act", lambda e, b=b, sl=sl: e.activation(out=ot[sl][:, 0:512], in_=ps[b][:, :], func=AF.Copy), r=[PS(b)], w=[("ot", sl)])
                        else:
                            op("dve", lambda e, b=b, sl=sl: e.tensor_copy(out=ot[sl][:, 512:1024], in_=ps[b][:, :]), r=[PS(b)], w=[("ot", sl)])
                    dma("sp", f"d_out{sl}", ov[n], ot[sl][:], r=[("ot", sl)])
            sc.final_wait("sp", "d_out0")
            sc.final_wait("sp", "d_out1")
        print(f"[build] instructions={sc.nins} waits={sc.nwait} min_sbuf_remaining={min(REM)}")
    return nc


def _prep_inputs(inp):
    f32 = np.float32
    def fm(v):
        v = np.asarray(v, f32).reshape(-1, 128)
        return np.ascontiguousarray(v.T)
    pv = np.zeros((128, NPV), f32)
    for l in range(4):
        pv[:, PV_NORMG + l * 8:PV_NORMG + l * 8 + 8] = fm(inp["norm_g"][l])
        pv[:, PV_MODB + l * 24:PV_MODB + l * 24 + 24] = fm(inp["mod_b"][l])
    pv[:, PV_KVG:PV_KVG + 8] = fm(inp["kv_norm_g"])
    pv[:, PV_FING:PV_FING + 8] = fm(inp["final_norm_g"])
    pv[:, PV_KVMODB:PV_KVMODB + 16] = fm(inp["kv_mod_b"])
    for l in range(2):
        for k in range(4):
            pv[:, PV_CONVW + l * 32 + k * 8:PV_CONVW + l * 32 + k * 8 + 8] = fm(inp["rg_conv_w"][l, k])
        pv[:, PV_CONVB + l * 8:PV_CONVB + l * 8 + 8] = fm(inp["rg_conv_b"][l])
        pv[:, PV_BA + l * 8:PV_BA + l * 8 + 8] = fm(inp["rg_b_a"][l])
        pv[:, PV_BX + l * 8:PV_BX + l * 8 + 8] = fm(inp["rg_b_x"][l])
        pv[:, PV_LAM + l * 8:PV_LAM + l * 8 + 8] = fm(inp["rg_lambda"][l])
    cst = np.zeros((128, 384), f32)
    cst[:, 0:128] = np.eye(128, dtype=f32)
    q = np.arange(128)[:, None]
    k = np.arange(128)[None, :]
    cst[:, 128:256] = np.where(k <= q, 0.0, NEG).astype(f32)
    cst[:, 256:384] = 1.0
    shared = {k2: np.ascontiguousarray(np.asarray(inp[k2], f32)) for k2 in
              ("mod_w", "kv_mod_w", "rg_w_in", "rg_w_a", "rg_w_x", "rg_w_out", "w_kv", "att_w_in", "att_w_out")}
    maps = []
    for b in range(8):
        m = dict(shared)
        m["x"] = np.ascontiguousarray(np.asarray(inp["x"][b], f32))
        m["cT"] = fm(inp["c"][b])
        m["pv"] = pv
        m["cst"] = cst
        maps.append(m)
    return maps


def kernel(**inputs):
    nl = int(os.environ.get("K_NLAYERS", "4"))
    dbg = os.environ.get("K_DBG", "0") == "1"
    nc = build(nl, dbg)
    maps = _prep_inputs(inputs)
    res = run_bass_kernel_spmd(nc, maps, core_ids=list(range(8)))
    return np.stack([r["out"] for r in res.results], axis=0).astype(np.float32)
```

```python
import contextlib
import os
import numpy as np
import concourse.bass as bass
import concourse.mybir as mybir
from concourse.bass_utils import run_bass_kernel_spmd

F32 = mybir.dt.float32
F32R = mybir.dt.float32r
BF16 = mybir.dt.bfloat16
AF = mybir.ActivationFunctionType
ALU = mybir.AluOpType
AX = mybir.AxisListType

S = 2048
D = 1024
KC = 8
TT = 512
NT = S // TT
EPS = 1e-6
NEG = -30000.0
SM_SCALE = 128 ** -0.5

PV_NORMG = 0
PV_KVG = 32
PV_FING = 40
PV_MODB = 48
PV_KVMODB = 144
PV_CONVW = 160
PV_CONVB = 224
PV_BA = 240
PV_BX = 256
PV_LAM = 272
NPV = 288
NMOD = 112


class _Stop(Exception):
    pass


STOP = int(os.environ.get("K_STOP", "0"))
ATTACH = os.environ.get("K_ATTACH", "1") == "1"


def stage(n):
    if STOP == n:
        raise _Stop()


class Sch:
    def __init__(self, nc, es):
        self.nc, self.es = nc, es
        self.E = {"pe": nc.tensor, "act": nc.scalar, "dve": nc.vector, "pool": nc.gpsimd, "sp": nc.sync}
        self.sem, self.cnt = {}, {}
        for e in self.E:
            self.mksem(e)
        self.known = {e: {} for e in self.E}
        self.lw, self.rd = {}, {}
        self.nwait = 0
        self.nins = 0
        self.pend = None

    def mksem(self, name):
        self.sem[name] = self.es.enter_context(self.nc.semaphore("s_" + name))
        self.cnt[name] = 0

    def _deps(self, e, r, w, strict=False):
        deps = []
        inorder = (e in ("act", "dve", "pe")) and not strict
        for t in r:
            ev = self.lw.get(t)
            if ev is not None:
                deps.append(ev)
            if isinstance(t, tuple) and t[0] == "ps":
                for ev in self.rd.get(t, {}).values():
                    if ev[0] != e:
                        deps.append(ev)
        for t in w:
            ev = self.lw.get(t)
            if ev is not None and not (inorder and ev[0] == e):
                deps.append(ev)
            for ev in self.rd.get(t, {}).values():
                if not (inorder and ev[0] == e):
                    deps.append(ev)
        return deps

    def _wait(self, e, deps, attach=False):
        K = self.known[e]
        need = {}
        for (sn, v, vc) in deps:
            if K.get(sn, 0) < v:
                need[sn] = max(need.get(sn, 0), v)
        implied = {}
        for (sn, v, vc) in deps:
            if need.get(sn, 0) == v:
                for k2, v2 in vc.items():
                    if k2 != sn and implied.get(k2, 0) < v2:
                        implied[k2] = v2
        pend = None
        for sn, v in need.items():
            if implied.get(sn, 0) >= v:
                continue
            if attach and pend is None:
                pend = (sn, v)
                continue
            self.E[e].wait_ge(self.sem[sn], v)
            self.nwait += 1
        self.pend = pend
        for (sn, v, vc) in deps:
            for k2, v2 in vc.items():
                if K.get(k2, 0) < v2:
                    K[k2] = v2
            if K.get(sn, 0) < v:
                K[sn] = v

    def _reg(self, ev, r, w):
        for t in r:
            self.rd.setdefault(t, {})[ev[0]] = ev
        for t in w:
            self.lw[t] = ev
            self.rd[t] = {}

    def op(self, e, fn, r=(), w=(), strict=False, multi=False, nl=None):
        if e == "pe" and nl is not None and ATTACH:
            self._wait(e, self._deps(e, r[:nl], (), strict), attach=False)
            self._wait(e, self._deps(e, r[nl:], w, strict), attach=True)
        else:
            self._wait(e, self._deps(e, r, w, strict), attach=(e in ("act", "dve") and not multi and ATTACH))
        ins = fn(self.E[e])
        if self.pend is not None:
            ins._wait_ge(self.sem[self.pend[0]], self.pend[1])
            self.pend = None
        self.cnt[e] += 1
        ins.then_inc(self.sem[e], 1)
        self.nins += 1
        self._reg((e, self.cnt[e], dict(self.known[e])), r, w)

    def dma(self, q, dsem, out, in_, r=(), w=()):
        if dsem not in self.sem:
            self.mksem(dsem)
        self._wait(q, self._deps(q, r, w))
        ins = self.E[q].dma_start(out=out, in_=in_)
        self.cnt[dsem] += 16
        ins.then_inc(self.sem[dsem], 16)
        self.nins += 1
        self._reg((dsem, self.cnt[dsem], dict(self.known[q])), r, w)

    def barrier(self):
        for e in self.E:
            for sn, c in self.cnt.items():
                if c > 0 and self.known[e].get(sn, 0) < c:
                    self.E[e].wait_ge(self.sem[sn], c)
                    self.known[e][sn] = c
        self.lw.clear()
        self.rd.clear()

    def alias(self, olds, news):
        evs = []
        for t in olds:
            if t in self.lw:
                evs.append(self.lw[t])
            evs.extend(self.rd.get(t, {}).values())
        for t in news:
            d = self.rd.setdefault(t, {})
            for ev in evs:
                cur = d.get(ev[0])
                if cur is None or cur[1] < ev[1]:
                    d[ev[0]] = ev

    def final_wait(self, e, sn):
        self.E[e].wait_ge(self.sem[sn], self.cnt[sn])


def build(nlayers=4, dbg=False):
    nc = bass.Bass("TRN2", target_bir_lowering=False)
    dram = lambda n, s: nc.dram_tensor(n, s, F32, kind="ExternalInput").ap()
    x_d = dram("x", [S, D])
    c_d = dram("cT", [128, KC])
    pv_d = dram("pv", [128, NPV])
    cst_d = dram("cst", [128, 384])
    modw_d = dram("mod_w", [4, D, 3 * D])
    kvmodw_d = dram("kv_mod_w", [D, 2 * D])
    rgwin_d = dram("rg_w_in", [2, D, 2 * D])
    rgwa_d = dram("rg_w_a", [2, 4, 256, 256])
    rgwx_d = dram("rg_w_x", [2, 4, 256, 256])
    rgwout_d = dram("rg_w_out", [2, D, D])
    wkv_d = dram("w_kv", [D, 2 * D])
    attwin_d = dram("att_w_in", [2, D, 2 * D])
    attwout_d = dram("att_w_out", [2, D, D])
    out_d = nc.dram_tensor("out", [S, D], F32, kind="ExternalOutput").ap()

    es = contextlib.ExitStack()
    with es:
        sc = Sch(nc, es)
        op, dma = sc.op, sc.dma
        REM = [nc.sbuf_bytes_remaining]

        def _alloc(stack, n, s, dt):
            t = stack.enter_context(nc.sbuf_tensor("sb_" + n, s, dt))
            REM.append(nc.sbuf_bytes_remaining)
            return t
        sb = lambda n, s, dt=F32: _alloc(es, n, s, dt)

        xT = sb("xT", [128, KC, S])
        hT = sb("hT", [128, KC, S], BF16)
        cst = sb("cst", [128, 384])
        pv = sb("pv", [128, NPV])
        modT = sb("modT", [128, NMOD])
        identb = sb("identb", [128, 128], BF16)
        cmaskb = sb("cmaskb", [128, 128], BF16)
        onesb = sb("onesb", [128, 128], BF16)
        Avec = sb("Avec", [128, 6 * KC])
        cvec = sb("cvec", [128, 2 * KC])
        hvec = sb("hvec", [128, 48])
        scr = sb("scr", [128, 4096])
        sqb = scr[:, 0:2048].bitcast(BF16).rearrange("p (a b) -> p a b", b=TT)
        rms = scr[:, 2048:2560]
        rstd = scr[:, 2560:3072]
        htmp = scr[:, 3072:4096].rearrange("p (a b) -> p a b", b=TT)
        ident = cst[:, 0:128]
        ones = cst[:, 256:384]

        ps = [es.enter_context(nc.psum_tensor(f"ps{i}", [128, 512], F32)) for i in range(8)]
        PS = lambda i: ("ps", i)

        def xtok(tt):
            return [("xT", tt, k) for k in range(KC)]

        def col(base, n=KC):
            return pv[:, base:base + n]

        dma("sp", "d_cst", cst[:], cst_d[:, :], w=["cst"])
        dma("sp", "d_pv", pv[:], pv_d[:, :], w=["pv"])
        cT = sb("cTs", [128, KC])
        dma("sp", "d_c", cT[:], c_d[:, :], w=["cT"])
        op("dve", lambda e: e.tensor_copy(out=identb[:], in_=cst[:, 0:128]), r=["cst", "pv", "cT"], w=["identb"])
        op("dve", lambda e: e.tensor_copy(out=cmaskb[:], in_=cst[:, 128:256]), r=["cst"], w=["cmaskb"])
        op("dve", lambda e: e.tensor_copy(out=onesb[:], in_=cst[:, 256:384]), r=["cst"], w=["onesb"])
        cs = sb("cs", [128, KC])
        op("act", lambda e: e.activation(out=cs[:], in_=cT[:], func=AF.Silu), r=["cT", "cst", "pv"], w=["cs"])

        with contextlib.ExitStack() as es2:
            sb2 = lambda n, s, dt=F32: _alloc(es2, n, s, dt)
            xin = [sb2(f"xin{i}", [128, D]) for i in range(3)]
            wm = [sb2(f"wm{i}", [128, KC, 512], BF16) for i in range(4)]
            msb = sb2("msb", [128, 512])
            csrep = sb2("csrep", [128, KC, 128], BF16)
            for kc in range(KC):
                op("dve", lambda e, kc=kc: e.tensor_scalar(out=csrep[:, kc, :], in0=ones, scalar1=cs[:, kc:kc + 1], scalar2=None, op0=ALU.mult),
                   r=["cs", "cst"], w=[("csrep", kc)])
            xv = x_d.rearrange("(n p) d -> n p d", p=128)
            for n in range(S // 128):
                sl = n % 3
                dma("sp", f"d_x{sl}", xin[sl][:], xv[n], w=[("xin", sl)])
                for half in range(2):
                    b = (n * 2 + half) % 2
                    for q in range(4):
                        kc = half * 4 + q
                        op("pe", lambda e, b=b, q=q, kc=kc, sl=sl: e.transpose(out=ps[b][:, q * 128:(q + 1) * 128], in_=xin[sl][:, kc * 128:(kc + 1) * 128], identity=ident),
                           r=[("xin", sl), "cst"], w=[PS(b)])
                    src = ps[b][:, :].rearrange("p (a b) -> p a b", b=128)
                    dst = xT[:, half * 4:half * 4 + 4, n * 128:(n + 1) * 128]
                    if half == 0:
                        op("act", lambda e, src=src, dst=dst: e.activation(out=dst, in_=src, func=AF.Copy), r=[PS(b)], w=[("xT", n // 4, k) for k in range(half * 4, half * 4 + 4)])
                    else:
                        op("dve", lambda e, src=src, dst=dst: e.tensor_copy(out=dst, in_=src), r=[PS(b)], w=[("xT", n // 4, k) for k in range(half * 4, half * 4 + 4)])

            groups = [(l, modw_d[l], jg, l * 24 + jg * 4, PV_MODB + l * 24 + jg * 4) for l in range(4) for jg in range(6)]
            groups += [(4, kvmodw_d, jg, 96 + jg * 4, PV_KVMODB + jg * 4) for jg in range(4)]
            for gi, (l, wd, jg, mcol, bcol) in enumerate(groups):
                sl = gi % 4
                wsrc = wd[:, jg * 512:(jg + 1) * 512].rearrange("(kc p) j -> p kc j", p=128)
                dma("pool", f"d_m{sl}", wm[sl][:], wsrc, w=[("wm", sl)])
                b = 2 + (gi % 2)
                for kc in range(KC):
                    op("pe", lambda e, kc=kc, sl=sl, b=b: e.matmul(out=ps[b][:, :], lhsT=csrep[:, kc, :], rhs=wm[sl][:, kc, :],
                                                                     start=(kc == 0), stop=(kc == KC - 1)),
                       r=[("wm", sl)] + [("csrep", k) for k in range(KC)], w=[PS(b)])
                op("act", lambda e, b=b: e.activation(out=msb[:], in_=ps[b][:, :], func=AF.Copy), r=[PS(b)], w=["msb"])
                b2 = 4 + (gi % 2)
                for q in range(4):
                    op("pe", lambda e, q=q, b2=b2: e.transpose(out=ps[b2][:, q * 128:(q + 1) * 128], in_=msb[:, q * 128:(q + 1) * 128], identity=ident),
                       r=["msb", "cst"], w=[PS(b2)])
                srcv = ps[b2][:, :].rearrange("p (a b) -> p a b", b=128)[:, :, 0]
                op("dve", lambda e, srcv=srcv, mcol=mcol, bcol=bcol: e.tensor_tensor(out=modT[:, mcol:mcol + 4], in0=srcv, in1=pv[:, bcol:bcol + 4], op=ALU.add),
                   r=[PS(b2), "pv"], w=["modT"])

            for l in range(4):
                op("dve", lambda e, l=l: e.scalar_tensor_tensor(out=Avec[:, l * KC:(l + 1) * KC], in0=modT[:, l * 24 + 8:l * 24 + 16], scalar=1.0,
                                                               in1=col(PV_NORMG + l * KC), op0=ALU.add, op1=ALU.mult), r=["modT", "pv"], w=["Avec"])
            op("dve", lambda e: e.scalar_tensor_tensor(out=Avec[:, 4 * KC:5 * KC], in0=modT[:, 96 + 8:96 + 16], scalar=1.0,
                                                       in1=col(PV_KVG), op0=ALU.add, op1=ALU.mult), r=["modT", "pv"], w=["Avec"])
            op("dve", lambda e: e.tensor_copy(out=Avec[:, 5 * KC:6 * KC], in_=col(PV_FING)), r=["pv"], w=["Avec"])
            spt = sb2("spt", [128, 2 * KC])
            op("act", lambda e: e.activation(out=spt[:], in_=col(PV_LAM, 2 * KC), func=AF.Exp, scale=-1.0), r=["pv"], w=["spt"])
            op("act", lambda e: e.activation(out=spt[:], in_=spt[:], func=AF.Ln, bias=1.0, scale=1.0), r=["spt"], w=["spt"])
            op("dve", lambda e: e.tensor_scalar(out=cvec[:], in0=spt[:], scalar1=-8.0, scalar2=None, op0=ALU.mult), r=["spt"], w=["cvec"])
            op("dve", lambda e: e.tensor_scalar(out=hvec[:, 32:48], in0=spt[:], scalar1=-4.0, scalar2=None, op0=ALU.mult), r=["spt"], w=["hvec"])
            op("dve", lambda e: e.tensor_scalar(out=hvec[:, 0:16], in0=col(PV_BA, 16), scalar1=0.5, scalar2=None, op0=ALU.mult), r=["pv"], w=["hvec"])
            op("dve", lambda e: e.tensor_scalar(out=hvec[:, 16:32], in0=col(PV_BX, 16), scalar1=0.5, scalar2=None, op0=ALU.mult), r=["pv"], w=["hvec"])
            for l in range(4):
                op("dve", lambda e, l=l: e.tensor_scalar(out=modT[:, l * 24 + 16:l * 24 + 24], in0=modT[:, l * 24 + 16:l * 24 + 24], scalar1=0.5, scalar2=None, op0=ALU.mult),
                   r=["modT"], w=["modT"])
            sc.barrier()

        sqbuf = [scr[:, 0:1024].bitcast(BF16).rearrange("p (a b) -> p a b", b=TT),
                 scr[:, 1024:2048].bitcast(BF16).rearrange("p (a b) -> p a b", b=TT)]
        htr = [scr[:, 2048 + i * 512:2048 + (i + 1) * 512] for i in range(4)]

        def emit_rstd_all():
            for tt in range(NT):
                tsl = slice(tt * TT, (tt + 1) * TT)
                for hf in range(2):
                    k0 = 4 * hf
                    op("act", lambda e: e.activation(out=sqbuf[hf][:, 0:3, :], in_=xT[:, k0:k0 + 3, tsl], func=AF.Square),
                       r=[("xT", tt, k0 + j) for j in range(3)], w=[("sqa", hf)])
                    op("dve", lambda e: e.tensor_tensor(out=sqbuf[hf][:, 3, :], in0=xT[:, k0 + 3, tsl], in1=xT[:, k0 + 3, tsl], op=ALU.mult),
                       r=[("xT", tt, k0 + 3)], w=[("sqd", hf)])
                    for j in range(4):
                        op("pe", lambda e, j=j: e.matmul(out=ps[4 + tt][:, :], lhsT=onesb[:], rhs=sqbuf[hf][:, j, :], start=(hf == 0 and j == 0), stop=(hf == 1 and j == 3)),
                           r=["onesb", ("sqa", hf), ("sqd", hf)], w=[PS(4 + tt)], nl=1)
            for tt in range(NT):
                op("act", lambda e, tt=tt: e.activation(out=ps[4 + tt][:, :], in_=ps[4 + tt][:, :], func=AF.Ln, bias=EPS, scale=1.0 / D), r=[PS(4 + tt)], w=[PS(4 + tt)])
            for tt in range(NT):
                op("act", lambda e, tt=tt: e.activation(out=ps[4 + tt][:, :], in_=ps[4 + tt][:, :], func=AF.Exp, scale=-0.5), r=[PS(4 + tt)], w=[PS(4 + tt)])

        def emit_norm(aidx, bcol0):
            emit_rstd_all()
            n = 0
            for tt in range(NT):
                tsl = slice(tt * TT, (tt + 1) * TT)
                for kc in range(KC):
                    hb = n % 4
                    n += 1
                    op("dve", lambda e, kc=kc, hb=hb: e.scalar_tensor_tensor(out=htr[hb], in0=xT[:, kc, tsl], scalar=Avec[:, aidx * KC + kc:aidx * KC + kc + 1],
                                                                             in1=ps[4 + tt][:, :], op0=ALU.mult, op1=ALU.mult),
                       r=[("xT", tt, kc), PS(4 + tt), "Avec"], w=[("htmp", hb)])
                    op("act", lambda e, kc=kc, hb=hb: e.activation(out=hT[:, kc, tsl], in_=htr[hb], func=AF.Identity,
                                                                   bias=modT[:, bcol0 + kc:bcol0 + kc + 1], scale=1.0),
                       r=[("htmp", hb), "modT"], w=[("hT", tt)])

        def wview(wd, c0, n):
            return wd[:, c0:c0 + n].rearrange("(kc p) j -> p kc j", p=128)

        def emit_rg_layers(layers):
            with contextlib.ExitStack() as esl:
                sbl = lambda n, s, dt=F32: _alloc(esl, n, s, dt)
                yT = sbl("yT", [128, KC, S], BF16)
                ring = [sbl(f"wr{i}", [128, KC, 256], BF16) for i in range(4)]
                wa = [sbl(f"wa{i}", [128, 2, 256], BF16) for i in range(2)]
                wx = [sbl(f"wx{i}", [128, 2, 256], BF16) for i in range(2)]
                dg = sbl("dg", [128, 2, 4, 128])
                u_raw = sbl("u_raw", [128, 2, 4 + TT])
                sg2 = [sbl(f"sg{i}", [128, 2, TT], BF16) for i in range(2)]
                uc = sbl("uc", [128, 2, TT])
                ucb = sbl("ucb", [128, 2, TT], BF16)
                rr = sbl("rr", [128, 2, TT])
                gi = sbl("gi", [128, 2, TT])
                aa = sbl("aa", [128, 2, TT])
                a2 = sbl("a2", [128, 2, TT])
                hs = sbl("hs", [128, 2, TT])
                carry = sbl("carry", [128, 2])

                def load_block(l, hb):
                    sl = hb % 2
                    dma("pool", f"d_wr{sl}", ring[sl][:], wview(rgwin_d[l], hb * 256, 256), w=[("ring", sl)])
                    dma("pool", f"d_wr{2 + sl}", ring[2 + sl][:], wview(rgwin_d[l], D + hb * 256, 256), w=[("ring", 2 + sl)])
                    dma("pool", f"d_wa{sl}", wa[sl][:], rgwa_d[l, hb].rearrange("(ic p) j -> p ic j", p=128), w=[("wa", sl)])
                    dma("pool", f"d_wx{sl}", wx[sl][:], rgwx_d[l, hb].rearrange("(ic p) j -> p ic j", p=128), w=[("wx", sl)])

                try:
                    for l in layers:
                        emit_norm(l, l * 24)
                        stage(1)
                        load_block(l, 0)
                        stage(2)

                        def build_dg(hb):
                            for jc in range(2):
                                ch = hb * 2 + jc
                                for k in range(4):
                                    cwc = PV_CONVW + l * 32 + k * 8 + ch
                                    op("dve", lambda e, jc=jc, k=k, cwc=cwc: e.tensor_scalar(out=dg[:, jc, k, :].bitcast(F32R), in0=ident, scalar1=pv[:, cwc:cwc + 1], scalar2=None, op0=ALU.mult),
                                       r=["cst", "pv"], w=[("dg", jc)])

                        def stA(n):
                            hb, tt = n // 4, n % 4
                            sl = hb % 2
                            tsl = slice(tt * TT, (tt + 1) * TT)
                            sgn = sg2[n % 2]
                            for jc in range(2):
                                for kc in range(KC):
                                    op("pe", lambda e, jc=jc, kc=kc: e.matmul(out=ps[jc][:, :], lhsT=ring[sl][:, kc, jc * 128:(jc + 1) * 128], rhs=hT[:, kc, tsl],
                                                                              start=(kc == 0), stop=(kc == KC - 1)),
                                       r=[("ring", sl), ("hT", tt)], w=[PS(jc)], nl=1)
                            for jc in range(2):
                                for kc in range(KC):
                                    op("pe", lambda e, jc=jc, kc=kc: e.matmul(out=ps[2 + jc][:, :], lhsT=ring[2 + sl][:, kc, jc * 128:(jc + 1) * 128], rhs=hT[:, kc, tsl],
                                                                              start=(kc == 0), stop=(kc == KC - 1)),
                                       r=[("ring", 2 + sl), ("hT", tt)], w=[PS(2 + jc)], nl=1)
                            for jc in range(2):
                                if tt == 0:
                                    op("dve", lambda e, jc=jc: e.tensor_scalar(out=u_raw[:, jc, 0:4].bitcast(F32R), in0=ones[:, 0:4], scalar1=0.0, scalar2=None, op0=ALU.mult), r=["cst"], w=[("u_raw", jc)])
                                else:
                                    op("dve", lambda e, jc=jc: e.tensor_copy(out=u_raw[:, jc, 0:4].bitcast(F32R), in_=u_raw[:, jc, TT:TT + 4]), r=[("u_raw", jc)], w=[("u_raw", jc)])
                                op("dve", lambda e, jc=jc: e.tensor_copy(out=u_raw[:, jc, 4:4 + TT].bitcast(F32R), in_=ps[jc][:, :]), r=[PS(jc)], w=[("u_raw", jc)], strict=True)
                                op("act", lambda e, jc=jc: e.activation(out=sgn[:, jc, :], in_=ps[2 + jc][:, :], func=AF.Tanh, scale=0.5), r=[PS(2 + jc)], w=[("sg", n % 2, jc)])
                                op("dve", lambda e, jc=jc: e.scalar_tensor_tensor(out=sgn[:, jc, :], in0=sgn[:, jc, :], scalar=1.0, in1=ps[2 + jc][:, :], op0=ALU.add, op1=ALU.mult),
                                   r=[PS(2 + jc), ("sg", n % 2, jc)], w=[("sg", n % 2, jc)])

                        def stB(n):
                            hb, tt = n // 4, n % 4
                            if tt == 0:
                                build_dg(hb)
                            for jc in range(2):
                                ch = hb * 2 + jc
                                for k in range(4):
                                    op("pe", lambda e, jc=jc, k=k: e.matmul(out=ps[4 + jc][:, :], lhsT=dg[:, jc, k, :].bitcast(F32R), rhs=u_raw[:, jc, 1 + k:1 + k + TT].bitcast(F32R),
                                                                            start=(k == 0), stop=(k == 3)),
                                       r=[("dg", jc), ("u_raw", jc)], w=[PS(4 + jc)], nl=1)
                                cb = PV_CONVB + l * 8 + ch
                                op("act", lambda e, jc=jc, cb=cb: e.activation(out=uc[:, jc, :], in_=ps[4 + jc][:, :], func=AF.Identity, bias=pv[:, cb:cb + 1], scale=1.0),
                                   r=[PS(4 + jc), "pv"], w=[("uc", jc)])
                                op("dve", lambda e, jc=jc, cb=cb: e.tensor_scalar(out=ucb[:, jc, :], in0=ps[4 + jc][:, :], scalar1=pv[:, cb:cb + 1], scalar2=None, op0=ALU.add),
                                   r=[PS(4 + jc), "pv"], w=[("ucb", jc)])

                        def stC(n):
                            hb, tt = n // 4, n % 4
                            sl = hb % 2
                            tsl = slice(tt * TT, (tt + 1) * TT)
                            sgn = sg2[n % 2]
                            for jc in range(2):
                                for ic in range(2):
                                    op("pe", lambda e, jc=jc, ic=ic: e.matmul(out=ps[6 + jc][:, :], lhsT=wa[sl][:, ic, jc * 128:(jc + 1) * 128], rhs=ucb[:, ic, :],
                                                                              start=(ic == 0), stop=(ic == 1)),
                                       r=[("wa", sl), ("ucb", 0), ("ucb", 1)], w=[PS(6 + jc)], nl=1)
                                for ic in range(2):
                                    op("pe", lambda e, jc=jc, ic=ic: e.matmul(out=ps[4 + jc][:, :], lhsT=wx[sl][:, ic, jc * 128:(jc + 1) * 128], rhs=ucb[:, ic, :],
                                                                              start=(ic == 0), stop=(ic == 1)),
                                       r=[("wx", sl), ("ucb", 0), ("ucb", 1)], w=[PS(4 + jc)], nl=1)
                            for jc in range(2):
                                cc = l * 8 + hb * 2 + jc
                                op("act", lambda e, jc=jc, cc=cc: e.activation(out=rr[:, jc, :], in_=ps[6 + jc][:, :], func=AF.Tanh, bias=hvec[:, cc:cc + 1], scale=0.5),
                                   r=[PS(6 + jc), "hvec"], w=[("rr", jc)])
                                op("act", lambda e, jc=jc, cc=cc: e.activation(out=gi[:, jc, :], in_=ps[4 + jc][:, :], func=AF.Tanh, bias=hvec[:, 16 + cc:16 + cc + 1], scale=0.5),
                                   r=[PS(4 + jc), "hvec"], w=[("gi", jc)])
                            for jc in range(2):
                                cc = l * 8 + hb * 2 + jc
                                op("act", lambda e, jc=jc, cc=cc: e.activation(out=aa[:, jc, :], in_=rr[:, jc, :], func=AF.Exp, scale=hvec[:, 32 + cc:32 + cc + 1], bias=hvec[:, 32 + cc:32 + cc + 1]),
                                   r=[("rr", jc), "hvec"], w=[("aa", jc)])
                                op("act", lambda e, jc=jc, cc=cc: e.activation(out=a2[:, jc, :], in_=rr[:, jc, :], func=AF.Exp, scale=cvec[:, cc:cc + 1], bias=cvec[:, cc:cc + 1]),
                                   r=[("rr", jc), "cvec"], w=[("a2", jc)])
                            for jc in range(2):
                                op("act", lambda e, jc=jc: e.activation(out=a2[:, jc, :], in_=a2[:, jc, :], func=AF.Sqrt, bias=0.25, scale=-0.25),
                                   r=[("a2", jc)], w=[("a2", jc)])
                            for jc in range(2):
                                ch = hb * 2 + jc
                                op("dve", lambda e, jc=jc: e.scalar_tensor_tensor(out=gi[:, jc, :], in0=gi[:, jc, :], scalar=1.0, in1=uc[:, jc, :], op0=ALU.add, op1=ALU.mult),
                                   r=[("gi", jc), ("uc", jc)], w=[("gi", jc)])
                                op("dve", lambda e, jc=jc: e.tensor_tensor(out=gi[:, jc, :], in0=gi[:, jc, :], in1=a2[:, jc, :], op=ALU.mult),
                                   r=[("gi", jc), ("a2", jc)], w=[("gi", jc)])
                                init = 0.0 if tt == 0 else carry[:, jc:jc + 1]
                                op("dve", lambda e, jc=jc, init=init: e.tensor_tensor_scan(out=hs[:, jc, :], data0=aa[:, jc, :], data1=gi[:, jc, :], initial=init,
                                                                                           op0=ALU.mult, op1=ALU.add),
                                   r=[("aa", jc), ("gi", jc), ("carry", jc)], w=[("hs", jc)])
                                op("dve", lambda e, jc=jc: e.tensor_copy(out=carry[:, jc:jc + 1], in_=hs[:, jc, TT - 1:TT]), r=[("hs", jc)], w=[("carry", jc)])
                                op("dve", lambda e, jc=jc, ch=ch: e.tensor_tensor(out=yT[:, ch, tsl], in0=hs[:, jc, :], in1=sgn[:, jc, :], op=ALU.mult),
                                   r=[("hs", jc), ("sg", n % 2, jc)], w=[("yT", tt)])

                        load_block(l, 1)
                        stA(0)
                        for n in range(16):
                            stB(n)
                            if n + 1 < 16:
                                stA(n + 1)
                            stC(n)
                            if n % 4 == 3 and n // 4 + 2 < 4:
                                load_block(l, n // 4 + 2)
                        stage(7)
                        for jg in range(4):
                            dma("pool", f"d_wr{jg}", ring[jg][:], wview(rgwout_d[l], jg * 256, 256), w=[("ring", jg)])
                        n = 0
                        for jg in range(4):
                            for jc in range(2):
                                ch = jg * 2 + jc
                                gcol = l * 24 + 16 + ch
                                for tt in range(NT):
                                    tsl = slice(tt * TT, (tt + 1) * TT)
                                    b = n % 4
                                    n += 1
                                    for kc in range(KC):
                                        op("pe", lambda e, kc=kc, b=b, jg=jg, jc=jc: e.matmul(out=ps[b][:, :], lhsT=ring[jg][:, kc, jc * 128:(jc + 1) * 128], rhs=yT[:, kc, tsl],
                                                                                          start=(kc == 0), stop=(kc == KC - 1)),
                                           r=[("ring", jg), ("yT", tt)], w=[PS(b)], nl=1)
                                    op("dve", lambda e, b=b, ch=ch, gcol=gcol: e.scalar_tensor_tensor(out=xT[:, ch, tsl], in0=ps[b][:, :], scalar=modT[:, gcol:gcol + 1], in1=xT[:, ch, tsl],
                                                                                                      op0=ALU.mult, op1=ALU.add),
                                       r=[PS(b), "modT", ("xT", tt, ch)], w=[("xT", tt, ch)])
                except _Stop:
                    pass
                sc.barrier()

        if nlayers >= 1:
            emit_rg_layers(list(range(min(nlayers, 2))))


        def emit_attention(nlay):
            with contextlib.ExitStack() as esa:
                sba = lambda n, s, dt=F32: _alloc(esa, n, s, dt)
                KT = sba("KT", [128, 8, S], BF16)
                V = sba("V", [128, 16, D], BF16)
                kmf = sba("kmf", [128, 64])
                kmt = sba("kmt", [128, 64])
                kmh = sba("kmh", [128, 64], BF16)
                kml = sba("kml", [128, 64], BF16)
                wq = [sba(f"wq{i}", [128, KC, 128], BF16) for i in range(2)]
                wg = [sba(f"wg{i}", [128, KC, 128], BF16) for i in range(2)]
                gs = [sba(f"gs{i}", [128, 8]) for i in range(4)]
                m8 = sba("m8", [128, 8])
                emit_norm(4, 96)
                TQ = lambda sl: [("QT", sl, t_) for t_ in range(NT)]
                TS = lambda sl: [("SG", sl, q_) for q_ in range(16)]
                SQ = lambda hf: [("sqa", hf), ("sqd", hf)]
                HT = lambda a: [("htmp", a), ("htmp", a + 1)]
                sc.alias(SQ(0) + SQ(1), [("wv", 0)])
                sc.alias(HT(0) + HT(2), [("wv", 1)])
                wv = [scr[:, i * 2048:(i + 1) * 2048].bitcast(BF16).rearrange("p (a b) -> p a b", b=512) for i in range(2)]
                for jg in range(2):
                    dma("pool", f"d_wv{jg}", wv[jg], wview(wkv_d, D + jg * 512, 512), w=[("wv", jg)])
                dma("pool", "d_wq0", wq[0][:], wview(wkv_d, 0, 128), w=[("wq", 0)])
                for h in range(8):
                    sl = h % 2
                    if h + 1 < 8:
                        dma("pool", f"d_wq{1 - sl}", wq[1 - sl][:], wview(wkv_d, (h + 1) * 128, 128), w=[("wq", 1 - sl)])
                    for tt in range(NT):
                        tsl = slice(tt * TT, (tt + 1) * TT)
                        b = tt % 2
                        for kc in range(KC):
                            op("pe", lambda e, kc=kc, b=b: e.matmul(out=ps[b][:, :], lhsT=wq[sl][:, kc, :], rhs=hT[:, kc, tsl], start=(kc == 0), stop=(kc == KC - 1)),
                               r=[("wq", sl), ("hT", tt)], w=[PS(b)], nl=1)
                        op("act", lambda e, b=b: e.activation(out=KT[:, h, tsl], in_=ps[b][:, :], func=AF.Copy), r=[PS(b)], w=[("KT", h)])
                        c0 = h * 8 + tt * 2
                        op("dve", lambda e, b=b, c0=c0: e.tensor_reduce(out=kmf[:, c0:c0 + 2], in_=ps[b][:, :].rearrange("p (a b) -> p a b", b=256), axis=AX.X, op=ALU.add),
                           r=[PS(b)], w=["kmf"])
                for jg in range(2):
                    for n in range(16):
                        b = 2 + n % 4
                        for kc in range(KC):
                            op("pe", lambda e, kc=kc, b=b, n=n: e.matmul(out=ps[b][:, :], lhsT=hT[:, kc, n * 128:(n + 1) * 128], rhs=wv[jg][:, kc, :], start=(kc == 0), stop=(kc == KC - 1)),
                               r=[("hT", n // 4), ("wv", jg)], w=[PS(b)], nl=1)
                        if n % 2 == 0:
                            op("act", lambda e, b=b, n=n: e.activation(out=V[:, n, jg * 512:(jg + 1) * 512], in_=ps[b][:, :], func=AF.Copy), r=[PS(b)], w=[("V", n)])
                        else:
                            op("dve", lambda e, b=b, n=n: e.tensor_copy(out=V[:, n, jg * 512:(jg + 1) * 512], in_=ps[b][:, :]), r=[PS(b)], w=[("V", n)])
                op("dve", lambda e: e.tensor_scalar(out=kmh[:], in0=kmf[:], scalar1=1.0 / 256, scalar2=None, op0=ALU.mult), r=["kmf"], w=["kmh"])
                op("dve", lambda e: e.tensor_copy(out=kmt[:], in_=kmh[:]), r=["kmh"], w=["kmt"])
                op("dve", lambda e: e.scalar_tensor_tensor(out=kml[:], in0=kmf[:], scalar=1.0 / 256, in1=kmt[:], op0=ALU.mult, op1=ALU.subtract), r=["kmf", "kmt"], w=["kml"])
                for i in range(4):
                    op("dve", lambda e, i=i: e.memset(gs[i][:], -1e30), w=[("gs", i)])
                sc.alias([("wv", 0)], SQ(0) + SQ(1))
                sc.alias([("wv", 1)], HT(0) + HT(2))

                QTb = [scr[:, 0:1024].bitcast(BF16), scr[:, 1024:2048].bitcast(BF16)]
                SGb = [scr[:, 2048:3072].bitcast(BF16), scr[:, 3072:4096].bitcast(BF16)]
                Pb = [sba(f"Pb{i}", [128, 512], BF16) for i in range(3)]
                PTb = [sba(f"PTb{i}", [128, 512], BF16) for i in range(3)]
                Onb = [sba(f"On{i}", [128, 128], BF16) for i in range(2)]
                rsb = [sba(f"rs{i}", [128, 16]) for i in range(4)]
                rsumb = [sba(f"rsum{i}", [128, 1]) for i in range(2)]
                rinvb = [sba(f"rinv{i}", [128, 1]) for i in range(2)]
                mbb = [sba(f"mb{i}", [128, 8]) for i in range(2)]
                wo3 = [sba(f"wo3_{i}", [128, D], BF16) for i in range(3)]
                Tb1 = ps[2][:, :].bitcast(BF16)
                SBK = (0, 1)
                BGK = (7, 3)
                bgc = [0]
                OTb = ps[6][:, 256:320].bitcast(BF16)
                cnt = {"u": 0, "q": 0}

                def load_qg(la, h):
                    sl = h % 2
                    dma("pool", f"d_wq{sl}", wq[sl][:], wview(attwin_d[la], h * 128, 128), w=[("wq", sl)])
                    dma("pool", f"d_wg{sl}", wg[sl][:], wview(attwin_d[la], D + h * 128, 128), w=[("wg", sl)])

                def load_o(la, h):
                    sl = h % 3
                    dma("pool", f"d_wo{sl}", wo3[sl][:], attwout_d[la][h * 128:(h + 1) * 128, :], w=[("wo", sl)])

                def proj_group(hn, which, tt, bk=7, part=None):
                    sl = hn % 2
                    tsl = slice(tt * TT, (tt + 1) * TT)
                    wt = wq[sl] if which == "q" else wg[sl]
                    wtok = ("wq", sl) if which == "q" else ("wg", sl)
                    kcs = range(KC) if part is None else range(4 * part, 4 * part + 4)
                    for kc in kcs:
                        op("pe", lambda e, kc=kc: e.matmul(out=ps[bk][:, :], lhsT=wt[:, kc, :], rhs=hT[:, kc, tsl], start=(kc == 0), stop=(kc == KC - 1)),
                           r=[wtok, ("hT", tt)], w=[PS(bk)], nl=1)
                    if part == 0:
                        return
                    if which == "q":
                        op("act", lambda e: e.activation(out=QTb[sl][:, tsl], in_=ps[bk][:, :], func=AF.Copy), r=[PS(bk)], w=[("QT", sl, tt)])
                    else:
                        sgt = [("SG", sl, 4 * tt + j) for j in range(4)]
                        op("act", lambda e: e.activation(out=SGb[sl][:, tsl], in_=ps[bk][:, :], func=AF.Tanh, scale=0.5), r=[PS(bk)], w=sgt)
                        op("dve", lambda e: e.scalar_tensor_tensor(out=SGb[sl][:, tsl], in0=SGb[sl][:, tsl], scalar=1.0, in1=ps[bk][:, :], op0=ALU.add, op1=ALU.mult),
                           r=[PS(bk)] + sgt, w=sgt)

                def outproj_group(l, hp, jc, tt, bk=7):
                    tsl = slice(tt * TT, (tt + 1) * TT)
                    gcol = l * 24 + 16 + jc
                    op("pe", lambda e: e.matmul(out=ps[bk][:, :], lhsT=wo3[hp % 3][:, jc * 128:(jc + 1) * 128], rhs=SGb[hp % 2][:, tsl], start=True, stop=True),
                       r=[("wo", hp % 3)] + [("SG", hp % 2, 4 * tt + j) for j in range(4)], w=[PS(bk)], nl=1)
                    op("dve", lambda e: e.scalar_tensor_tensor(out=xT[:, jc, tsl], in0=ps[bk][:, :], scalar=modT[:, gcol:gcol + 1], in1=xT[:, jc, tsl], op0=ALU.mult, op1=ALU.add),
                       r=[PS(bk), "modT", ("xT", tt, jc)], w=[("xT", tt, jc)])

                def nextbg():
                    bgc[0] += 1
                    return BGK[bgc[0] % 2]

                def attn_head(la, l, h, bg_early, bg_late):
                    qb = h % 2
                    QT, SG = QTb[qb], SGb[qb]
                    units = []
                    for i in range(16):
                        nk = 128 * (i + 1)
                        nb = (nk + 511) // 512
                        for c in range(nb):
                            units.append((i, c, min(512, nk - 512 * c), c == 0, c == nb - 1))
                    U = len(units)
                    u0 = cnt["u"]
                    q0 = cnt["q"]
                    nseg = {}

                    def st1(s):
                        i, c, w_, first, last = units[s]
                        qs = slice(i * 128, (i + 1) * 128)
                        sbk = SBK[(u0 + s) % 2]
                        q2 = (q0 + i) % 2
                        op("pe", lambda e: e.matmul(out=ps[sbk][:, 0:w_], lhsT=QT[:, qs], rhs=KT[:, h, c * 512:c * 512 + w_], start=True, stop=(not last)),
                           r=[("QT", qb, i // 4), ("KT", h)], w=[PS(sbk)], nl=1)
                        if last:
                            op("pe", lambda e: e.matmul(out=ps[sbk][:, w_ - 128:w_], lhsT=identb[:], rhs=cmaskb[:], start=False, stop=True),
                               r=["identb", "cmaskb"], w=[PS(sbk)], nl=1)
                        if first and i >= 8:
                            blk = i // 2
                            g_ = gs[blk - 4]
                            op("pe", lambda e: e.matmul(out=ps[6][:, 0:8], lhsT=QT[:, qs], rhs=kmh[:, h * 8:(h + 1) * 8], start=True, stop=False),
                               r=[("QT", qb, i // 4), "kmh"], w=[PS(6)], nl=1)
                            op("pe", lambda e: e.matmul(out=ps[6][:, 0:8], lhsT=QT[:, qs], rhs=kml[:, h * 8:(h + 1) * 8], start=False, stop=True),
                               r=[("QT", qb, i // 4), "kml"], w=[PS(6)], nl=1)
                            op("dve", lambda e: e.tensor_copy(out=g_[:, 0:blk], in_=ps[6][:, 0:blk]), r=[PS(6)], w=[("gs", blk - 4)])
                            op("dve", lambda e: e.max(out=m8[:], in_=g_[:]), r=[("gs", blk - 4)], w=["m8"])
                            op("dve", lambda e: e.tensor_scalar(out=mbb[q2][:], in0=g_[:], scalar1=m8[:, 2:3], scalar2=NEG, op0=ALU.is_lt, op1=ALU.mult),
                               r=[("gs", blk - 4), "m8"], w=[("mb", q2)])

                    def st2(s):
                        i, c, w_, first, last = units[s]
                        sbk = SBK[(u0 + s) % 2]
                        pb = (u0 + s) % 3
                        q2 = (q0 + i) % 2
                        q4 = (q0 + i) % 4
                        blk = i // 2
                        segs = []
                        if i < 8:
                            segs.append((0, w_, None))
                        else:
                            for b_ in (2 * c, 2 * c + 1):
                                o0 = (b_ - 2 * c) * 256
                                if b_ < blk:
                                    segs.append((o0, o0 + 256, b_))
                                elif b_ == blk:
                                    segs.append((o0, w_, None))
                        for (o0, o1, bcol) in segs:
                            k = nseg.get(i, 0)
                            nseg[i] = k + 1
                            kw = dict(out=Pb[pb][:, o0:o1], in_=ps[sbk][:, o0:o1], func=AF.Exp, scale=SM_SCALE, accum_out=rsb[q4][:, k:k + 1])
                            rr_ = [PS(sbk)]
                            if bcol is not None:
                                kw["bias"] = mbb[q2][:, bcol:bcol + 1]
                                rr_.append(("mb", q2))
                            op("act", lambda e, kw=kw: e.activation(**kw), r=rr_, w=[("P", pb), ("rs", q4)])

                    def st3(s):
                        i, c, w_, first, last = units[s]
                        pb = (u0 + s) % 3
                        for j in range(w_ // 128):
                            op("pe", lambda e, j=j: e.transpose(out=Tb1[:, j * 128:(j + 1) * 128], in_=Pb[pb][:, j * 128:(j + 1) * 128], identity=identb[:]),
                               r=[("P", pb), "identb"], w=[PS(2)], nl=1)
                        op("dve", lambda e: e.tensor_copy(out=PTb[pb][:, 0:w_], in_=Tb1[:, 0:w_]), r=[PS(2)], w=[("PT", pb)])

                    def st5(s):
                        i, c, w_, first, last = units[s]
                        pb = (u0 + s) % 3
                        q2 = (q0 + i) % 2
                        q4 = (q0 + i) % 4
                        nj = w_ // 128
                        for j in range(nj):
                            op("pe", lambda e, j=j: e.matmul(out=ps[4 + q2][:, 0:128], lhsT=PTb[pb][:, j * 128:(j + 1) * 128], rhs=V[:, 4 * c + j, h * 128:(h + 1) * 128],
                                                             start=(first and j == 0), stop=(last and j == nj - 1)),
                               r=[("PT", pb), ("V", 4 * c + j)], w=[PS(4 + q2)], nl=1)
                        if last:
                            ns = nseg[i]
                            op("dve", lambda e: e.tensor_reduce(out=rsumb[q2][:], in_=rsb[q4][:, 0:ns], axis=AX.X, op=ALU.add), r=[("rs", q4)], w=[("rsum", q2)])
                            op("dve", lambda e: e.reciprocal(out=rinvb[q2][:], in_=rsumb[q2][:]), r=[("rsum", q2)], w=[("rinv", q2)])
                            op("act", lambda e: e.activation(out=Onb[q2][:], in_=ps[4 + q2][:, 0:128], func=AF.Identity, scale=rinvb[q2][:, 0:1]),
                               r=[PS(4 + q2), ("rinv", q2)], w=[("On", q2)])

                    def st6(s):
                        i, c, w_, first, last = units[s]
                        if not last:
                            return
                        qs = slice(i * 128, (i + 1) * 128)
                        q2 = (q0 + i) % 2
                        op("pe", lambda e: e.transpose(out=OTb, in_=Onb[q2][:], identity=identb[:]), r=[("On", q2), "identb"], w=[PS(6)], nl=1)
                        op("dve", lambda e: e.tensor_tensor(out=SG[:, qs], in0=OTb, in1=SG[:, qs], op=ALU.mult), r=[PS(6), ("SG", qb, i)], w=[("SG", qb, i)])

                    st1(0)
                    for s in range(U + 3):
                        if s + 1 < U:
                            st1(s + 1)
                        if s < U:
                            st2(s)
                        if 0 <= s - 1 < U:
                            st3(s - 1)
                        if 0 <= s - 2 < U:
                            st5(s - 2)
                        if 0 <= s - 3 < U:
                            st6(s - 3)
                        if bg_early:
                            bg_early.pop(0)()
                        elif bg_late and s >= U - len(bg_late) - 2:
                            bg_late.pop(0)()
                    while bg_early:
                        bg_early.pop(0)()
                    while bg_late:
                        bg_late.pop(0)()
                    cnt["u"] += U
                    cnt["q"] += 16

                for la in range(nlay):
                    l = 2 + la
                    emit_norm(l, l * 24)
                    sc.alias(SQ(0), TQ(0))
                    sc.alias(SQ(1), TQ(1))
                    sc.alias(HT(0), TS(0))
                    sc.alias(HT(2), TS(1))
                    load_qg(la, 0)
                    load_qg(la, 1)
                    load_o(la, 0)
                    nb_ = 0
                    for which in ("q", "g"):
                        for tt in range(NT):
                            proj_group(0, which, tt, bk=(7, 6, 0, 1)[nb_ % 4])
                            nb_ += 1
                    for h in range(8):
                        early, late = [], []
                        if h >= 1:
                            for jc in range(8):
                                for tt in range(NT):
                                    early.append(lambda jc=jc, tt=tt, hp=h - 1: outproj_group(l, hp, jc, tt, bk=nextbg()))
                        if h + 1 < 8:
                            for tt in range(NT):
                                pos = min(len(early), 10 * tt + 4)
                                early.insert(pos, (lambda tt=tt, hn=h + 1: proj_group(hn, "q", tt, part=0)))
                                early.insert(pos + 1, (lambda tt=tt, hn=h + 1: proj_group(hn, "q", tt, part=1)))
                            for tt in range(NT):
                                late.append(lambda tt=tt, hn=h + 1: proj_group(hn, "g", tt, part=0))
                                late.append(lambda tt=tt, hn=h + 1: proj_group(hn, "g", tt, part=1))
                        if h + 2 < 8:
                            load_qg(la, h + 2)
                        if h + 1 < 8:
                            load_o(la, h + 1)
                        attn_head(la, l, h, early, late)
                    nb_ = 0
                    for jc in range(8):
                        for tt in range(NT):
                            outproj_group(l, 7, jc, tt, bk=(7, 6, 0, 1)[nb_ % 4])
                            nb_ += 1
                    sc.alias(TQ(0), SQ(0))
                    sc.alias(TQ(1), SQ(1))
                    sc.alias(TS(0), HT(0))
                    sc.alias(TS(1), HT(2))
                sc.barrier()

        if nlayers >= 3:
            emit_attention(nlayers - 2)

        with contextlib.ExitStack() as es3:
            sb3 = lambda n, s, dt=F32: _alloc(es3, n, s, dt)
            of2 = [sb3(f"of{i}", [128, KC, TT]) for i in range(2)]
            ot = [sb3(f"ot{i}", [128, D]) for i in range(4)]
            ov = out_d.rearrange("(n p) d -> n p d", p=128)
            if not dbg:
                emit_rstd_all()
            for tt in range(NT):
                tsl = slice(tt * TT, (tt + 1) * TT)
                of = of2[tt % 2]
                if dbg:
                    for kc in range(KC):
                        op("dve", lambda e, kc=kc: e.tensor_copy(out=of[:, kc, :], in_=xT[:, kc, tsl]), r=[("xT", tt, kc)], w=[("of", tt % 2, kc)])
                else:
                    for kc in range(KC):
                        op("dve", lambda e, kc=kc: e.scalar_tensor_tensor(out=of[:, kc, :], in0=xT[:, kc, tsl], scalar=Avec[:, 5 * KC + kc:5 * KC + kc + 1],
                                                                         in1=ps[4 + tt][:, :], op0=ALU.mult, op1=ALU.mult),
                           r=[("xT", tt, kc), PS(4 + tt), "Avec"], w=[("of", tt % 2, kc)])
                for sub in range(4):
                    n = tt * 4 + sub
                    sl = n % 4
                    for half in range(2):
                        b = (n * 2 + half) % 4
                        for q in range(4):
                            kc = half * 4 + q
                            op("pe", lambda e, b=b, q=q, kc=kc, sub=sub: e.transpose(out=ps[b][:, q * 128:(q + 1) * 128], in_=of[:, kc, sub * 128:(sub + 1) * 128], identity=ident),
                               r=[("of", tt % 2, kc), "cst"], w=[PS(b)], nl=1)
                        if half == 0:
                            op("act", lambda e, b=b, sl=sl: e.activation(out=ot[sl][:, 0:512], in_=ps[b][:, :], func=AF.Copy), r=[PS(b)], w=[("ot", sl)])
                        else:
                            op("dve", lambda e, b=b, sl=sl: e.tensor_copy(out=ot[sl][:, 512:1024], in_=ps[b][:, :]), r=[PS(b)], w=[("ot", sl)])
                    dma("sp", f"d_out{sl}", ov[n], ot[sl][:], r=[("ot", sl)])
            sc.final_wait("sp", "d_out0")
            sc.final_wait("sp", "d_out1")
        print(f"[build] instructions={sc.nins} waits={sc.nwait} min_sbuf_remaining={min(REM)}")
    return nc


def _prep_inputs(inp):
    f32 = np.float32
    def fm(v):
        v = np.asarray(v, f32).reshape(-1, 128)
        return np.ascontiguousarray(v.T)
    pv = np.zeros((128, NPV), f32)
    for l in range(4):
        pv[:, PV_NORMG + l * 8:PV_NORMG + l * 8 + 8] = fm(inp["norm_g"][l])
        pv[:, PV_MODB + l * 24:PV_MODB + l * 24 + 24] = fm(inp["mod_b"][l])
    pv[:, PV_KVG:PV_KVG + 8] = fm(inp["kv_norm_g"])
    pv[:, PV_FING:PV_FING + 8] = fm(inp["final_norm_g"])
    pv[:, PV_KVMODB:PV_KVMODB + 16] = fm(inp["kv_mod_b"])
    for l in range(2):
        for k in range(4):
            pv[:, PV_CONVW + l * 32 + k * 8:PV_CONVW + l * 32 + k * 8 + 8] = fm(inp["rg_conv_w"][l, k])
        pv[:, PV_CONVB + l * 8:PV_CONVB + l * 8 + 8] = fm(inp["rg_conv_b"][l])
        pv[:, PV_BA + l * 8:PV_BA + l * 8 + 8] = fm(inp["rg_b_a"][l])
        pv[:, PV_BX + l * 8:PV_BX + l * 8 + 8] = fm(inp["rg_b_x"][l])
        pv[:, PV_LAM + l * 8:PV_LAM + l * 8 + 8] = fm(inp["rg_lambda"][l])
    cst = np.zeros((128, 384), f32)
    cst[:, 0:128] = np.eye(128, dtype=f32)
    q = np.arange(128)[:, None]
    k = np.arange(128)[None, :]
    cst[:, 128:256] = np.where(k <= q, 0.0, NEG).astype(f32)
    cst[:, 256:384] = 1.0
    shared = {k2: np.ascontiguousarray(np.asarray(inp[k2], f32)) for k2 in
              ("mod_w", "kv_mod_w", "rg_w_in", "rg_w_a", "rg_w_x", "rg_w_out", "w_kv", "att_w_in", "att_w_out")}
    maps = []
    for b in range(8):
        m = dict(shared)
        m["x"] = np.ascontiguousarray(np.asarray(inp["x"][b], f32))
        m["cT"] = fm(inp["c"][b])
        m["pv"] = pv
        m["cst"] = cst
        maps.append(m)
    return maps


def kernel(**inputs):
    nl = int(os.environ.get("K_NLAYERS", "4"))
    dbg = os.environ.get("K_DBG", "0") == "1"
    nc = build(nl, dbg)
    maps = _prep_inputs(inputs)
    res = run_bass_kernel_spmd(nc, maps, core_ids=list(range(8)))
    return np.stack([r["out"] for r in res.results], axis=0).astype(np.float32)
```
